# Optimizing a Trainium2 kernel written in Bass

```python
import jax, jax.numpy as jnp
from jax import lax
import numpy as np

D_MODEL = 1024
BATCH = 8
SEQ = 4096
DEPTH = 1

EPS = 1e-6
GMLP_WIDTH = 768
GMLP_GROUPS = 4
GMLP_GROUP_DIM = GMLP_WIDTH // GMLP_GROUPS
GMLP_CHUNK = 128
HEAD_DIM = 64
HEADS_PER_GROUP = 4
DILATED_GROUPS = ((128, 1), (512, 4), (2048, 16))
N_ATTN_GROUPS = 3
N_ATTN_HEADS = N_ATTN_GROUPS * HEADS_PER_GROUP
ATTN_WIDTH = N_ATTN_HEADS * HEAD_DIM
ATTN_OUT_WIDTH = HEADS_PER_GROUP * HEAD_DIM
ATTN_BLOCK = 128
ROPE_THETA = 500000.0
ROT_DIM = HEAD_DIM // 4
N_BRANCHES = 2
IN_WIDTH = 2 * GMLP_WIDTH + 3 * ATTN_WIDTH + N_BRANCHES * D_MODEL
D_FF = 2816
CONV_WIDTH = 3

kernel_name = "hybrid_gmlp_dilated_attn_convffn"


def rms_norm(x, g):
    xf = x.astype(jnp.float32)
    y = xf * lax.rsqrt(jnp.mean(xf * xf, axis=-1, keepdims=True) + EPS)
    return (y * g.astype(jnp.float32)).astype(x.dtype)


def apply_partial_rope(t, cos, sin):
    half = ROT_DIM // 2
    rot = t[..., :ROT_DIM].astype(jnp.float32)
    t1, t2 = rot[..., :half], rot[..., half:]
    rotated = jnp.concatenate([t1 * cos - t2 * sin, t1 * sin + t2 * cos], axis=-1)
    return jnp.concatenate([rotated.astype(t.dtype), t[..., ROT_DIM:]], axis=-1)


def chunked_spatial_gating(z, norm_g, w_s, b_s):
    b, s, _ = z.shape
    u, v = z[..., :GMLP_WIDTH], z[..., GMLP_WIDTH:]
    v = rms_norm(v, norm_g)
    n_chunks = s // GMLP_CHUNK
    vg = v.reshape(b, n_chunks, GMLP_CHUNK, GMLP_GROUPS, GMLP_GROUP_DIM)
    causal = jnp.tril(jnp.ones((GMLP_CHUNK, GMLP_CHUNK), dtype=bool))
    w = jnp.where(causal[None], w_s, jnp.zeros_like(w_s))
    mixed = jnp.einsum('gts,bcsgd->bctgd', w, vg) + b_s.T[None, None, :, :, None]
    return u * mixed.reshape(b, s, GMLP_WIDTH)


def banded_window_attention(q, k, v, steps):
    n, l, h, dh = q.shape
    nb = -(-l // ATTN_BLOCK)
    pad = nb * ATTN_BLOCK - l

    def blocks(t):
        t = jnp.pad(t, ((0, 0), (0, pad), (0, 0), (0, 0)))
        return t.reshape(n, nb, ATTN_BLOCK, h, dh)

    def with_prev(t):
        prev = jnp.pad(t, ((0, 0), (1, 0), (0, 0), (0, 0), (0, 0)))[:, :-1]
        return jnp.concatenate([prev, t], axis=2)

    qb = blocks(q)
    kc = with_prev(blocks(k))
    vc = with_prev(blocks(v))
    scores = jnp.einsum('nbqhd,nbkhd->nbhqk', qb, kc,
                        preferred_element_type=jnp.float32) * (HEAD_DIM ** -0.5)
    qi = jnp.arange(ATTN_BLOCK)[:, None]
    ki = jnp.arange(2 * ATTN_BLOCK)[None, :]
    dist = ATTN_BLOCK + qi - ki
    band = (dist >= 0) & (dist <= steps)
    first_ok = (jnp.arange(nb)[:, None, None] > 0) | (ki[None] >= ATTN_BLOCK)
    valid = band[None] & first_ok
    scores = jnp.where(valid[None, :, None], scores, -jnp.inf)
    m = jnp.max(scores, axis=-1, keepdims=True)
    p = jnp.exp(scores - m)
    den = jnp.sum(p, axis=-1, keepdims=True)
    out = jnp.einsum('nbhqk,nbkhd->nbqhd', (p / den).astype(v.dtype), vc)
    lse = (m + jnp.log(den))[..., 0]
    lse = lse.transpose(0, 1, 3, 2).reshape(n, nb * ATTN_BLOCK, h)[:, :l]
    out = out.reshape(n, nb * ATTN_BLOCK, h, dh)[:, :l]
    return out, lse


def dilated_mixture_attention(q, k, v):
    b, s = q.shape[:2]
    outs, lses = [], []
    for g, (window, dil) in enumerate(DILATED_GROUPS):
        lo, hi = g * HEADS_PER_GROUP, (g + 1) * HEADS_PER_GROUP
        sub = s // dil

        def to_residue(t):
            t = t[:, :, lo:hi].reshape(b, sub, dil, HEADS_PER_GROUP, HEAD_DIM)
            return t.transpose(0, 2, 1, 3, 4).reshape(b * dil, sub, HEADS_PER_GROUP, HEAD_DIM)

        o, lse = banded_window_attention(to_residue(q), to_residue(k), to_residue(v), window // dil)
        o = o.reshape(b, dil, sub, HEADS_PER_GROUP, HEAD_DIM).transpose(0, 2, 1, 3, 4)
        lse = lse.reshape(b, dil, sub, HEADS_PER_GROUP).transpose(0, 2, 1, 3)
        outs.append(o.reshape(b, s, HEADS_PER_GROUP, HEAD_DIM))
        lses.append(lse.reshape(b, s, HEADS_PER_GROUP))
    o_all = jnp.stack(outs, axis=3)
    alpha = jax.nn.softmax(jnp.stack(lses, axis=-1), axis=-1)
    mixed = jnp.einsum('bshg,bshgd->bshd', alpha.astype(o_all.dtype), o_all)
    return mixed.reshape(b, s, ATTN_OUT_WIDTH)


def causal_depthwise_conv(a, w, bias):
    s = a.shape[1]
    a_pad = jnp.pad(a, ((0, 0), (CONV_WIDTH - 1, 0), (0, 0)))
    y = a_pad[:, 0:s] * w[0]
    for j in range(1, CONV_WIDTH):
        y = y + a_pad[:, j:j + s] * w[j]
    return y + bias


def setup_inputs(seed: int = 0) -> dict:
    key = jax.random.key(seed)
    ks = jax.random.split(key, 16)
    f32 = jnp.float32
    nrm = lambda k, shape, scale: jax.random.normal(k, shape, f32) * scale
    return {
        "x": nrm(ks[0], (BATCH, SEQ, D_MODEL), 1.0),
        "positions": jnp.broadcast_to(jnp.arange(SEQ, dtype=jnp.int32), (BATCH, SEQ)),
        "mix_norm_g": 1.0 + nrm(ks[1], (DEPTH, D_MODEL), 0.02),
        "w_in": nrm(ks[2], (DEPTH, D_MODEL, IN_WIDTH), D_MODEL ** -0.5),
        "gmlp_norm_g": 1.0 + nrm(ks[3], (DEPTH, GMLP_WIDTH), 0.02),
        "w_spatial": nrm(ks[4], (DEPTH, GMLP_GROUPS, GMLP_CHUNK, GMLP_CHUNK), GMLP_CHUNK ** -0.5),
        "b_spatial": 1.0 + nrm(ks[5], (DEPTH, GMLP_GROUPS, GMLP_CHUNK), 0.1),
        "w_branch_a": nrm(ks[6], (DEPTH, GMLP_WIDTH, D_MODEL), GMLP_WIDTH ** -0.5),
        "w_branch_b": nrm(ks[7], (DEPTH, ATTN_OUT_WIDTH, D_MODEL), ATTN_OUT_WIDTH ** -0.5),
        "w_out": nrm(ks[8], (DEPTH, D_MODEL, D_MODEL), D_MODEL ** -0.5),
        "ffn_norm_g": 1.0 + nrm(ks[9], (DEPTH, D_MODEL), 0.02),
        "w_up": nrm(ks[10], (DEPTH, D_MODEL, 2 * D_FF), D_MODEL ** -0.5),
        "conv_w": nrm(ks[11], (DEPTH, CONV_WIDTH, D_FF), CONV_WIDTH ** -0.5),
        "conv_b": nrm(ks[12], (DEPTH, D_FF), 0.01),
        "w_down": nrm(ks[13], (DEPTH, D_FF, D_MODEL), D_FF ** -0.5),
        "final_norm_g": 1.0 + nrm(ks[14], (D_MODEL,), 0.02),
    }


def reference(x, positions, mix_norm_g, w_in, gmlp_norm_g, w_spatial, b_spatial,
              w_branch_a, w_branch_b, w_out, ffn_norm_g, w_up, conv_w, conv_b,
              w_down, final_norm_g):
    b, s, _ = x.shape
    inv_freq = ROPE_THETA ** (-jnp.arange(0, ROT_DIM, 2, dtype=jnp.float32) / ROT_DIM)
    ang = positions.astype(jnp.float32)[..., None] * inv_freq
    cos = jnp.cos(ang)[:, :, None, :]
    sin = jnp.sin(ang)[:, :, None, :]
    splits = [2 * GMLP_WIDTH, 2 * GMLP_WIDTH + ATTN_WIDTH,
              2 * GMLP_WIDTH + 2 * ATTN_WIDTH, 2 * GMLP_WIDTH + 3 * ATTN_WIDTH]
    for layer in range(DEPTH):
        h = rms_norm(x, mix_norm_g[layer])
        proj = h @ w_in[layer]
        z_a, q, k, v, gate_logits = jnp.split(proj, splits, axis=-1)
        y_a = chunked_spatial_gating(jax.nn.gelu(z_a, approximate=False),
                                     gmlp_norm_g[layer], w_spatial[layer], b_spatial[layer])
        q = apply_partial_rope(q.reshape(b, s, N_ATTN_HEADS, HEAD_DIM), cos, sin)
        k = apply_partial_rope(k.reshape(b, s, N_ATTN_HEADS, HEAD_DIM), cos, sin)
        v = v.reshape(b, s, N_ATTN_HEADS, HEAD_DIM)
        y_b = dilated_mixture_attention(q, k, v)
        gates = jax.nn.sigmoid(gate_logits.reshape(b, s, N_BRANCHES, D_MODEL))
        merged = gates[:, :, 0] * (y_a @ w_branch_a[layer]) + gates[:, :, 1] * (y_b @ w_branch_b[layer])
        x = x + merged @ w_out[layer]
        h2 = rms_norm(x, ffn_norm_g[layer])
        up = h2 @ w_up[layer]
        a, val = up[..., :D_FF], up[..., D_FF:]
        a = causal_depthwise_conv(a, conv_w[layer], conv_b[layer])
        x = x + (jax.nn.gelu(a, approximate=False) * val) @ w_down[layer]
    return rms_norm(x, final_norm_g)
```

```python
import numpy as np
import concourse.bass as bass
import concourse.mybir as mybir
from concourse.bass_utils import run_bass_kernel_spmd

F32 = mybir.dt.float32
BF16 = mybir.dt.bfloat16
I32 = mybir.dt.int32
AF = mybir.ActivationFunctionType
ALU = mybir.AluOpType

D = 1024
SEQ = 4096
TT = 512
NT = SEQ // TT
INW = 5888
DFF = 2816
NFF = DFF // 128
EPS = 1e-6
KB = 1024


class Res:
    _n = 0

    def __init__(self, name, lo=None, hi=None):
        Res._n += 1
        self.rid = Res._n
        self.name = name
        self.lo, self.hi = lo, hi
        self.ov = [self]


class Sched:
    ENG = ("pe", "act", "dve", "pool", "sp")

    def __init__(self):
        self.ops = {e: [] for e in self.ENG}
        self.lastw = {}
        self.readers = {}
        self.dma_cum = {}

    def op(self, eng, fn, reads=(), writes=(), dma=None):
        deps = set()
        for b in reads:
            ev = self.lastw.get(b.rid)
            if ev is not None:
                deps.add(ev)
        for b in writes:
            for x in b.ov:
                ev = self.lastw.get(x.rid)
                if ev is not None:
                    deps.add(ev)
                for r in self.readers.get(x.rid, ()):
                    deps.add(r)
        if dma is not None:
            v = self.dma_cum.get(dma, 0) + 16
            self.dma_cum[dma] = v
            ev = ("dma:" + dma, v)
        else:
            ev = (eng, len(self.ops[eng]))
        self.ops[eng].append((fn, deps, dma))
        for b in reads:
            self.readers.setdefault(b.rid, []).append(ev)
        for b in writes:
            for x in b.ov:
                self.lastw[x.rid] = ev
                self.readers[x.rid] = []
        return ev

    def emit(self, nc, block, esem, dsem):
        comp = ("pe", "act", "dve", "pool")
        needed = {e: set() for e in comp}
        for e in self.ENG:
            for (_, deps, _) in self.ops[e]:
                for (k, v) in deps:
                    if k in needed and not (k == "pe" and e == "pe"):
                        needed[k].add(v)
        rank = {e: {idx: i + 1 for i, idx in enumerate(sorted(needed[e]))} for e in comp}
        ops = self.ops

        def run(ename, eng):
            known = {}
            for idx, (fn, deps, dma) in enumerate(ops[ename]):
                want = {}
                for (k, v) in deps:
                    if k == "pe" and ename == "pe":
                        continue
                    if k in rank:
                        val = rank[k][v]
                    else:
                        val = v
                    if val > want.get(k, 0):
                        want[k] = val
                for k, val in want.items():
                    if known.get(k, 0) >= val:
                        continue
                    known[k] = val
                    sem = esem[k] if k in esem else dsem[k[4:]]
                    eng.wait_ge(sem, val)
                ins = fn(eng)
                if dma is not None:
                    ins.then_inc(dsem[dma], 16)
                elif ename in rank and idx in rank[ename]:
                    ins.then_inc(esem[ename], 1)

        @block.tensor
        def _(eng):
            run("pe", eng)

        @block.scalar
        def _(eng):
            run("act", eng)

        @block.vector
        def _(eng):
            run("dve", eng)

        @block.gpsimd
        def _(eng):
            run("pool", eng)

        @block.sync
        def _(eng):
            run("sp", eng)
            for name, v in self.dma_cum.items():
                eng.wait_ge(dsem[name], v)


def build_nc(ntiles=NT, dbg=None):
    nc = bass.Bass("TRN2", target_bir_lowering=False)
    S = Sched()
    dbg = dbg or {}
    LIM = dbg.get("_lim", 99)

    def din(name, shape, dt=F32):
        return nc.dram_tensor(name, list(shape), dt, kind="ExternalInput").ap()

    xT = din("xT", [D, SEQ])
    pos_d = din("pos", [128, 32], I32)
    w_in = din("w_in", [D, INW])
    w_pa = din("w_pa", [768, D])
    w_pb = din("w_pb", [256, D])
    w_out = din("w_out", [D, D])
    w_up = din("w_up", [D, 2 * DFF])
    w_down = din("w_down", [DFF, D])
    gvec_d = din("gvec", [128, 24])
    gm_d = din("gm", [128, 6])
    bbc_d = din("bbc", [128, 6 * 128])
    wsp_d = din("wspT", [128, 4 * 128])
    cw_d = din("convw", [128, NFF * 3])
    cb_d = din("convb", [128, NFF])
    cst_d = din("consts", [128, 3 * 128 + 8])
    outT = nc.dram_tensor("outT", [D, SEQ], F32, kind="ExternalOutput").ap()
    dbg_out = {}
    for name, (shape, dt) in dbg.items():
        dbg_out[name] = nc.dram_tensor("dbg_" + name, list(shape), dt, kind="ExternalOutput").ap()

    ARENA_BYTES = 204 * KB
    ctx = []
    arena_t = nc.sbuf_tensor("arena", [128, ARENA_BYTES // 4], F32)
    arena = arena_t.__enter__()
    ctx.append(arena_t)
    banks = []
    for i in range(8):
        t = nc.psum_tensor("ps%d" % i, [128, 512], F32)
        banks.append(t.__enter__())
        ctx.append(t)
    bankres = [Res("ps%d" % i) for i in range(8)]

    allocs = []
    top = [0]

    class Buf(Res):
        def __init__(self, name, shape, dt, at=None):
            esz = 4 if dt in (F32, I32) else 2
            n = 1
            for s_ in shape:
                n *= s_
            nbytes = n * esz
            if at is None:
                off = (top[0] + 63) // 64 * 64
                top[0] = off + nbytes
            else:
                off = at
            assert off % 4 == 0 and off + nbytes <= ARENA_BYTES, (name, off, nbytes)
            Res.__init__(self, name, off, off + nbytes)
            for o in allocs:
                if o.lo < self.hi and self.lo < o.hi:
                    o.ov.append(self)
                    self.ov.append(o)
            allocs.append(self)
            v = arena[:, off // 4:(off + nbytes) // 4]
            if dt != F32:
                v = v.bitcast(dt)
            self.flat = v
            if len(shape) == 1:
                self.ap = v
            elif len(shape) == 2:
                self.ap = v.rearrange("p (a b) -> p a b", a=shape[0])
            elif len(shape) == 3:
                self.ap = v.rearrange("p (a b c) -> p a b c", a=shape[0], b=shape[1])
            else:
                raise ValueError(shape)
            self.shape = shape
            self.dt = dt

    cst = Buf("cst", [3 * 128 + 8], F32)
    ident_f = cst.ap[:, 0:128]
    mcur_f = cst.ap[:, 128:256]
    mprev_f = cst.ap[:, 256:384]
    invf = cst.ap[:, 384:392]
    ident_b = Buf("ident_b", [128], BF16)
    mcur_b = Buf("mcur_b", [128], BF16)
    mprev_b = Buf("mprev_b", [128], BF16)
    ones_f = Buf("ones_f", [128], F32)
    ones_b = Buf("ones_b", [128], BF16)
    gvec = Buf("gvec", [24], F32)
    gm = Buf("gm", [6], F32)
    bbc = Buf("bbc", [6, 128], F32)
    wsp_f = Buf("wsp_f", [4, 128], F32)
    wsp_b = Buf("wsp_b", [4, 128], BF16)
    cw = Buf("cw", [NFF, 3], F32)
    cb = Buf("cb", [NFF], F32)
    pos_i = Buf("pos_i", [32], I32)
    pos_f = Buf("pos_f", [32], F32)
    cos_t = Buf("cos_t", [32, 8], F32)
    sin_t = Buf("sin_t", [32, 8], F32)
    halo = Buf("halo", [NFF, 2], F32)
    ss_sb = Buf("ss_sb", [4], F32)
    sd_sb = Buf("sd_sb", [4], F32)
    rstd = Buf("rstd", [4], F32)
    rstd_v = Buf("rstd_v", [4], F32)
    ssv = Buf("ssv", [4], F32)
    diag = Buf("diag", [4, 128], F32)
    junk = Buf("junk", [768], BF16)
    ropeA = Buf("ropeA", [24, 8], F32)
    ropeB = Buf("ropeB", [24, 8], F32)
    xr = [Buf("xr%d" % i, [TT], F32) for i in range(3)]
    sq = [Buf("sq%d" % i, [TT], BF16) for i in range(2)]
    hT = [Buf("hT%d" % i, [TT], BF16) for i in range(8)]
    x1 = [Buf("x1_%d" % i, [TT], F32) for i in range(8)]
    K0 = [[Buf("K0_%d_%d" % (p, c), [TT], BF16) for c in range(2)] for p in range(2)]
    K1 = [[Buf("K1_%d_%d" % (p, c), [TT], BF16) for c in range(2)] for p in range(2)]
    K2 = [Buf("K2_%d" % c, [SEQ], BF16) for c in range(2)]
    V0 = [Buf("V0_%d" % p, [4, 256], BF16) for p in range(2)]
    V1 = [Buf("V1_%d" % p, [4, 256], BF16) for p in range(2)]
    V2 = [Buf("V2_%d" % p, [16, 256], BF16) for p in range(2)]
    USZ = 11 * 256
    wst = [Buf("wst%d" % i, [USZ], F32) for i in range(2)]
    wbf = [Buf("wbf%d" % i, [USZ], BF16) for i in range(4)]
    P0 = (top[0] + 63) // 64 * 64

    def PB(name, shape, dt, kb_off):
        return Buf(name, shape, dt, at=P0 + int(kb_off * KB))

    ya = [PB("ya%d" % c, [TT], BF16, 0 + c) for c in range(6)]
    vg = [PB("vg%d" % b, [768], BF16, 6 + 1.5 * b) for b in range(4)]
    wp = [PB("wp%d" % b, [4, 128], BF16, 12 + b) for b in range(4)]
    qk_tm = [PB("qk%d" % b, [1536], BF16, 16 + 3 * b) for b in range(4)]
    rot = [PB("rot%d" % b, [24, 16], F32, 28 + 1.5 * b) for b in range(4)]
    qT = [PB("qT%d" % c, [TT], BF16, 34 + c) for c in range(6)]
    vT = [PB("vT%d" % c, [TT], BF16, 40 + c) for c in range(6)]
    Eb = [PB("E%d" % i, [TT], BF16, 46 + i) for i in range(2)]
    rden = PB("rden", [TT], F32, 48)
    stgs = [PB("stg%d" % i, [256], F32, 48 + i) for i in range(2)]
    yb = [PB("yb%d" % i, [TT], BF16, 50 + i) for i in range(2)]
    tg = [PB("tg%d" % c, [TT], BF16, 16 + c) for c in range(16)]
    t1b = PB("t1b", [TT], F32, 6)
    t2b = PB("t2b", [TT], F32, 8)
    mrg = [PB("mrg%d" % c, [TT], BF16, 40 + c) for c in range(8)]
    gg = [PB("gg%d" % c, [TT], BF16, 0 + c) for c in range(NFF)]
    ob = [PB("ob%d" % i, [TT], F32, 22 + 2 * i) for i in range(2)]
    gel = [PB("gel%d" % i, [TT], F32, 26 + 2 * i) for i in range(2)]
    abuf = [PB("abuf%d" % i, [TT + 2], F32, 30 + 2.25 * i) for i in range(2)]
    assert P0 + 52 * KB <= ARENA_BYTES, P0

    def dma_in(dst_buf, dst_ap, src_ap, sem, extra_w=()):
        S.op("sp", lambda e: e.dma_start(out=dst_ap, in_=src_ap), reads=(), writes=(dst_buf,) + tuple(extra_w), dma=sem)

    def act(out, in_, func, reads, writes, **kw):
        return S.op("act", lambda e: e.activation(out, in_, func, **kw), reads=reads, writes=writes)

    def dve_tt(out, in0, in1, op, reads, writes, eng="dve"):
        return S.op(eng, lambda e: e.tensor_tensor(out, in0, in1, op), reads=reads, writes=writes)

    def dve_ts(out, in0, s1, s2, op0, op1, reads, writes, eng="dve"):
        if op1 is None:
            return S.op(eng, lambda e: e.tensor_scalar(out, in0, s1, None, op0), reads=reads, writes=writes)
        return S.op(eng, lambda e: e.tensor_scalar(out, in0, s1, s2, op0, op1), reads=reads, writes=writes)

    def dve_stt(out, in0, scalar, in1, op0, op1, reads, writes):
        return S.op("dve", lambda e: e.scalar_tensor_tensor(out, in0, scalar, in1, op0, op1), reads=reads, writes=writes)

    def copy(eng, out, in_, reads, writes):
        if eng == "act":
            return S.op("act", lambda e: e.activation(out, in_, AF.Copy), reads=reads, writes=writes)
        return S.op(eng, lambda e: e.tensor_copy(out, in_), reads=reads, writes=writes)

    class Sess:
        def __init__(self, bi):
            self.bi = bi
            self.res = bankres[bi]
            self.t = banks[bi]
            self.started = set()

        def mm(self, out, lhsT, rhs, reads, row0=0, rows=128, tp=None):
            qs = set(range(row0 // 32, (row0 + rows + 31) // 32))
            first = not (qs & self.started)
            assert first or qs <= self.started
            self.started |= qs
            kw = {}
            if tp is not None:
                kw["tile_position"] = tp
            S.op("pe", lambda e: e.matmul(out, lhsT, rhs, start=first, stop=True, skip_group_check=True, **kw),
                 reads=reads, writes=(self.res,))

        def tr(self, out, in_, ident, reads, tp=None):
            kw = {}
            if tp is not None:
                kw["tile_position"] = tp
            S.op("pe", lambda e: e.transpose(out, in_, ident, **kw), reads=reads, writes=(self.res,))

    accn = [0]

    acc_ring = [[0, 1, 2, 3, 6, 7]]

    def acc():
        r_ = acc_ring[0]
        b = r_[accn[0] % len(r_)]
        accn[0] += 1
        return Sess(b)

    units = []

    def add_unit(w, r0, kc, c0):
        units.append((w[r0:r0 + kc * 128, c0:c0 + 256], kc))

    for t in range(ntiles):
        for j in range(23):
            add_unit(w_in, 0, 8, 256 * j)
        for j in range(4):
            add_unit(w_pa, 0, 6, 256 * j)
            add_unit(w_pb, 0, 2, 256 * j)
        for j in range(4):
            add_unit(w_out, 0, 8, 256 * j)
        for j in range(11):
            add_unit(w_up, 0, 8, 256 * j)
            add_unit(w_up, 0, 8, DFF + 256 * j)
        for j in range(4):
            add_unit(w_down, 0, 11, 256 * j)
            add_unit(w_down, 11 * 128, 11, 256 * j)
    uloaded = [0]
    uused = [0]
    PREF = 2
    UPT = len(units) // ntiles
    wscr = nc.dram_tensor("wscr", [UPT, 128, USZ], BF16, kind="Internal").ap()
    scr_res = [Res("wscr%d" % u) for u in range(UPT)]

    def _writeback(i):
        src, kc = units[i]
        wb = wbf[i % 4]
        u = i % UPT
        S.op("sp", lambda e: e.dma_start(out=wscr[u, :, 0:kc * 256], in_=wb.ap[:, 0:kc * 256]),
             reads=(wb,), writes=(scr_res[u],), dma="wsb%d" % (i % 4))

    def _load(i):
        src, kc = units[i]
        wb = wbf[i % 4]
        u = i % UPT
        if i < UPT:
            st = wst[i % 2]
            dst = st.ap[:, 0:kc * 256].rearrange("p (k c) -> p k c", k=kc)
            S.op("sp", lambda e: e.dma_start(out=dst, in_=src.rearrange("(k p) c -> p k c", p=128)),
                 writes=(st,), dma="wst%d" % (i % 2))
            if ntiles > 1 and i >= 1:
                _writeback(i - 1)
            if i % 2 == 0:
                S.op("pool", lambda e: e.tensor_copy(wb.ap[:, 0:kc * 256], st.ap[:, 0:kc * 256]), reads=(st,), writes=(wb,))
            else:
                S.op("act", lambda e: e.activation(wb.ap[:, 0:kc * 256], st.ap[:, 0:kc * 256], AF.Copy), reads=(st,), writes=(wb,))
        else:
            if i == UPT:
                _writeback(UPT - 1)
            S.op("sp", lambda e: e.dma_start(out=wb.ap[:, 0:kc * 256], in_=wscr[u, :, 0:kc * 256]),
                 reads=(scr_res[u],), writes=(wb,), dma="wld%d" % (i % 4))

    def next_unit():
        i = uused[0]
        uused[0] += 1
        while uloaded[0] < min(len(units), i + 1 + PREF):
            _load(uloaded[0])
            uloaded[0] += 1
        kc = units[i][1]
        wb = wbf[i % 4]
        return wb, wb.ap[:, 0:kc * 256].rearrange("p (k c) -> p k c", k=kc)

    dma_in(cst, cst.ap, cst_d, "c_cst")
    dma_in(gvec, gvec.ap, gvec_d, "c_gvec")
    dma_in(gm, gm.ap, gm_d, "c_gm")
    dma_in(bbc, bbc.flat, bbc_d, "c_bbc")
    dma_in(wsp_f, wsp_f.flat, wsp_d, "c_wsp")
    dma_in(cw, cw.flat, cw_d, "c_cw")
    dma_in(cb, cb.ap, cb_d, "c_cb")
    dma_in(pos_i, pos_i.ap, pos_d, "c_pos")
    copy("dve", ident_b.ap, ident_f, (cst,), (ident_b,))
    copy("dve", mcur_b.ap, mcur_f, (cst,), (mcur_b,))
    copy("dve", mprev_b.ap, mprev_f, (cst,), (mprev_b,))
    S.op("pool", lambda e: e.memset(ones_f.ap, 1.0), writes=(ones_f,))
    S.op("pool", lambda e: e.memset(ones_b.ap, 1.0), writes=(ones_b,))
    S.op("pool", lambda e: e.memset(halo.flat, 0.0), writes=(halo,))
    for c in range(2):
        S.op("pool", lambda e, c=c: e.memset(K2[c].ap, 0.0), writes=(K2[c],))
    for p in range(2):
        S.op("pool", lambda e, p=p: e.memset(V2[p].flat, 0.0), writes=(V2[p],))
    dve_tt(wsp_f.ap, wsp_f.ap, mcur_f.unsqueeze(1).broadcast_to([128, 4, 128]), ALU.mult, (wsp_f, cst), (wsp_f,))
    copy("dve", pos_f.ap, pos_i.ap, (pos_i,), (pos_f,))
    ang = Buf("ang", [32, 8], F32, at=P0)
    kq = Buf("kq", [32, 8], F32, at=P0 + 2 * KB)
    ki = Buf("ki", [32, 8], I32, at=P0 + 4 * KB)
    red = Buf("red", [32, 8], F32, at=P0 + 6 * KB)
    dve_tt(ang.ap, pos_f.ap.unsqueeze(2).broadcast_to([128, 32, 8]), invf.unsqueeze(1).broadcast_to([128, 32, 8]),
           ALU.mult, (pos_f, cst), (ang,))
    TWO_PI = 2.0 * np.pi
    C1 = float(np.float32(6.28125))
    C2 = float(np.float32(TWO_PI - 6.28125))
    C3 = float(TWO_PI - 6.28125 - float(np.float32(TWO_PI - 6.28125)))
    msk = Buf("msk", [32, 8], F32, at=P0 + 8 * KB)
    PI = float(np.pi)
    for (tab, shift) in ((sin_t, 0.0), (cos_t, float(np.pi / 2))):
        dve_ts(kq.ap, ang.ap, float(1.0 / TWO_PI), None, ALU.mult, None, (ang,), (kq,))
        copy("dve", ki.ap, kq.ap, (kq,), (ki,))
        copy("dve", kq.ap, ki.ap, (ki,), (kq,))
        dve_stt(red.ap, kq.ap, -C1, ang.ap, ALU.mult, ALU.add, (kq, ang), (red,))
        dve_stt(red.ap, kq.ap, -C2, red.ap, ALU.mult, ALU.add, (kq, red), (red,))
        dve_stt(red.ap, kq.ap, -C3, red.ap, ALU.mult, ALU.add, (kq, red), (red,))
        if shift != 0.0:
            dve_ts(red.ap, red.ap, shift, None, ALU.add, None, (red,), (red,))
        for _ in range(2):
            dve_ts(msk.ap, red.ap, PI, -TWO_PI, ALU.is_gt, ALU.mult, (red,), (msk,))
            dve_tt(red.ap, red.ap, msk.ap, ALU.add, (red, msk), (red,))
            dve_ts(msk.ap, red.ap, -PI, TWO_PI, ALU.is_lt, ALU.mult, (red,), (msk,))
            dve_tt(red.ap, red.ap, msk.ap, ALU.add, (red, msk), (red,))
        act(tab.ap, red.ap, AF.Sin, (red,), (tab,))

    def dump(name, buf, ap):
        if name in dbg_out:
            S.op("sp", lambda e: e.dma_start(out=dbg_out[name], in_=ap), reads=(buf,), dma="dbg_" + name)

    dump("cos", cos_t, cos_t.flat)
    dump("sin", sin_t, sin_t.flat)

    def rms_norm(src_chunks, load_from, tok0, gcol0, write_fn, mid_fn=None):
        ssS = Sess(4)
        for kc in range(8):
            if load_from is not None:
                slot = xr[kc % 3]
                dma_in(slot, slot.ap, load_from[kc * 128:(kc + 1) * 128, tok0:tok0 + TT], "xr%d" % (kc % 3))
                src, sres = slot.ap, slot
            else:
                src, sres = src_chunks[kc].ap, src_chunks[kc]
            sqb = sq[kc % 2]
            act(sqb.ap, src, AF.Square, (sres,), (sqb,))
            for b in range(4):
                ssS.mm(ssS.t[:, b:b + 1], sqb.ap[:, b * 128:(b + 1) * 128], ones_b.ap[:, 0:1], (sqb, ones_b))
        dve_ts(ss_sb.ap, ssS.t[:, 0:4], 1.0 / D, EPS, ALU.mult, ALU.add, (ssS.res,), (ss_sb,))
        act(sd_sb.ap, ss_sb.ap, AF.Sqrt, (ss_sb,), (sd_sb,))
        S.op("dve", lambda e: e.reciprocal(rstd.ap, sd_sb.ap), reads=(sd_sb,), writes=(rstd,))
        dve_tt(diag.ap, ident_f.unsqueeze(1).broadcast_to([128, 4, 128]),
               rstd.ap.unsqueeze(2).broadcast_to([128, 4, 128]), ALU.mult, (cst, rstd), (diag,))
        if mid_fn is not None:
            mid_fn()
        rb = Sess(5)
        rb.mm(rb.t[:, 0:TT], ones_f.ap, diag.flat, (ones_f, diag))
        for kc in range(8):
            if load_from is not None:
                slot = xr[(kc + 2) % 3]
                dma_in(slot, slot.ap, load_from[kc * 128:(kc + 1) * 128, tok0:tok0 + TT], "xr%d" % ((kc + 2) % 3))
                src, sres = slot.ap, slot
            else:
                src, sres = src_chunks[kc].ap, src_chunks[kc]
            write_fn(kc, src, sres, gvec.ap[:, gcol0 + kc:gcol0 + kc + 1], rb.t[:, 0:TT], rb.res)

    def proj_fm(wb, wv, col0, rhs_chunks, kc_n):
        s_ = acc()
        for kc in range(kc_n):
            s_.mm(s_.t[:, 0:TT], wv[:, kc, col0:col0 + 128], rhs_chunks[kc].ap, (wb, rhs_chunks[kc]))
        return s_

    def w_h(kc, src, sres, gcol, rbc, rbres):
        dve_stt(hT[kc].ap, src, gcol, rbc, ALU.mult, ALU.mult, (sres, gvec, rbres), (hT[kc],))

    def emit_S2():
        for j in range(3):
            wb, wv = next_unit()
            for oc in range(2):
                c = 2 * j + oc
                s_ = proj_fm(wb, wv, oc * 128, hT, 8)
                act(ya[c].ap, s_.t[:, 0:TT], AF.Gelu, (s_.res,), (ya[c],))

    for t in range(ntiles):
        tok0 = t * TT
        par = t % 2
        j2, s2 = t // 4, t % 4

        if LIM < 1:
            continue
        if t == 0:
            rms_norm(None, xT, tok0, 0, w_h)
        if t == dbg.get("_tile", 0):
            for kc in range(8):
                if "hT" in dbg_out:
                    S.op("sp", lambda e, kc=kc: e.dma_start(out=dbg_out["hT"][kc * 128:(kc + 1) * 128, :], in_=hT[kc].ap),
                         reads=(hT[kc],), dma="dbg_hT")
        if LIM < 2:
            continue
        if t == 0:
            emit_S2()
        if LIM < 3:
            continue
        for j in range(3):
            wb, wv = next_unit()
            for b in range(4):
                s_ = acc()
                for kc in range(8):
                    s_.mm(s_.t[:, 0:256], hT[kc].ap[:, b * 128:(b + 1) * 128], wv[:, kc, :], (hT[kc], wb))
                act(vg[b].ap[:, j * 256:(j + 1) * 256], s_.t[:, 0:256], AF.Gelu, (s_.res,), (vg[b],))
        for b in range(4):
            S.op("act", lambda e, b=b: e.activation(junk.ap, vg[b].ap, AF.Square, accum_out=ssv.ap[:, b:b + 1]),
                 reads=(vg[b],), writes=(junk, ssv))
        dve_ts(ss_sb.ap, ssv.ap, 1.0 / 768, EPS, ALU.mult, ALU.add, (ssv,), (ss_sb,))
        act(sd_sb.ap, ss_sb.ap, AF.Sqrt, (ss_sb,), (sd_sb,))
        S.op("dve", lambda e: e.reciprocal(rstd_v.ap, sd_sb.ap), reads=(sd_sb,), writes=(rstd_v,))
        for b in range(4):
            dve_ts(wp[b].ap, wsp_f.ap, rstd_v.ap[:, b:b + 1], None, ALU.mult, None, (wsp_f, rstd_v), (wp[b],))

        if LIM < 4.2:
            continue
        for j in range(6):
            wb, wv = next_unit()
            for b in range(4):
                s_ = acc()
                for kc in range(8):
                    s_.mm(s_.t[:, 0:256], hT[kc].ap[:, b * 128:(b + 1) * 128], wv[:, kc, :], (hT[kc], wb))
                stg = stgs[(4 * j + b) % 2]
                copy("act", stg.ap, s_.t[:, 0:256], (s_.res,), (stg,))
                copy("pool", qk_tm[b].ap[:, j * 256:(j + 1) * 256], stg.ap, (stg,), (qk_tm[b],))
                copy("dve", rot[b].ap[:, 4 * j:4 * j + 4, :],
                     stg.ap.rearrange("p (h d) -> p h d", h=4)[:, :, 0:16], (stg,), (rot[b],))
        if LIM < 4.5:
            continue
        for b in range(4):
            B = 4 * t + b
            Cb = cos_t.ap[:, B, :].unsqueeze(1).broadcast_to([128, 24, 8])
            Sb = sin_t.ap[:, B, :].unsqueeze(1).broadcast_to([128, 24, 8])
            t1v = rot[b].ap[:, :, 0:8]
            t2v = rot[b].ap[:, :, 8:16]
            qv = qk_tm[b].ap.rearrange("p (h d) -> p h d", h=24)
            dve_tt(ropeA.ap, t1v, Cb, ALU.mult, (rot[b], cos_t), (ropeA,), eng="pool")
            dve_tt(ropeB.ap, t2v, Sb, ALU.mult, (rot[b], sin_t), (ropeB,), eng="pool")
            dve_tt(qv[:, :, 0:8], ropeA.ap, ropeB.ap, ALU.subtract, (ropeA, ropeB), (qk_tm[b],), eng="pool")
            dve_tt(ropeA.ap, t1v, Sb, ALU.mult, (rot[b], sin_t), (ropeA,), eng="pool")
            dve_tt(ropeB.ap, t2v, Cb, ALU.mult, (rot[b], cos_t), (ropeB,), eng="pool")
            dve_tt(qv[:, :, 8:16], ropeA.ap, ropeB.ap, ALU.add, (ropeA, ropeB), (qk_tm[b],), eng="pool")
        if LIM < 4:
            continue
        for c in range(6):
            s_ = acc()
            f0 = 128 * c
            pieces = []
            f = f0
            while f < f0 + 128:
                g = f // 192
                fe = min(f0 + 128, (g + 1) * 192)
                pieces.append((f, fe, g))
                f = fe
            for b in range(4):
                for (fa, fe, g) in pieces:
                    r0 = fa - f0
                    s_.mm(s_.t[r0:r0 + (fe - fa), b * 128:(b + 1) * 128], vg[b].ap[:, fa:fe], wp[b].ap[:, g, :],
                          (vg[b], wp[b]), row0=r0, rows=fe - fa)
            dve_stt(rden.ap.rearrange("p (b t) -> p b t", b=4), s_.t[:, 0:TT].rearrange("p (b t) -> p b t", b=4),
                    gm.ap[:, c:c + 1], bbc.ap[:, c, :].unsqueeze(1).broadcast_to([128, 4, 128]),
                    ALU.mult, ALU.add, (s_.res, gm, bbc), (rden,))
            dve_tt(ya[c].ap, rden.ap, ya[c].ap, ALU.mult, (rden, ya[c]), (ya[c],), eng="pool")
        if t == dbg.get("_tile", 0) and "ya" in dbg_out:
            for c in range(6):
                S.op("sp", lambda e, c=c: e.dma_start(out=dbg_out["ya"][c * 128:(c + 1) * 128, :], in_=ya[c].ap),
                     reads=(ya[c],), dma="dbg_ya")

        if LIM < 6:
            continue
        for j in range(3):
            wb, wv = next_unit()
            for oc in range(2):
                c = 2 * j + oc
                s_ = proj_fm(wb, wv, oc * 128, hT, 8)
                copy("act", vT[c].ap, s_.t[:, 0:TT], (s_.res,), (vT[c],))
        if LIM < 4.8:
            continue
        for c in range(12):
            s_ = acc()
            tb = s_.t[:, 0:256].bitcast(BF16)
            for b in range(4):
                s_.tr(tb[:, b * 128:(b + 1) * 128], qk_tm[b].ap[:, c * 128:(c + 1) * 128], ident_b.ap, (qk_tm[b], ident_b))
            if c < 6:
                copy("act", qT[c].ap, tb, (s_.res,), (qT[c],))
            else:
                g, cc = (c - 6) // 2, (c - 6) % 2
                if g == 0:
                    copy("act", K0[par][cc].ap, tb, (s_.res,), (K0[par][cc],))
                elif g == 1:
                    copy("act", K1[par][cc].ap, tb, (s_.res,), (K1[par][cc],))
                else:
                    copy("act", K2[cc].ap[:, tok0:tok0 + TT], tb, (s_.res,), (K2[cc],))
        if t == dbg.get("_tile", 0) and "qT" in dbg_out:
            for c in range(6):
                S.op("sp", lambda e, c=c: e.dma_start(out=dbg_out["qT"][c * 128:(c + 1) * 128, :], in_=qT[c].ap),
                     reads=(qT[c],), dma="dbg_qT")

        s_ = acc()
        tb = s_.t[:, 0:512].bitcast(BF16)
        for b in range(4):
            for cc in range(2):
                s_.tr(tb[:, (b * 2 + cc) * 128:(b * 2 + cc + 1) * 128], vT[cc].ap[:, b * 128:(b + 1) * 128], ident_b.ap,
                      (vT[cc], ident_b))
        copy("act", V0[par].flat, tb, (s_.res,), (V0[par],))
        s_ = acc()
        tb = s_.t[:, 0:512].bitcast(BF16)
        for r in range(4):
            for cc in range(2):
                s_.tr(tb[:, (r * 2 + cc) * 128:(r * 2 + cc + 1) * 128], vT[2 + cc].ap[:, r::4], ident_b.ap,
                      (vT[2 + cc], ident_b))
        copy("act", V1[par].flat, tb, (s_.res,), (V1[par],))
        p0 = 32 * s2
        for rq in range(4):
            s_ = acc()
            tb = s_.t[:, 0:512].bitcast(BF16)
            for rr in range(4):
                r = 4 * rq + rr
                for cc in range(2):
                    s_.tr(tb[p0:p0 + 32, (rr * 2 + cc) * 128:(rr * 2 + cc + 1) * 128], vT[4 + cc].ap[:, r::16], ident_b.ap,
                          (vT[4 + cc], ident_b), tp=(0, p0))
            copy("act", V2[j2 % 2].flat[p0:p0 + 32, rq * 1024:(rq + 1) * 1024], tb[p0:p0 + 32, :], (s_.res,), (V2[j2 % 2],))

        if LIM < 9:
            continue
        acc_ring[0] = [0, 1, 2, 3]
        for pair in range(2):
            NS = Sess(4 + pair)
            DS = Sess(6 + pair)
            for half in range(2):
                hg = 2 * pair + half
                hp = 64 * half
                nE = [0]

                def attend(tiles, ncol, mask_ap, dview, col_lo=0):
                    sS = acc()
                    for i, tl in enumerate(tiles):
                        if tl is None:
                            continue
                        (k_ap, kres, v_ap, vres, q_ap, qres, osel) = tl
                        sS.mm(sS.t[:, i * ncol:(i + 1) * ncol], k_ap, q_ap, (kres, qres))
                    E = Eb[nE[0] % 2]
                    nE[0] += 1
                    act(E.ap[:, col_lo:TT], sS.t[:, col_lo:TT], AF.Exp, (sS.res,), (E,), scale=0.125)
                    nt_ = TT // ncol
                    ev = E.ap.rearrange("p (a c) -> p a c", a=nt_)
                    a0 = col_lo // ncol
                    S.op("pool", lambda e: e.tensor_tensor(ev[:, a0:, :], ev[:, a0:, :],
                                                           mask_ap.unsqueeze(1).broadcast_to([128, nt_ - a0, ncol]), ALU.mult),
                         reads=(E, mcur_b, mprev_b), writes=(E,))
                    for i, tl in enumerate(tiles):
                        if tl is None:
                            continue
                        (k_ap, kres, v_ap, vres, q_ap, qres, osel) = tl
                        NS.mm(osel(NS.t[hp:hp + 64, :]), v_ap, E.ap[:, i * ncol:(i + 1) * ncol], (vres, E), row0=hp, rows=64)
                    dv_ = dview(DS.t[hp:hp + 64, :])
                    er_ = E.ap[:, col_lo:TT]
                    if len(dv_.shape) == 3:
                        er_ = er_.rearrange("p (r i) -> p r i", r=dv_.shape[1])
                    DS.mm(dv_, ones_b.ap[:, 0:64], er_, (ones_b, E), row0=hp, rows=64)

                c0 = half_c = pair
                qc = qT[0 + pair]
                cur, prv = [], []
                for b in range(4):
                    q_ap = qc.ap[hp:hp + 64, b * 128:(b + 1) * 128]
                    osel = (lambda a, b=b: a[:, b * 128:(b + 1) * 128])
                    cur.append((K0[par][pair].ap[hp:hp + 64, b * 128:(b + 1) * 128], K0[par][pair],
                                V0[par].ap[:, b, hg * 64:(hg + 1) * 64], V0[par], q_ap, qc, osel))
                    if b >= 1:
                        prv.append((K0[par][pair].ap[hp:hp + 64, (b - 1) * 128:b * 128], K0[par][pair],
                                    V0[par].ap[:, b - 1, hg * 64:(hg + 1) * 64], V0[par], q_ap, qc, osel))
                    elif t >= 1:
                        prv.append((K0[1 - par][pair].ap[hp:hp + 64, 384:512], K0[1 - par][pair],
                                    V0[1 - par].ap[:, 3, hg * 64:(hg + 1) * 64], V0[1 - par], q_ap, qc, osel))
                    else:
                        prv.append(None)
                attend(cur, 128, mcur_b.ap, lambda a: a[:, 0:TT])
                lo = 0 if t >= 1 else 128
                attend(prv, 128, mprev_b.ap, lambda a, lo=lo: a[:, lo:TT], col_lo=lo)
                qc = qT[2 + pair]
                cur, prv = [], []
                for r in range(4):
                    q_ap = qc.ap[hp:hp + 64, r::4]
                    osel = (lambda a, r=r: a[:, r::4])
                    cur.append((K1[par][pair].ap[hp:hp + 64, r::4], K1[par][pair],
                                V1[par].ap[:, r, hg * 64:(hg + 1) * 64], V1[par], q_ap, qc, osel))
                    prv.append((K1[1 - par][pair].ap[hp:hp + 64, r::4], K1[1 - par][pair],
                                V1[1 - par].ap[:, r, hg * 64:(hg + 1) * 64], V1[1 - par], q_ap, qc, osel))
                dv1 = lambda a: a.rearrange("p (i r) -> p r i", r=4)
                attend(cur, 128, mcur_b.ap, dv1)
                if t >= 1:
                    attend(prv, 128, mprev_b.ap, dv1)
                qc = qT[4 + pair]
                cur, prv = [], []
                for r in range(16):
                    q_ap = qc.ap[hp:hp + 64, r::16]
                    osel = (lambda a, r=r: a[:, r::16])
                    kc_ap = K2[pair].ap[hp:hp + 64, 2048 * j2 + r:2048 * (j2 + 1):16]
                    cur.append((kc_ap, K2[pair], V2[j2 % 2].ap[:, r, hg * 64:(hg + 1) * 64], V2[j2 % 2], q_ap, qc, osel))
                    if j2 >= 1:
                        kp_ap = K2[pair].ap[hp:hp + 64, 2048 * (j2 - 1) + r:2048 * j2:16]
                        prv.append((kp_ap, K2[pair], V2[(j2 - 1) % 2].ap[:, r, hg * 64:(hg + 1) * 64], V2[(j2 - 1) % 2],
                                    q_ap, qc, osel))
                dv2 = lambda a: a.rearrange("p (i r) -> p r i", r=16)
                attend(cur, 32, mcur_b.ap[:, p0:p0 + 32], dv2)
                if j2 >= 1:
                    attend(prv, 32, mprev_b.ap[:, p0:p0 + 32], dv2)
            S.op("dve", lambda e, DS=DS: e.reciprocal(rden.ap, DS.t[:, 0:TT]), reads=(DS.res,), writes=(rden,))
            dve_tt(yb[pair].ap, NS.t[:, 0:TT], rden.ap, ALU.mult, (NS.res, rden), (yb[pair],))
        if t == dbg.get("_tile", 0) and "yb" in dbg_out:
            for c in range(2):
                S.op("sp", lambda e, c=c: e.dma_start(out=dbg_out["yb"][c * 128:(c + 1) * 128, :], in_=yb[c].ap),
                     reads=(yb[c],), dma="dbg_yb")

        if LIM < 10:
            continue
        acc_ring[0] = [0, 1, 2, 3, 6, 7]
        for j in range(8):
            wb, wv = next_unit()
            for oc in range(2):
                c = 2 * j + oc
                s_ = proj_fm(wb, wv, oc * 128, hT, 8)
                act(tg[c].ap, s_.t[:, 0:TT], AF.Tanh, (s_.res,), (tg[c],), scale=0.5)

        if LIM < 11:
            continue
        for j in range(4):
            wba, wva = next_unit()
            wbb, wvb = next_unit()
            for oc in range(2):
                m = 2 * j + oc
                sa = proj_fm(wba, wva, oc * 128, ya, 6)
                sb = proj_fm(wbb, wvb, oc * 128, yb, 2)
                dve_stt(t1b.ap, tg[m].ap, 1.0, sa.t[:, 0:TT], ALU.add, ALU.mult, (tg[m], sa.res), (t1b,))
                dve_stt(t2b.ap, tg[8 + m].ap, 1.0, sb.t[:, 0:TT], ALU.add, ALU.mult, (tg[8 + m], sb.res), (t2b,))
                dve_tt(mrg[m].ap, t1b.ap, t2b.ap, ALU.add, (t1b, t2b), (mrg[m],), eng="pool")

        if LIM < 12:
            continue
        for j in range(4):
            wb, wv = next_unit()
            for oc in range(2):
                m = 2 * j + oc
                s_ = proj_fm(wb, wv, oc * 128, mrg, 8)
                slot = xr[m % 3]
                dma_in(slot, slot.ap, xT[m * 128:(m + 1) * 128, tok0:tok0 + TT], "xr%d" % (m % 3))
                dve_stt(x1[m].ap, s_.t[:, 0:TT], 0.5, slot.ap, ALU.mult, ALU.add, (s_.res, slot), (x1[m],))
        if t == dbg.get("_tile", 0) and "x1" in dbg_out:
            for c in range(8):
                S.op("sp", lambda e, c=c: e.dma_start(out=dbg_out["x1"][c * 128:(c + 1) * 128, :], in_=x1[c].ap),
                     reads=(x1[c],), dma="dbg_x1")

        if LIM < 13:
            continue
        rms_norm(x1, None, tok0, 8, w_h)

        if LIM < 14:
            continue
        for j in range(11):
            wba, wva = next_unit()
            wbv, wvv = next_unit()
            for oc in range(2):
                c = 2 * j + oc
                sa = proj_fm(wba, wva, oc * 128, hT, 8)
                sv = proj_fm(wbv, wvv, oc * 128, hT, 8)
                o = ob[c % 2]
                g_ = gel[c % 2]
                ab = abuf[c % 2]
                w0 = cw.ap[:, c, 0:1]
                w1 = cw.ap[:, c, 1:2]
                w2 = cw.ap[:, c, 2:3]
                copy("pool", ab.ap[:, 0:2], halo.ap[:, c, :], (halo,), (ab,))
                copy("act", ab.ap[:, 2:TT + 2], sa.t[:, 0:TT], (sa.res,), (ab,))
                S.op("act", lambda e, o=o, ab=ab, w2=w2, c=c: e.activation(o.ap, ab.ap[:, 2:TT + 2], AF.Identity,
                                                                    bias=cb.ap[:, c:c + 1], scale=w2),
                     reads=(ab, cw, cb), writes=(o,))
                copy("pool", halo.ap[:, c, :], ab.ap[:, TT:TT + 2], (ab,), (halo,))
                dve_stt(o.ap, ab.ap[:, 1:TT + 1], w1, o.ap, ALU.mult, ALU.add, (ab, cw, o), (o,))
                dve_stt(o.ap, ab.ap[:, 0:TT], w0, o.ap, ALU.mult, ALU.add, (ab, cw, o), (o,))
                act(g_.ap, o.ap, AF.Gelu, (o,), (g_,))
                dve_tt(gg[c].ap, g_.ap, sv.t[:, 0:TT], ALU.mult, (g_, sv.res), (gg[c],))

        if LIM < 15:
            continue
        for j in range(4):
            wb0, wv0 = next_unit()
            wb1, wv1 = next_unit()
            for oc in range(2):
                m = 2 * j + oc
                s_ = acc()
                for kc in range(NFF):
                    wbx, wvx = (wb0, wv0) if kc < 11 else (wb1, wv1)
                    s_.mm(s_.t[:, 0:TT], wvx[:, kc % 11, oc * 128:(oc + 1) * 128], gg[kc].ap, (wbx, gg[kc]))
                dve_tt(x1[m].ap, s_.t[:, 0:TT], x1[m].ap, ALU.add, (s_.res, x1[m]), (x1[m],))

        if LIM < 16:
            continue
        def w_o(kc, src, sres, gcol, rbc, rbres):
            dve_stt(x1[kc].ap, src, gcol, rbc, ALU.mult, ALU.mult, (sres, gvec, rbres), (x1[kc],))
            S.op("sp", lambda e, kc=kc, tok0=tok0: e.dma_start(out=outT[kc * 128:(kc + 1) * 128, tok0:tok0 + TT], in_=x1[kc].ap),
                 reads=(x1[kc],), dma="out%d" % kc)
        if t + 1 < ntiles:
            rms_norm(None, xT, tok0 + TT, 0, w_h)
            rms_norm(x1, None, tok0, 16, w_o, mid_fn=emit_S2)
        else:
            rms_norm(x1, None, tok0, 16, w_o)

    sem_ctx = []
    esem = {}
    for e in ("pe", "act", "dve", "pool"):
        c_ = nc.semaphore("sem_" + e)
        esem[e] = c_.__enter__()
        sem_ctx.append(c_)
    dsem = {}
    for name in S.dma_cum:
        c_ = nc.semaphore("dsem_" + name)
        dsem[name] = c_.__enter__()
        sem_ctx.append(c_)
    with nc.Block() as block:
        S.emit(nc, block, esem, dsem)
    for c_ in reversed(sem_ctx):
        c_.__exit__(None, None, None)
    for c_ in reversed(ctx):
        c_.__exit__(None, None, None)
    return nc


def make_shared(mix_norm_g, w_in, gmlp_norm_g, w_spatial, b_spatial, w_branch_a, w_branch_b, w_out,
                ffn_norm_g, w_up, conv_w, conv_b, w_down, final_norm_g):
    f32 = np.float32
    A = lambda a: np.ascontiguousarray(np.asarray(a, dtype=f32))

    def pk(v):
        v = np.asarray(v, dtype=f32)
        return v.reshape(-1, 128).T

    gvec = np.concatenate([pk(mix_norm_g[0]), pk(ffn_norm_g[0]), pk(final_norm_g)], axis=1)
    gm = pk(gmlp_norm_g[0])
    bsp = np.asarray(b_spatial[0], dtype=f32)
    grp = (np.arange(768) // 192).reshape(6, 128)
    bbc = bsp[grp]
    bbc = np.transpose(bbc, (1, 0, 2)).reshape(128, 6 * 128)
    wspT = np.transpose(np.asarray(w_spatial[0], dtype=f32), (2, 0, 1)).reshape(128, 4 * 128)
    cwv = np.asarray(conv_w[0], dtype=f32)
    cwl = np.transpose(cwv.reshape(3, NFF, 128), (2, 1, 0)).reshape(128, NFF * 3)
    cbl = pk(conv_b[0])
    ident = np.eye(128, dtype=f32)
    kk = np.arange(128)[:, None]
    qq = np.arange(128)[None, :]
    mcur = (kk <= qq).astype(f32)
    mprev = (kk >= qq).astype(f32)
    invf = (np.float32(500000.0) ** (-np.arange(0, 16, 2, dtype=f32) / np.float32(16))).astype(f32)
    consts = np.concatenate([ident, mcur, mprev, np.broadcast_to(invf[None, :], (128, 8))], axis=1)
    return {
        "w_in": A(w_in[0]), "w_pa": A(w_branch_a[0]), "w_pb": A(w_branch_b[0]), "w_out": A(w_out[0]),
        "w_up": A(w_up[0]), "w_down": A(w_down[0]),
        "gvec": A(gvec), "gm": A(gm), "bbc": A(bbc), "wspT": A(wspT), "convw": A(cwl), "convb": A(cbl),
        "consts": A(consts),
    }


_NC_CACHE = {}


def kernel(x, positions, mix_norm_g, w_in, gmlp_norm_g, w_spatial, b_spatial, w_branch_a, w_branch_b, w_out,
           ffn_norm_g, w_up, conv_w, conv_b, w_down, final_norm_g):
    x = np.asarray(x, dtype=np.float32)
    positions = np.asarray(positions)
    shared = make_shared(mix_norm_g, w_in, gmlp_norm_g, w_spatial, b_spatial, w_branch_a, w_branch_b, w_out,
                         ffn_norm_g, w_up, conv_w, conv_b, w_down, final_norm_g)
    n = x.shape[0]
    in_maps = []
    for b in range(n):
        m = dict(shared)
        m["xT"] = np.ascontiguousarray(x[b].T)
        m["pos"] = np.ascontiguousarray(positions[b].astype(np.int32).reshape(32, 128).T)
        in_maps.append(m)
    if "nc" not in _NC_CACHE:
        _NC_CACHE["nc"] = build_nc()
    res = run_bass_kernel_spmd(_NC_CACHE["nc"], in_maps, core_ids=list(range(n)))
    out = np.stack([np.ascontiguousarray(r["outT"].T) for r in res.results], axis=0)
    return out.astype(np.float32)
```

```python
import numpy as np
import concourse.bass as bass
import concourse.mybir as mybir
from concourse.bass_utils import run_bass_kernel_spmd

F32 = mybir.dt.float32
BF16 = mybir.dt.bfloat16
I32 = mybir.dt.int32
AF = mybir.ActivationFunctionType
ALU = mybir.AluOpType

D = 1024
SEQ = 4096
TT = 512
NT = SEQ // TT
INW = 5888
DFF = 2816
NFF = DFF // 128
EPS = 1e-6
KB = 1024


class Res:
    _n = 0

    def __init__(self, name, lo=None, hi=None):
        Res._n += 1
        self.rid = Res._n
        self.name = name
        self.lo, self.hi = lo, hi
        self.ov = [self]


class Sched:
    ENG = ("pe", "act", "dve", "pool", "sp")

    def __init__(self):
        self.ops = {e: [] for e in self.ENG}
        self.lastw = {}
        self.readers = {}
        self.dma_cum = {}

    def op(self, eng, fn, reads=(), writes=(), dma=None):
        deps = set()
        for b in reads:
            ev = self.lastw.get(b.rid)
            if ev is not None:
                deps.add(ev)
        for b in writes:
            for x in b.ov:
                ev = self.lastw.get(x.rid)
                if ev is not None:
                    deps.add(ev)
                for r in self.readers.get(x.rid, ()):
                    deps.add(r)
        if dma is not None:
            v = self.dma_cum.get(dma, 0) + 16
            self.dma_cum[dma] = v
            ev = ("dma:" + dma, v)
        else:
            ev = (eng, len(self.ops[eng]))
        self.ops[eng].append((fn, deps, dma))
        for b in reads:
            self.readers.setdefault(b.rid, []).append(ev)
        for b in writes:
            for x in b.ov:
                self.lastw[x.rid] = ev
                self.readers[x.rid] = []
        return ev

    def emit(self, nc, block, esem, dsem):
        comp = ("pe", "act", "dve", "pool")
        needed = {e: set() for e in comp}
        for e in self.ENG:
            for (_, deps, _) in self.ops[e]:
                for (k, v) in deps:
                    if k in needed and not (k == "pe" and e == "pe"):
                        needed[k].add(v)
        rank = {e: {idx: i + 1 for i, idx in enumerate(sorted(needed[e]))} for e in comp}
        ops = self.ops

        def run(ename, eng):
            known = {}
            for idx, (fn, deps, dma) in enumerate(ops[ename]):
                want = {}
                for (k, v) in deps:
                    if k == "pe" and ename == "pe":
                        continue
                    if k in rank:
                        val = rank[k][v]
                    else:
                        val = v
                    if val > want.get(k, 0):
                        want[k] = val
                for k, val in want.items():
                    if known.get(k, 0) >= val:
                        continue
                    known[k] = val
                    sem = esem[k] if k in esem else dsem[k[4:]]
                    eng.wait_ge(sem, val)
                ins = fn(eng)
                if dma is not None:
                    ins.then_inc(dsem[dma], 16)
                elif ename in rank and idx in rank[ename]:
                    ins.then_inc(esem[ename], 1)

        @block.tensor
        def _(eng):
            run("pe", eng)

        @block.scalar
        def _(eng):
            run("act", eng)

        @block.vector
        def _(eng):
            run("dve", eng)

        @block.gpsimd
        def _(eng):
            run("pool", eng)

        @block.sync
        def _(eng):
            run("sp", eng)
            for name, v in self.dma_cum.items():
                eng.wait_ge(dsem[name], v)


def build_nc(ntiles=NT, dbg=None):
    nc = bass.Bass("TRN2", target_bir_lowering=False)
    S = Sched()
    dbg = dbg or {}
    LIM = dbg.get("_lim", 99)

    def din(name, shape, dt=F32):
        return nc.dram_tensor(name, list(shape), dt, kind="ExternalInput").ap()

    xT = din("xT", [D, SEQ])
    pos_d = din("pos", [128, 32], I32)
    w_in = din("w_in", [D, INW])
    w_pa = din("w_pa", [768, D])
    w_pb = din("w_pb", [256, D])
    w_out = din("w_out", [D, D])
    w_up = din("w_up", [D, 2 * DFF])
    w_down = din("w_down", [DFF, D])
    gvec_d = din("gvec", [128, 24])
    gm_d = din("gm", [128, 6])
    bbc_d = din("bbc", [128, 6 * 128])
    wsp_d = din("wspT", [128, 4 * 128])
    cw_d = din("convw", [128, NFF * 3])
    cb_d = din("convb", [128, NFF])
    cst_d = din("consts", [128, 3 * 128 + 8])
    outT = nc.dram_tensor("outT", [D, SEQ], F32, kind="ExternalOutput").ap()
    dbg_out = {}
    for name, (shape, dt) in dbg.items():
        dbg_out[name] = nc.dram_tensor("dbg_" + name, list(shape), dt, kind="ExternalOutput").ap()

    ARENA_BYTES = 207 * KB
    ctx = []
    arena_t = nc.sbuf_tensor("arena", [128, ARENA_BYTES // 4], F32)
    arena = arena_t.__enter__()
    ctx.append(arena_t)
    banks = []
    for i in range(8):
        t = nc.psum_tensor("ps%d" % i, [128, 512], F32)
        banks.append(t.__enter__())
        ctx.append(t)
    bankres = [Res("ps%d" % i) for i in range(8)]

    allocs = []
    top = [0]

    class Buf(Res):
        def __init__(self, name, shape, dt, at=None):
            esz = 4 if dt in (F32, I32) else 2
            n = 1
            for s_ in shape:
                n *= s_
            nbytes = n * esz
            if at is None:
                off = (top[0] + 63) // 64 * 64
                top[0] = off + nbytes
            else:
                off = at
            assert off % 4 == 0 and off + nbytes <= ARENA_BYTES, (name, off, nbytes)
            Res.__init__(self, name, off, off + nbytes)
            for o in allocs:
                if o.lo < self.hi and self.lo < o.hi:
                    o.ov.append(self)
                    self.ov.append(o)
            allocs.append(self)
            v = arena[:, off // 4:(off + nbytes) // 4]
            if dt != F32:
                v = v.bitcast(dt)
            self.flat = v
            if len(shape) == 1:
                self.ap = v
            elif len(shape) == 2:
                self.ap = v.rearrange("p (a b) -> p a b", a=shape[0])
            elif len(shape) == 3:
                self.ap = v.rearrange("p (a b c) -> p a b c", a=shape[0], b=shape[1])
            else:
                raise ValueError(shape)
            self.shape = shape
            self.dt = dt

    cst = Buf("cst", [3 * 128 + 8], F32)
    ident_f = cst.ap[:, 0:128]
    mcur_f = cst.ap[:, 128:256]
    mprev_f = cst.ap[:, 256:384]
    invf = cst.ap[:, 384:392]
    ident_b = Buf("ident_b", [128], BF16)
    mcur_b = Buf("mcur_b", [128], BF16)
    mprev_b = Buf("mprev_b", [128], BF16)
    ones_f = Buf("ones_f", [128], F32)
    ones_b = Buf("ones_b", [128], BF16)
    gvec = Buf("gvec", [24], F32)
    gm = Buf("gm", [6], F32)
    bbc = Buf("bbc", [6, 128], F32)
    wsp_f = Buf("wsp_f", [4, 128], F32)
    wsp_b = Buf("wsp_b", [4, 128], BF16)
    cw = Buf("cw", [NFF, 3], F32)
    cb = Buf("cb", [NFF], F32)
    pos_i = Buf("pos_i", [32], I32)
    pos_f = Buf("pos_f", [32], F32)
    cos_t = Buf("cos_t", [32, 8], F32)
    sin_t = Buf("sin_t", [32, 8], F32)
    halo = Buf("halo", [NFF, 2], F32)
    ss_sb = Buf("ss_sb", [4], F32)
    sd_sb = Buf("sd_sb", [4], F32)
    rstd = Buf("rstd", [4], F32)
    rstd_v = Buf("rstd_v", [4], F32)
    ssv = Buf("ssv", [4], F32)
    diag = Buf("diag", [4, 128], F32)
    junk = Buf("junk", [768], BF16)
    ropeA = Buf("ropeA", [24, 8], F32)
    ropeB = Buf("ropeB", [24, 8], F32)
    xr = [Buf("xr%d" % i, [TT], F32) for i in range(3)]
    sq = [Buf("sq%d" % i, [TT], BF16) for i in range(2)]
    hT = [Buf("hT%d" % i, [TT], BF16) for i in range(8)]
    x1 = [Buf("x1_%d" % i, [TT], F32) for i in range(8)]
    K0 = [[Buf("K0_%d_%d" % (p, c), [TT], BF16) for c in range(2)] for p in range(2)]
    K1 = [[Buf("K1_%d_%d" % (p, c), [TT], BF16) for c in range(2)] for p in range(2)]
    K2 = [Buf("K2_%d" % c, [SEQ], BF16) for c in range(2)]
    V0 = [Buf("V0_%d" % p, [4, 256], BF16) for p in range(2)]
    V1 = [Buf("V1_%d" % p, [4, 256], BF16) for p in range(2)]
    V2 = [Buf("V2_%d" % p, [16, 256], BF16) for p in range(2)]
    USZ = 11 * 256
    wst = [Buf("wst%d" % i, [USZ], F32) for i in range(2)]
    wbf = [Buf("wbf%d" % i, [USZ], BF16) for i in range(6)]
    P0 = (top[0] + 63) // 64 * 64

    def PB(name, shape, dt, kb_off):
        return Buf(name, shape, dt, at=P0 + int(kb_off * KB))

    ya = [PB("ya%d" % c, [TT], BF16, 0 + c) for c in range(6)]
    vg = [PB("vg%d" % b, [768], BF16, 6 + 1.5 * b) for b in range(4)]
    wp = [PB("wp%d" % b, [4, 128], BF16, 12 + b) for b in range(4)]
    qk_tm = [PB("qk%d" % b, [1536], BF16, 16 + 3 * b) for b in range(4)]
    rot = [PB("rot%d" % b, [24, 16], F32, 28 + 1.5 * b) for b in range(4)]
    qT = [PB("qT%d" % c, [TT], BF16, 34 + c) for c in range(6)]
    vT = [PB("vT%d" % c, [TT], BF16, 40 + c) for c in range(6)]
    Eb = [PB("E%d" % i, [TT], BF16, 46 + i) for i in range(2)]
    rden = PB("rden", [TT], F32, 48)
    stgs = [PB("stg%d" % i, [256], F32, 48 + i) for i in range(2)]
    yb = [PB("yb%d" % i, [TT], BF16, 50 + i) for i in range(2)]
    tg = [PB("tg%d" % c, [TT], BF16, 16 + c) for c in range(16)]
    t1b = PB("t1b", [TT], F32, 6)
    t2b = PB("t2b", [TT], F32, 8)
    mrg = [PB("mrg%d" % c, [TT], BF16, 40 + c) for c in range(8)]
    gg = [PB("gg%d" % c, [TT], BF16, 0 + c) for c in range(NFF)]
    ob = [PB("ob%d" % i, [TT], F32, 22 + 2 * i) for i in range(2)]
    gel = [PB("gel%d" % i, [TT], F32, 26 + 2 * i) for i in range(2)]
    abuf = [PB("abuf%d" % i, [TT + 2], F32, 30 + 2.25 * i) for i in range(2)]
    assert P0 + 52 * KB <= ARENA_BYTES, P0

    def dma_in(dst_buf, dst_ap, src_ap, sem, extra_w=()):
        S.op("sp", lambda e: e.dma_start(out=dst_ap, in_=src_ap), reads=(), writes=(dst_buf,) + tuple(extra_w), dma=sem)

    def act(out, in_, func, reads, writes, **kw):
        return S.op("act", lambda e: e.activation(out, in_, func, **kw), reads=reads, writes=writes)

    def dve_tt(out, in0, in1, op, reads, writes, eng="dve"):
        return S.op(eng, lambda e: e.tensor_tensor(out, in0, in1, op), reads=reads, writes=writes)

    def dve_ts(out, in0, s1, s2, op0, op1, reads, writes, eng="dve"):
        if op1 is None:
            return S.op(eng, lambda e: e.tensor_scalar(out, in0, s1, None, op0), reads=reads, writes=writes)
        return S.op(eng, lambda e: e.tensor_scalar(out, in0, s1, s2, op0, op1), reads=reads, writes=writes)

    def dve_stt(out, in0, scalar, in1, op0, op1, reads, writes):
        return S.op("dve", lambda e: e.scalar_tensor_tensor(out, in0, scalar, in1, op0, op1), reads=reads, writes=writes)

    def copy(eng, out, in_, reads, writes):
        if eng == "act":
            return S.op("act", lambda e: e.activation(out, in_, AF.Copy), reads=reads, writes=writes)
        return S.op(eng, lambda e: e.tensor_copy(out, in_), reads=reads, writes=writes)

    class Sess:
        def __init__(self, bi):
            self.bi = bi
            self.res = bankres[bi]
            self.t = banks[bi]
            self.started = set()

        def mm(self, out, lhsT, rhs, reads, row0=0, rows=128, tp=None):
            qs = set(range(row0 // 32, (row0 + rows + 31) // 32))
            first = not (qs & self.started)
            assert first or qs <= self.started
            self.started |= qs
            kw = {}
            if tp is not None:
                kw["tile_position"] = tp
            S.op("pe", lambda e: e.matmul(out, lhsT, rhs, start=first, stop=True, skip_group_check=True, **kw),
                 reads=reads, writes=(self.res,))

        def tr(self, out, in_, ident, reads, tp=None):
            kw = {}
            if tp is not None:
                kw["tile_position"] = tp
            S.op("pe", lambda e: e.transpose(out, in_, ident, **kw), reads=reads, writes=(self.res,))

    accn = [0]

    acc_ring = [[0, 1, 2, 3, 6, 7]]

    def acc():
        r_ = acc_ring[0]
        b = r_[accn[0] % len(r_)]
        accn[0] += 1
        return Sess(b)

    units = []

    def add_unit(w, r0, kc, c0):
        units.append((w[r0:r0 + kc * 128, c0:c0 + 256], kc))

    for t in range(ntiles):
        for j in range(23):
            add_unit(w_in, 0, 8, 256 * j)
        for j in range(4):
            add_unit(w_pa, 0, 6, 256 * j)
            add_unit(w_pb, 0, 2, 256 * j)
        for j in range(4):
            add_unit(w_out, 0, 8, 256 * j)
        for j in range(11):
            add_unit(w_up, 0, 8, 256 * j)
            add_unit(w_up, 0, 8, DFF + 256 * j)
        for j in range(4):
            add_unit(w_down, 0, 11, 256 * j)
            add_unit(w_down, 11 * 128, 11, 256 * j)
    uloaded = [0]
    uused = [0]
    PREF = 4
    UPT = len(units) // ntiles
    wscr = nc.dram_tensor("wscr", [UPT, 128, USZ], BF16, kind="Internal").ap()
    scr_res = [Res("wscr%d" % u) for u in range(UPT)]

    def _writeback(i):
        src, kc = units[i]
        wb = wbf[i % 6]
        u = i % UPT
        S.op("sp", lambda e: e.dma_start(out=wscr[u, :, 0:kc * 256], in_=wb.ap[:, 0:kc * 256]),
             reads=(wb,), writes=(scr_res[u],), dma="wsb%d" % (i % 6))

    def _load(i):
        src, kc = units[i]
        wb = wbf[i % 6]
        u = i % UPT
        if i < UPT:
            st = wst[i % 2]
            dst = st.ap[:, 0:kc * 256].rearrange("p (k c) -> p k c", k=kc)
            S.op("sp", lambda e: e.dma_start(out=dst, in_=src.rearrange("(k p) c -> p k c", p=128)),
                 writes=(st,), dma="wst%d" % (i % 2))
            if ntiles > 1 and i >= 1:
                _writeback(i - 1)
            if i % 3 == 0:
                S.op("pool", lambda e: e.tensor_copy(wb.ap[:, 0:kc * 256], st.ap[:, 0:kc * 256]), reads=(st,), writes=(wb,))
            elif i % 3 == 1:
                S.op("dve", lambda e: e.tensor_copy(wb.ap[:, 0:kc * 256], st.ap[:, 0:kc * 256]), reads=(st,), writes=(wb,))
            else:
                S.op("act", lambda e: e.activation(wb.ap[:, 0:kc * 256], st.ap[:, 0:kc * 256], AF.Copy), reads=(st,), writes=(wb,))
        else:
            if i == UPT:
                _writeback(UPT - 1)
            S.op("sp", lambda e: e.dma_start(out=wb.ap[:, 0:kc * 256], in_=wscr[u, :, 0:kc * 256]),
                 reads=(scr_res[u],), writes=(wb,), dma="wld%d" % (i % 6))

    def next_unit():
        i = uused[0]
        uused[0] += 1
        while uloaded[0] < min(len(units), i + 1 + PREF):
            _load(uloaded[0])
            uloaded[0] += 1
        kc = units[i][1]
        wb = wbf[i % 6]
        return wb, wb.ap[:, 0:kc * 256].rearrange("p (k c) -> p k c", k=kc)

    dma_in(cst, cst.ap, cst_d, "c_cst")
    dma_in(gvec, gvec.ap, gvec_d, "c_gvec")
    dma_in(gm, gm.ap, gm_d, "c_gm")
    dma_in(bbc, bbc.flat, bbc_d, "c_bbc")
    dma_in(wsp_f, wsp_f.flat, wsp_d, "c_wsp")
    dma_in(cw, cw.flat, cw_d, "c_cw")
    dma_in(cb, cb.ap, cb_d, "c_cb")
    dma_in(pos_i, pos_i.ap, pos_d, "c_pos")
    copy("dve", ident_b.ap, ident_f, (cst,), (ident_b,))
    copy("dve", mcur_b.ap, mcur_f, (cst,), (mcur_b,))
    copy("dve", mprev_b.ap, mprev_f, (cst,), (mprev_b,))
    S.op("pool", lambda e: e.memset(ones_f.ap, 1.0), writes=(ones_f,))
    S.op("pool", lambda e: e.memset(ones_b.ap, 1.0), writes=(ones_b,))
    S.op("pool", lambda e: e.memset(halo.flat, 0.0), writes=(halo,))
    for c in range(2):
        S.op("pool", lambda e, c=c: e.memset(K2[c].ap, 0.0), writes=(K2[c],))
    for p in range(2):
        S.op("pool", lambda e, p=p: e.memset(V2[p].flat, 0.0), writes=(V2[p],))
    dve_tt(wsp_f.ap, wsp_f.ap, mcur_f.unsqueeze(1).broadcast_to([128, 4, 128]), ALU.mult, (wsp_f, cst), (wsp_f,))
    copy("dve", pos_f.ap, pos_i.ap, (pos_i,), (pos_f,))
    ang = Buf("ang", [32, 8], F32, at=P0)
    kq = Buf("kq", [32, 8], F32, at=P0 + 2 * KB)
    ki = Buf("ki", [32, 8], I32, at=P0 + 4 * KB)
    red = Buf("red", [32, 8], F32, at=P0 + 6 * KB)
    dve_tt(ang.ap, pos_f.ap.unsqueeze(2).broadcast_to([128, 32, 8]), invf.unsqueeze(1).broadcast_to([128, 32, 8]),
           ALU.mult, (pos_f, cst), (ang,))
    TWO_PI = 2.0 * np.pi
    C1 = float(np.float32(6.28125))
    C2 = float(np.float32(TWO_PI - 6.28125))
    C3 = float(TWO_PI - 6.28125 - float(np.float32(TWO_PI - 6.28125)))
    msk = Buf("msk", [32, 8], F32, at=P0 + 8 * KB)
    PI = float(np.pi)
    for (tab, shift) in ((sin_t, 0.0), (cos_t, float(np.pi / 2))):
        dve_ts(kq.ap, ang.ap, float(1.0 / TWO_PI), None, ALU.mult, None, (ang,), (kq,))
        copy("dve", ki.ap, kq.ap, (kq,), (ki,))
        copy("dve", kq.ap, ki.ap, (ki,), (kq,))
        dve_stt(red.ap, kq.ap, -C1, ang.ap, ALU.mult, ALU.add, (kq, ang), (red,))
        dve_stt(red.ap, kq.ap, -C2, red.ap, ALU.mult, ALU.add, (kq, red), (red,))
        dve_stt(red.ap, kq.ap, -C3, red.ap, ALU.mult, ALU.add, (kq, red), (red,))
        if shift != 0.0:
            dve_ts(red.ap, red.ap, shift, None, ALU.add, None, (red,), (red,))
        for _ in range(2):
            dve_ts(msk.ap, red.ap, PI, -TWO_PI, ALU.is_gt, ALU.mult, (red,), (msk,))
            dve_tt(red.ap, red.ap, msk.ap, ALU.add, (red, msk), (red,))
            dve_ts(msk.ap, red.ap, -PI, TWO_PI, ALU.is_lt, ALU.mult, (red,), (msk,))
            dve_tt(red.ap, red.ap, msk.ap, ALU.add, (red, msk), (red,))
        act(tab.ap, red.ap, AF.Sin, (red,), (tab,))

    def dump(name, buf, ap):
        if name in dbg_out:
            S.op("sp", lambda e: e.dma_start(out=dbg_out[name], in_=ap), reads=(buf,), dma="dbg_" + name)

    dump("cos", cos_t, cos_t.flat)
    dump("sin", sin_t, sin_t.flat)

    def rms_norm(src_chunks, load_from, tok0, gcol0, write_fn):
        ssS = Sess(4)
        for kc in range(8):
            if load_from is not None:
                slot = xr[kc % 3]
                dma_in(slot, slot.ap, load_from[kc * 128:(kc + 1) * 128, tok0:tok0 + TT], "xr%d" % (kc % 3))
                src, sres = slot.ap, slot
            else:
                src, sres = src_chunks[kc].ap, src_chunks[kc]
            sqb = sq[kc % 2]
            act(sqb.ap, src, AF.Square, (sres,), (sqb,))
            for b in range(4):
                ssS.mm(ssS.t[:, b:b + 1], sqb.ap[:, b * 128:(b + 1) * 128], ones_b.ap[:, 0:1], (sqb, ones_b))
        dve_ts(ss_sb.ap, ssS.t[:, 0:4], 1.0 / D, EPS, ALU.mult, ALU.add, (ssS.res,), (ss_sb,))
        act(sd_sb.ap, ss_sb.ap, AF.Sqrt, (ss_sb,), (sd_sb,))
        S.op("dve", lambda e: e.reciprocal(rstd.ap, sd_sb.ap), reads=(sd_sb,), writes=(rstd,))
        dve_tt(diag.ap, ident_f.unsqueeze(1).broadcast_to([128, 4, 128]),
               rstd.ap.unsqueeze(2).broadcast_to([128, 4, 128]), ALU.mult, (cst, rstd), (diag,))
        rb = Sess(5)
        rb.mm(rb.t[:, 0:TT], ones_f.ap, diag.flat, (ones_f, diag))
        for kc in range(8):
            if load_from is not None:
                slot = xr[(kc + 2) % 3]
                dma_in(slot, slot.ap, load_from[kc * 128:(kc + 1) * 128, tok0:tok0 + TT], "xr%d" % ((kc + 2) % 3))
                src, sres = slot.ap, slot
            else:
                src, sres = src_chunks[kc].ap, src_chunks[kc]
            write_fn(kc, src, sres, gvec.ap[:, gcol0 + kc:gcol0 + kc + 1], rb.t[:, 0:TT], rb.res)

    def proj_fm(wb, wv, col0, rhs_chunks, kc_n):
        s_ = acc()
        for kc in range(kc_n):
            s_.mm(s_.t[:, 0:TT], wv[:, kc, col0:col0 + 128], rhs_chunks[kc].ap, (wb, rhs_chunks[kc]))
        return s_

    def w_h(kc, src, sres, gcol, rbc, rbres):
        dve_stt(hT[kc].ap, src, gcol, rbc, ALU.mult, ALU.mult, (sres, gvec, rbres), (hT[kc],))

    for t in range(ntiles):
        tok0 = t * TT
        par = t % 2
        j2, s2 = t // 4, t % 4

        if LIM < 1:
            continue
        if t == 0:
            rms_norm(None, xT, tok0, 0, w_h)
        if t == dbg.get("_tile", 0):
            for kc in range(8):
                if "hT" in dbg_out:
                    S.op("sp", lambda e, kc=kc: e.dma_start(out=dbg_out["hT"][kc * 128:(kc + 1) * 128, :], in_=hT[kc].ap),
                         reads=(hT[kc],), dma="dbg_hT")
        if LIM < 2:
            continue
        for j in range(3):
            wb, wv = next_unit()
            for oc in range(2):
                c = 2 * j + oc
                s_ = proj_fm(wb, wv, oc * 128, hT, 8)
                act(ya[c].ap, s_.t[:, 0:TT], AF.Gelu, (s_.res,), (ya[c],))

        if LIM < 3:
            continue
        for j in range(3):
            wb, wv = next_unit()
            for b in range(4):
                s_ = acc()
                for kc in range(8):
                    s_.mm(s_.t[:, 0:256], hT[kc].ap[:, b * 128:(b + 1) * 128], wv[:, kc, :], (hT[kc], wb))
                act(vg[b].ap[:, j * 256:(j + 1) * 256], s_.t[:, 0:256], AF.Gelu, (s_.res,), (vg[b],))
        for b in range(4):
            S.op("act", lambda e, b=b: e.activation(junk.ap, vg[b].ap, AF.Square, accum_out=ssv.ap[:, b:b + 1]),
                 reads=(vg[b],), writes=(junk, ssv))
        dve_ts(ss_sb.ap, ssv.ap, 1.0 / 768, EPS, ALU.mult, ALU.add, (ssv,), (ss_sb,))
        act(sd_sb.ap, ss_sb.ap, AF.Sqrt, (ss_sb,), (sd_sb,))
        S.op("dve", lambda e: e.reciprocal(rstd_v.ap, sd_sb.ap), reads=(sd_sb,), writes=(rstd_v,))
        for b in range(4):
            dve_ts(wp[b].ap, wsp_f.ap, rstd_v.ap[:, b:b + 1], None, ALU.mult, None, (wsp_f, rstd_v), (wp[b],))

        if LIM < 4.2:
            continue
        for j in range(6):
            wb, wv = next_unit()
            for b in range(4):
                s_ = acc()
                for kc in range(8):
                    s_.mm(s_.t[:, 0:256], hT[kc].ap[:, b * 128:(b + 1) * 128], wv[:, kc, :], (hT[kc], wb))
                stg = stgs[(4 * j + b) % 2]
                copy("act", stg.ap, s_.t[:, 0:256], (s_.res,), (stg,))
                copy("pool", qk_tm[b].ap[:, j * 256:(j + 1) * 256], stg.ap, (stg,), (qk_tm[b],))
                copy("dve", rot[b].ap[:, 4 * j:4 * j + 4, :],
                     stg.ap.rearrange("p (h d) -> p h d", h=4)[:, :, 0:16], (stg,), (rot[b],))
        if LIM < 4.5:
            continue
        for b in range(4):
            B = 4 * t + b
            Cb = cos_t.ap[:, B, :].unsqueeze(1).broadcast_to([128, 24, 8])
            Sb = sin_t.ap[:, B, :].unsqueeze(1).broadcast_to([128, 24, 8])
            t1v = rot[b].ap[:, :, 0:8]
            t2v = rot[b].ap[:, :, 8:16]
            qv = qk_tm[b].ap.rearrange("p (h d) -> p h d", h=24)
            dve_tt(ropeA.ap, t1v, Cb, ALU.mult, (rot[b], cos_t), (ropeA,), eng="pool")
            dve_tt(ropeB.ap, t2v, Sb, ALU.mult, (rot[b], sin_t), (ropeB,), eng="pool")
            dve_tt(qv[:, :, 0:8], ropeA.ap, ropeB.ap, ALU.subtract, (ropeA, ropeB), (qk_tm[b],), eng="pool")
            dve_tt(ropeA.ap, t1v, Sb, ALU.mult, (rot[b], sin_t), (ropeA,), eng="pool")
            dve_tt(ropeB.ap, t2v, Cb, ALU.mult, (rot[b], cos_t), (ropeB,), eng="pool")
            dve_tt(qv[:, :, 8:16], ropeA.ap, ropeB.ap, ALU.add, (ropeA, ropeB), (qk_tm[b],), eng="pool")
        if LIM < 4.8:
            continue
        for c in range(12):
            s_ = acc()
            tb = s_.t[:, 0:256].bitcast(BF16)
            for b in range(4):
                s_.tr(tb[:, b * 128:(b + 1) * 128], qk_tm[b].ap[:, c * 128:(c + 1) * 128], ident_b.ap, (qk_tm[b], ident_b))
            if c < 6:
                copy("act", qT[c].ap, tb, (s_.res,), (qT[c],))
            else:
                g, cc = (c - 6) // 2, (c - 6) % 2
                if g == 0:
                    copy("act", K0[par][cc].ap, tb, (s_.res,), (K0[par][cc],))
                elif g == 1:
                    copy("act", K1[par][cc].ap, tb, (s_.res,), (K1[par][cc],))
                else:
                    copy("act", K2[cc].ap[:, tok0:tok0 + TT], tb, (s_.res,), (K2[cc],))
        if t == dbg.get("_tile", 0) and "qT" in dbg_out:
            for c in range(6):
                S.op("sp", lambda e, c=c: e.dma_start(out=dbg_out["qT"][c * 128:(c + 1) * 128, :], in_=qT[c].ap),
                     reads=(qT[c],), dma="dbg_qT")

        if LIM < 4:
            continue
        for c in range(6):
            s_ = acc()
            f0 = 128 * c
            pieces = []
            f = f0
            while f < f0 + 128:
                g = f // 192
                fe = min(f0 + 128, (g + 1) * 192)
                pieces.append((f, fe, g))
                f = fe
            for b in range(4):
                for (fa, fe, g) in pieces:
                    r0 = fa - f0
                    s_.mm(s_.t[r0:r0 + (fe - fa), b * 128:(b + 1) * 128], vg[b].ap[:, fa:fe], wp[b].ap[:, g, :],
                          (vg[b], wp[b]), row0=r0, rows=fe - fa)
            dve_stt(rden.ap.rearrange("p (b t) -> p b t", b=4), s_.t[:, 0:TT].rearrange("p (b t) -> p b t", b=4),
                    gm.ap[:, c:c + 1], bbc.ap[:, c, :].unsqueeze(1).broadcast_to([128, 4, 128]),
                    ALU.mult, ALU.add, (s_.res, gm, bbc), (rden,))
            dve_tt(ya[c].ap, rden.ap, ya[c].ap, ALU.mult, (rden, ya[c]), (ya[c],), eng="pool")
        if t == dbg.get("_tile", 0) and "ya" in dbg_out:
            for c in range(6):
                S.op("sp", lambda e, c=c: e.dma_start(out=dbg_out["ya"][c * 128:(c + 1) * 128, :], in_=ya[c].ap),
                     reads=(ya[c],), dma="dbg_ya")

        if LIM < 6:
            continue
        for j in range(3):
            wb, wv = next_unit()
            for oc in range(2):
                c = 2 * j + oc
                s_ = proj_fm(wb, wv, oc * 128, hT, 8)
                copy("act", vT[c].ap, s_.t[:, 0:TT], (s_.res,), (vT[c],))
        s_ = acc()
        tb = s_.t[:, 0:512].bitcast(BF16)
        for b in range(4):
            for cc in range(2):
                s_.tr(tb[:, (b * 2 + cc) * 128:(b * 2 + cc + 1) * 128], vT[cc].ap[:, b * 128:(b + 1) * 128], ident_b.ap,
                      (vT[cc], ident_b))
        copy("act", V0[par].flat, tb, (s_.res,), (V0[par],))
        s_ = acc()
        tb = s_.t[:, 0:512].bitcast(BF16)
        for r in range(4):
            for cc in range(2):
                s_.tr(tb[:, (r * 2 + cc) * 128:(r * 2 + cc + 1) * 128], vT[2 + cc].ap[:, r::4], ident_b.ap,
                      (vT[2 + cc], ident_b))
        copy("act", V1[par].flat, tb, (s_.res,), (V1[par],))
        p0 = 32 * s2
        for rq in range(4):
            s_ = acc()
            tb = s_.t[:, 0:512].bitcast(BF16)
            for rr in range(4):
                r = 4 * rq + rr
                for cc in range(2):
                    s_.tr(tb[p0:p0 + 32, (rr * 2 + cc) * 128:(rr * 2 + cc + 1) * 128], vT[4 + cc].ap[:, r::16], ident_b.ap,
                          (vT[4 + cc], ident_b), tp=(0, p0))
            copy("act", V2[j2 % 2].flat[p0:p0 + 32, rq * 1024:(rq + 1) * 1024], tb[p0:p0 + 32, :], (s_.res,), (V2[j2 % 2],))

        if LIM < 9:
            continue
        acc_ring[0] = [0, 1, 2, 3]
        for pair in range(2):
            NS = Sess(4 + pair)
            DS = Sess(6 + pair)
            for half in range(2):
                hg = 2 * pair + half
                hp = 64 * half
                nE = [0]

                def attend(tiles, ncol, mask_ap, dview, col_lo=0):
                    sS = acc()
                    for i, tl in enumerate(tiles):
                        if tl is None:
                            continue
                        (k_ap, kres, v_ap, vres, q_ap, qres, osel) = tl
                        sS.mm(sS.t[:, i * ncol:(i + 1) * ncol], k_ap, q_ap, (kres, qres))
                    E = Eb[nE[0] % 2]
                    nE[0] += 1
                    act(E.ap[:, col_lo:TT], sS.t[:, col_lo:TT], AF.Exp, (sS.res,), (E,), scale=0.125)
                    nt_ = TT // ncol
                    ev = E.ap.rearrange("p (a c) -> p a c", a=nt_)
                    a0 = col_lo // ncol
                    S.op("pool", lambda e: e.tensor_tensor(ev[:, a0:, :], ev[:, a0:, :],
                                                           mask_ap.unsqueeze(1).broadcast_to([128, nt_ - a0, ncol]), ALU.mult),
                         reads=(E, mcur_b, mprev_b), writes=(E,))
                    for i, tl in enumerate(tiles):
                        if tl is None:
                            continue
                        (k_ap, kres, v_ap, vres, q_ap, qres, osel) = tl
                        NS.mm(osel(NS.t[hp:hp + 64, :]), v_ap, E.ap[:, i * ncol:(i + 1) * ncol], (vres, E), row0=hp, rows=64)
                    dv_ = dview(DS.t[hp:hp + 64, :])
                    er_ = E.ap[:, col_lo:TT]
                    if len(dv_.shape) == 3:
                        er_ = er_.rearrange("p (r i) -> p r i", r=dv_.shape[1])
                    DS.mm(dv_, ones_b.ap[:, 0:64], er_, (ones_b, E), row0=hp, rows=64)

                c0 = half_c = pair
                qc = qT[0 + pair]
                cur, prv = [], []
                for b in range(4):
                    q_ap = qc.ap[hp:hp + 64, b * 128:(b + 1) * 128]
                    osel = (lambda a, b=b: a[:, b * 128:(b + 1) * 128])
                    cur.append((K0[par][pair].ap[hp:hp + 64, b * 128:(b + 1) * 128], K0[par][pair],
                                V0[par].ap[:, b, hg * 64:(hg + 1) * 64], V0[par], q_ap, qc, osel))
                    if b >= 1:
                        prv.append((K0[par][pair].ap[hp:hp + 64, (b - 1) * 128:b * 128], K0[par][pair],
                                    V0[par].ap[:, b - 1, hg * 64:(hg + 1) * 64], V0[par], q_ap, qc, osel))
                    elif t >= 1:
                        prv.append((K0[1 - par][pair].ap[hp:hp + 64, 384:512], K0[1 - par][pair],
                                    V0[1 - par].ap[:, 3, hg * 64:(hg + 1) * 64], V0[1 - par], q_ap, qc, osel))
                    else:
                        prv.append(None)
                attend(cur, 128, mcur_b.ap, lambda a: a[:, 0:TT])
                lo = 0 if t >= 1 else 128
                attend(prv, 128, mprev_b.ap, lambda a, lo=lo: a[:, lo:TT], col_lo=lo)
                qc = qT[2 + pair]
                cur, prv = [], []
                for r in range(4):
                    q_ap = qc.ap[hp:hp + 64, r::4]
                    osel = (lambda a, r=r: a[:, r::4])
                    cur.append((K1[par][pair].ap[hp:hp + 64, r::4], K1[par][pair],
                                V1[par].ap[:, r, hg * 64:(hg + 1) * 64], V1[par], q_ap, qc, osel))
                    prv.append((K1[1 - par][pair].ap[hp:hp + 64, r::4], K1[1 - par][pair],
                                V1[1 - par].ap[:, r, hg * 64:(hg + 1) * 64], V1[1 - par], q_ap, qc, osel))
                dv1 = lambda a: a.rearrange("p (i r) -> p r i", r=4)
                attend(cur, 128, mcur_b.ap, dv1)
                if t >= 1:
                    attend(prv, 128, mprev_b.ap, dv1)
                qc = qT[4 + pair]
                cur, prv = [], []
                for r in range(16):
                    q_ap = qc.ap[hp:hp + 64, r::16]
                    osel = (lambda a, r=r: a[:, r::16])
                    kc_ap = K2[pair].ap[hp:hp + 64, 2048 * j2 + r:2048 * (j2 + 1):16]
                    cur.append((kc_ap, K2[pair], V2[j2 % 2].ap[:, r, hg * 64:(hg + 1) * 64], V2[j2 % 2], q_ap, qc, osel))
                    if j2 >= 1:
                        kp_ap = K2[pair].ap[hp:hp + 64, 2048 * (j2 - 1) + r:2048 * j2:16]
                        prv.append((kp_ap, K2[pair], V2[(j2 - 1) % 2].ap[:, r, hg * 64:(hg + 1) * 64], V2[(j2 - 1) % 2],
                                    q_ap, qc, osel))
                dv2 = lambda a: a.rearrange("p (i r) -> p r i", r=16)
                attend(cur, 32, mcur_b.ap[:, p0:p0 + 32], dv2)
                if j2 >= 1:
                    attend(prv, 32, mprev_b.ap[:, p0:p0 + 32], dv2)
            S.op("dve", lambda e, DS=DS: e.reciprocal(rden.ap, DS.t[:, 0:TT]), reads=(DS.res,), writes=(rden,))
            dve_tt(yb[pair].ap, NS.t[:, 0:TT], rden.ap, ALU.mult, (NS.res, rden), (yb[pair],))
        if t == dbg.get("_tile", 0) and "yb" in dbg_out:
            for c in range(2):
                S.op("sp", lambda e, c=c: e.dma_start(out=dbg_out["yb"][c * 128:(c + 1) * 128, :], in_=yb[c].ap),
                     reads=(yb[c],), dma="dbg_yb")

        if LIM < 10:
            continue
        acc_ring[0] = [0, 1, 2, 3, 6, 7]
        for j in range(8):
            wb, wv = next_unit()
            for oc in range(2):
                c = 2 * j + oc
                s_ = proj_fm(wb, wv, oc * 128, hT, 8)
                act(tg[c].ap, s_.t[:, 0:TT], AF.Tanh, (s_.res,), (tg[c],), scale=0.5)

        if LIM < 11:
            continue
        for j in range(4):
            wba, wva = next_unit()
            wbb, wvb = next_unit()
            for oc in range(2):
                m = 2 * j + oc
                sa = proj_fm(wba, wva, oc * 128, ya, 6)
                sb = proj_fm(wbb, wvb, oc * 128, yb, 2)
                dve_stt(t1b.ap, tg[m].ap, 1.0, sa.t[:, 0:TT], ALU.add, ALU.mult, (tg[m], sa.res), (t1b,))
                dve_stt(t2b.ap, tg[8 + m].ap, 1.0, sb.t[:, 0:TT], ALU.add, ALU.mult, (tg[8 + m], sb.res), (t2b,))
                dve_tt(mrg[m].ap, t1b.ap, t2b.ap, ALU.add, (t1b, t2b), (mrg[m],), eng="pool")

        if LIM < 12:
            continue
        for j in range(4):
            wb, wv = next_unit()
            for oc in range(2):
                m = 2 * j + oc
                s_ = proj_fm(wb, wv, oc * 128, mrg, 8)
                slot = xr[m % 3]
                dma_in(slot, slot.ap, xT[m * 128:(m + 1) * 128, tok0:tok0 + TT], "xr%d" % (m % 3))
                dve_stt(x1[m].ap, s_.t[:, 0:TT], 0.5, slot.ap, ALU.mult, ALU.add, (s_.res, slot), (x1[m],))
        if t == dbg.get("_tile", 0) and "x1" in dbg_out:
            for c in range(8):
                S.op("sp", lambda e, c=c: e.dma_start(out=dbg_out["x1"][c * 128:(c + 1) * 128, :], in_=x1[c].ap),
                     reads=(x1[c],), dma="dbg_x1")

        if LIM < 13:
            continue
        rms_norm(x1, None, tok0, 8, w_h)

        if LIM < 14:
            continue
        for j in range(11):
            wba, wva = next_unit()
            wbv, wvv = next_unit()
            for oc in range(2):
                c = 2 * j + oc
                sa = proj_fm(wba, wva, oc * 128, hT, 8)
                sv = proj_fm(wbv, wvv, oc * 128, hT, 8)
                o = ob[c % 2]
                g_ = gel[c % 2]
                ab = abuf[c % 2]
                w0 = cw.ap[:, c, 0:1]
                w1 = cw.ap[:, c, 1:2]
                w2 = cw.ap[:, c, 2:3]
                copy("pool", ab.ap[:, 0:2], halo.ap[:, c, :], (halo,), (ab,))
                copy("act", ab.ap[:, 2:TT + 2], sa.t[:, 0:TT], (sa.res,), (ab,))
                S.op("act", lambda e, o=o, ab=ab, w2=w2, c=c: e.activation(o.ap, ab.ap[:, 2:TT + 2], AF.Identity,
                                                                    bias=cb.ap[:, c:c + 1], scale=w2),
                     reads=(ab, cw, cb), writes=(o,))
                copy("pool", halo.ap[:, c, :], ab.ap[:, TT:TT + 2], (ab,), (halo,))
                dve_stt(o.ap, ab.ap[:, 1:TT + 1], w1, o.ap, ALU.mult, ALU.add, (ab, cw, o), (o,))
                dve_stt(o.ap, ab.ap[:, 0:TT], w0, o.ap, ALU.mult, ALU.add, (ab, cw, o), (o,))
                act(g_.ap, o.ap, AF.Gelu, (o,), (g_,))
                dve_tt(gg[c].ap, g_.ap, sv.t[:, 0:TT], ALU.mult, (g_, sv.res), (gg[c],))

        if LIM < 15:
            continue
        for j in range(4):
            wb0, wv0 = next_unit()
            wb1, wv1 = next_unit()
            for oc in range(2):
                m = 2 * j + oc
                s_ = acc()
                for kc in range(NFF):
                    wbx, wvx = (wb0, wv0) if kc < 11 else (wb1, wv1)
                    s_.mm(s_.t[:, 0:TT], wvx[:, kc % 11, oc * 128:(oc + 1) * 128], gg[kc].ap, (wbx, gg[kc]))
                dve_tt(x1[m].ap, s_.t[:, 0:TT], x1[m].ap, ALU.add, (s_.res, x1[m]), (x1[m],))

        if LIM < 16:
            continue
        def w_o(kc, src, sres, gcol, rbc, rbres):
            dve_stt(x1[kc].ap, src, gcol, rbc, ALU.mult, ALU.mult, (sres, gvec, rbres), (x1[kc],))
            S.op("sp", lambda e, kc=kc, tok0=tok0: e.dma_start(out=outT[kc * 128:(kc + 1) * 128, tok0:tok0 + TT], in_=x1[kc].ap),
                 reads=(x1[kc],), dma="out%d" % kc)
        if t + 1 < ntiles:
            rms_norm(None, xT, tok0 + TT, 0, w_h)
        rms_norm(x1, None, tok0, 16, w_o)

    sem_ctx = []
    esem = {}
    for e in ("pe", "act", "dve", "pool"):
        c_ = nc.semaphore("sem_" + e)
        esem[e] = c_.__enter__()
        sem_ctx.append(c_)
    dsem = {}
    for name in S.dma_cum:
        c_ = nc.semaphore("dsem_" + name)
        dsem[name] = c_.__enter__()
        sem_ctx.append(c_)
    with nc.Block() as block:
        S.emit(nc, block, esem, dsem)
    for c_ in reversed(sem_ctx):
        c_.__exit__(None, None, None)
    for c_ in reversed(ctx):
        c_.__exit__(None, None, None)
    return nc


def make_shared(mix_norm_g, w_in, gmlp_norm_g, w_spatial, b_spatial, w_branch_a, w_branch_b, w_out,
                ffn_norm_g, w_up, conv_w, conv_b, w_down, final_norm_g):
    f32 = np.float32
    A = lambda a: np.ascontiguousarray(np.asarray(a, dtype=f32))

    def pk(v):
        v = np.asarray(v, dtype=f32)
        return v.reshape(-1, 128).T

    gvec = np.concatenate([pk(mix_norm_g[0]), pk(ffn_norm_g[0]), pk(final_norm_g)], axis=1)
    gm = pk(gmlp_norm_g[0])
    bsp = np.asarray(b_spatial[0], dtype=f32)
    grp = (np.arange(768) // 192).reshape(6, 128)
    bbc = bsp[grp]
    bbc = np.transpose(bbc, (1, 0, 2)).reshape(128, 6 * 128)
    wspT = np.transpose(np.asarray(w_spatial[0], dtype=f32), (2, 0, 1)).reshape(128, 4 * 128)
    cwv = np.asarray(conv_w[0], dtype=f32)
    cwl = np.transpose(cwv.reshape(3, NFF, 128), (2, 1, 0)).reshape(128, NFF * 3)
    cbl = pk(conv_b[0])
    ident = np.eye(128, dtype=f32)
    kk = np.arange(128)[:, None]
    qq = np.arange(128)[None, :]
    mcur = (kk <= qq).astype(f32)
    mprev = (kk >= qq).astype(f32)
    invf = (np.float32(500000.0) ** (-np.arange(0, 16, 2, dtype=f32) / np.float32(16))).astype(f32)
    consts = np.concatenate([ident, mcur, mprev, np.broadcast_to(invf[None, :], (128, 8))], axis=1)
    return {
        "w_in": A(w_in[0]), "w_pa": A(w_branch_a[0]), "w_pb": A(w_branch_b[0]), "w_out": A(w_out[0]),
        "w_up": A(w_up[0]), "w_down": A(w_down[0]),
        "gvec": A(gvec), "gm": A(gm), "bbc": A(bbc), "wspT": A(wspT), "convw": A(cwl), "convb": A(cbl),
        "consts": A(consts),
    }


_NC_CACHE = {}


def kernel(x, positions, mix_norm_g, w_in, gmlp_norm_g, w_spatial, b_spatial, w_branch_a, w_branch_b, w_out,
           ffn_norm_g, w_up, conv_w, conv_b, w_down, final_norm_g):
    x = np.asarray(x, dtype=np.float32)
    positions = np.asarray(positions)
    shared = make_shared(mix_norm_g, w_in, gmlp_norm_g, w_spatial, b_spatial, w_branch_a, w_branch_b, w_out,
                         ffn_norm_g, w_up, conv_w, conv_b, w_down, final_norm_g)
    n = x.shape[0]
    in_maps = []
    for b in range(n):
        m = dict(shared)
        m["xT"] = np.ascontiguousarray(x[b].T)
        m["pos"] = np.ascontiguousarray(positions[b].astype(np.int32).reshape(32, 128).T)
        in_maps.append(m)
    if "nc" not in _NC_CACHE:
        _NC_CACHE["nc"] = build_nc()
    res = run_bass_kernel_spmd(_NC_CACHE["nc"], in_maps, core_ids=list(range(n)))
    out = np.stack([np.ascontiguousarray(r["outT"].T) for r in res.results], axis=0)
    return out.astype(np.float32)
```

```python
import numpy as np
import concourse.bass as bass
import concourse.mybir as mybir
from concourse.bass_utils import run_bass_kernel_spmd

F32 = mybir.dt.float32
BF16 = mybir.dt.bfloat16
I32 = mybir.dt.int32
AF = mybir.ActivationFunctionType
ALU = mybir.AluOpType

D = 1024
SEQ = 4096
TT = 512
NT = SEQ // TT
INW = 5888
DFF = 2816
NFF = DFF // 128
EPS = 1e-6
KB = 1024


class Res:
    _n = 0

    def __init__(self, name, lo=None, hi=None):
        Res._n += 1
        self.rid = Res._n
        self.name = name
        self.lo, self.hi = lo, hi
        self.ov = [self]


class Sched:
    ENG = ("pe", "act", "dve", "pool", "sp")

    def __init__(self):
        self.ops = {e: [] for e in self.ENG}
        self.lastw = {}
        self.readers = {}
        self.dma_cum = {}

    def op(self, eng, fn, reads=(), writes=(), dma=None):
        deps = set()
        for b in reads:
            ev = self.lastw.get(b.rid)
            if ev is not None:
                deps.add(ev)
        for b in writes:
            for x in b.ov:
                ev = self.lastw.get(x.rid)
                if ev is not None:
                    deps.add(ev)
                for r in self.readers.get(x.rid, ()):
                    deps.add(r)
        if dma is not None:
            v = self.dma_cum.get(dma, 0) + 16
            self.dma_cum[dma] = v
            ev = ("dma:" + dma, v)
        else:
            ev = (eng, len(self.ops[eng]))
        self.ops[eng].append((fn, deps, dma))
        for b in reads:
            self.readers.setdefault(b.rid, []).append(ev)
        for b in writes:
            for x in b.ov:
                self.lastw[x.rid] = ev
                self.readers[x.rid] = []
        return ev

    def emit(self, nc, block, esem, dsem):
        comp = ("pe", "act", "dve", "pool")
        needed = {e: set() for e in comp}
        for e in self.ENG:
            for (_, deps, _) in self.ops[e]:
                for (k, v) in deps:
                    if k in needed and not (k == "pe" and e == "pe"):
                        needed[k].add(v)
        rank = {e: {idx: i + 1 for i, idx in enumerate(sorted(needed[e]))} for e in comp}
        ops = self.ops

        def run(ename, eng):
            known = {}
            for idx, (fn, deps, dma) in enumerate(ops[ename]):
                want = {}
                for (k, v) in deps:
                    if k == "pe" and ename == "pe":
                        continue
                    if k in rank:
                        val = rank[k][v]
                    else:
                        val = v
                    if val > want.get(k, 0):
                        want[k] = val
                for k, val in want.items():
                    if known.get(k, 0) >= val:
                        continue
                    known[k] = val
                    sem = esem[k] if k in esem else dsem[k[4:]]
                    eng.wait_ge(sem, val)
                ins = fn(eng)
                if dma is not None:
                    ins.then_inc(dsem[dma], 16)
                elif ename in rank and idx in rank[ename]:
                    ins.then_inc(esem[ename], 1)

        @block.tensor
        def _(eng):
            run("pe", eng)

        @block.scalar
        def _(eng):
            run("act", eng)

        @block.vector
        def _(eng):
            run("dve", eng)

        @block.gpsimd
        def _(eng):
            run("pool", eng)

        @block.sync
        def _(eng):
            run("sp", eng)
            for name, v in self.dma_cum.items():
                eng.wait_ge(dsem[name], v)


def build_nc(ntiles=NT, dbg=None):
    nc = bass.Bass("TRN2", target_bir_lowering=False)
    S = Sched()
    dbg = dbg or {}
    LIM = dbg.get("_lim", 99)

    def din(name, shape, dt=F32):
        return nc.dram_tensor(name, list(shape), dt, kind="ExternalInput").ap()

    xT = din("xT", [D, SEQ])
    pos_d = din("pos", [128, 32], I32)
    w_in = din("w_in", [D, INW])
    w_pa = din("w_pa", [768, D])
    w_pb = din("w_pb", [256, D])
    w_out = din("w_out", [D, D])
    w_up = din("w_up", [D, 2 * DFF])
    w_down = din("w_down", [DFF, D])
    gvec_d = din("gvec", [128, 24])
    gm_d = din("gm", [128, 6])
    bbc_d = din("bbc", [128, 6 * 128])
    wsp_d = din("wspT", [128, 4 * 128])
    cw_d = din("convw", [128, NFF * 3])
    cb_d = din("convb", [128, NFF])
    cst_d = din("consts", [128, 3 * 128 + 8])
    outT = nc.dram_tensor("outT", [D, SEQ], F32, kind="ExternalOutput").ap()
    dbg_out = {}
    for name, (shape, dt) in dbg.items():
        dbg_out[name] = nc.dram_tensor("dbg_" + name, list(shape), dt, kind="ExternalOutput").ap()

    ARENA_BYTES = 207 * KB
    ctx = []
    arena_t = nc.sbuf_tensor("arena", [128, ARENA_BYTES // 4], F32)
    arena = arena_t.__enter__()
    ctx.append(arena_t)
    banks = []
    for i in range(8):
        t = nc.psum_tensor("ps%d" % i, [128, 512], F32)
        banks.append(t.__enter__())
        ctx.append(t)
    bankres = [Res("ps%d" % i) for i in range(8)]

    allocs = []
    top = [0]

    class Buf(Res):
        def __init__(self, name, shape, dt, at=None):
            esz = 4 if dt in (F32, I32) else 2
            n = 1
            for s_ in shape:
                n *= s_
            nbytes = n * esz
            if at is None:
                off = (top[0] + 63) // 64 * 64
                top[0] = off + nbytes
            else:
                off = at
            assert off % 4 == 0 and off + nbytes <= ARENA_BYTES, (name, off, nbytes)
            Res.__init__(self, name, off, off + nbytes)
            for o in allocs:
                if o.lo < self.hi and self.lo < o.hi:
                    o.ov.append(self)
                    self.ov.append(o)
            allocs.append(self)
            v = arena[:, off // 4:(off + nbytes) // 4]
            if dt != F32:
                v = v.bitcast(dt)
            self.flat = v
            if len(shape) == 1:
                self.ap = v
            elif len(shape) == 2:
                self.ap = v.rearrange("p (a b) -> p a b", a=shape[0])
            elif len(shape) == 3:
                self.ap = v.rearrange("p (a b c) -> p a b c", a=shape[0], b=shape[1])
            else:
                raise ValueError(shape)
            self.shape = shape
            self.dt = dt

    cst = Buf("cst", [3 * 128 + 8], F32)
    ident_f = cst.ap[:, 0:128]
    mcur_f = cst.ap[:, 128:256]
    mprev_f = cst.ap[:, 256:384]
    invf = cst.ap[:, 384:392]
    ident_b = Buf("ident_b", [128], BF16)
    mcur_b = Buf("mcur_b", [128], BF16)
    mprev_b = Buf("mprev_b", [128], BF16)
    ones_f = Buf("ones_f", [128], F32)
    ones_b = Buf("ones_b", [128], BF16)
    gvec = Buf("gvec", [24], F32)
    gm = Buf("gm", [6], F32)
    bbc = Buf("bbc", [6, 128], F32)
    wsp_f = Buf("wsp_f", [4, 128], F32)
    wsp_b = Buf("wsp_b", [4, 128], BF16)
    cw = Buf("cw", [NFF, 3], F32)
    cb = Buf("cb", [NFF], F32)
    pos_i = Buf("pos_i", [32], I32)
    pos_f = Buf("pos_f", [32], F32)
    cos_t = Buf("cos_t", [32, 8], F32)
    sin_t = Buf("sin_t", [32, 8], F32)
    halo = Buf("halo", [NFF, 2], F32)
    ss_sb = Buf("ss_sb", [4], F32)
    sd_sb = Buf("sd_sb", [4], F32)
    rstd = Buf("rstd", [4], F32)
    rstd_v = Buf("rstd_v", [4], F32)
    ssv = Buf("ssv", [4], F32)
    diag = Buf("diag", [4, 128], F32)
    junk = Buf("junk", [768], BF16)
    ropeA = Buf("ropeA", [24, 8], F32)
    ropeB = Buf("ropeB", [24, 8], F32)
    xr = [Buf("xr%d" % i, [TT], F32) for i in range(3)]
    sq = [Buf("sq%d" % i, [TT], BF16) for i in range(2)]
    hT = [Buf("hT%d" % i, [TT], BF16) for i in range(8)]
    x1 = [Buf("x1_%d" % i, [TT], F32) for i in range(8)]
    K0 = [[Buf("K0_%d_%d" % (p, c), [TT], BF16) for c in range(2)] for p in range(2)]
    K1 = [[Buf("K1_%d_%d" % (p, c), [TT], BF16) for c in range(2)] for p in range(2)]
    K2 = [Buf("K2_%d" % c, [SEQ], BF16) for c in range(2)]
    V0 = [Buf("V0_%d" % p, [4, 256], BF16) for p in range(2)]
    V1 = [Buf("V1_%d" % p, [4, 256], BF16) for p in range(2)]
    V2 = [Buf("V2_%d" % p, [16, 256], BF16) for p in range(2)]
    USZ = 11 * 256
    wst = [Buf("wst%d" % i, [USZ], F32) for i in range(2)]
    wbf = [Buf("wbf%d" % i, [USZ], BF16) for i in range(6)]
    P0 = (top[0] + 63) // 64 * 64

    def PB(name, shape, dt, kb_off):
        return Buf(name, shape, dt, at=P0 + int(kb_off * KB))

    ya = [PB("ya%d" % c, [TT], BF16, 0 + c) for c in range(6)]
    vg = [PB("vg%d" % b, [768], BF16, 6 + 1.5 * b) for b in range(4)]
    wp = [PB("wp%d" % b, [4, 128], BF16, 12 + b) for b in range(4)]
    qk_tm = [PB("qk%d" % b, [1536], BF16, 16 + 3 * b) for b in range(4)]
    rot = [PB("rot%d" % b, [24, 16], F32, 28 + 1.5 * b) for b in range(4)]
    qT = [PB("qT%d" % c, [TT], BF16, 34 + c) for c in range(6)]
    vT = [PB("vT%d" % c, [TT], BF16, 40 + c) for c in range(6)]
    Eb = [PB("E%d" % i, [TT], BF16, 46 + i) for i in range(2)]
    rden = PB("rden", [TT], F32, 48)
    stgs = [PB("stg%d" % i, [256], F32, 48 + i) for i in range(2)]
    yb = [PB("yb%d" % i, [TT], BF16, 50 + i) for i in range(2)]
    tg = [PB("tg%d" % c, [TT], BF16, 16 + c) for c in range(16)]
    t1b = PB("t1b", [TT], F32, 6)
    t2b = PB("t2b", [TT], F32, 8)
    mrg = [PB("mrg%d" % c, [TT], BF16, 40 + c) for c in range(8)]
    gg = [PB("gg%d" % c, [TT], BF16, 0 + c) for c in range(NFF)]
    ob = [PB("ob%d" % i, [TT], F32, 22 + 2 * i) for i in range(2)]
    gel = [PB("gel%d" % i, [TT], F32, 26 + 2 * i) for i in range(2)]
    abuf = [PB("abuf%d" % i, [TT + 2], F32, 30 + 2.25 * i) for i in range(2)]
    assert P0 + 52 * KB <= ARENA_BYTES, P0

    def dma_in(dst_buf, dst_ap, src_ap, sem, extra_w=()):
        S.op("sp", lambda e: e.dma_start(out=dst_ap, in_=src_ap), reads=(), writes=(dst_buf,) + tuple(extra_w), dma=sem)

    def act(out, in_, func, reads, writes, **kw):
        return S.op("act", lambda e: e.activation(out, in_, func, **kw), reads=reads, writes=writes)

    def dve_tt(out, in0, in1, op, reads, writes, eng="dve"):
        return S.op(eng, lambda e: e.tensor_tensor(out, in0, in1, op), reads=reads, writes=writes)

    def dve_ts(out, in0, s1, s2, op0, op1, reads, writes, eng="dve"):
        if op1 is None:
            return S.op(eng, lambda e: e.tensor_scalar(out, in0, s1, None, op0), reads=reads, writes=writes)
        return S.op(eng, lambda e: e.tensor_scalar(out, in0, s1, s2, op0, op1), reads=reads, writes=writes)

    def dve_stt(out, in0, scalar, in1, op0, op1, reads, writes):
        return S.op("dve", lambda e: e.scalar_tensor_tensor(out, in0, scalar, in1, op0, op1), reads=reads, writes=writes)

    def copy(eng, out, in_, reads, writes):
        if eng == "act":
            return S.op("act", lambda e: e.activation(out, in_, AF.Copy), reads=reads, writes=writes)
        return S.op(eng, lambda e: e.tensor_copy(out, in_), reads=reads, writes=writes)

    class Sess:
        def __init__(self, bi):
            self.bi = bi
            self.res = bankres[bi]
            self.t = banks[bi]
            self.started = set()

        def mm(self, out, lhsT, rhs, reads, row0=0, rows=128, tp=None):
            qs = set(range(row0 // 32, (row0 + rows + 31) // 32))
            first = not (qs & self.started)
            assert first or qs <= self.started
            self.started |= qs
            kw = {}
            if tp is not None:
                kw["tile_position"] = tp
            S.op("pe", lambda e: e.matmul(out, lhsT, rhs, start=first, stop=True, skip_group_check=True, **kw),
                 reads=reads, writes=(self.res,))

        def tr(self, out, in_, ident, reads, tp=None):
            kw = {}
            if tp is not None:
                kw["tile_position"] = tp
            S.op("pe", lambda e: e.transpose(out, in_, ident, **kw), reads=reads, writes=(self.res,))

    accn = [0]

    acc_ring = [[0, 1, 2, 3, 6, 7]]

    def acc():
        r_ = acc_ring[0]
        b = r_[accn[0] % len(r_)]
        accn[0] += 1
        return Sess(b)

    units = []

    def add_unit(w, r0, kc, c0):
        units.append((w[r0:r0 + kc * 128, c0:c0 + 256], kc))

    for t in range(ntiles):
        for j in range(23):
            add_unit(w_in, 0, 8, 256 * j)
        for j in range(4):
            add_unit(w_pa, 0, 6, 256 * j)
            add_unit(w_pb, 0, 2, 256 * j)
        for j in range(4):
            add_unit(w_out, 0, 8, 256 * j)
        for j in range(11):
            add_unit(w_up, 0, 8, 256 * j)
            add_unit(w_up, 0, 8, DFF + 256 * j)
        for j in range(4):
            add_unit(w_down, 0, 11, 256 * j)
            add_unit(w_down, 11 * 128, 11, 256 * j)
    uloaded = [0]
    uused = [0]
    PREF = 4
    UPT = len(units) // ntiles
    wscr = nc.dram_tensor("wscr", [UPT, 128, USZ], BF16, kind="Internal").ap()
    scr_res = [Res("wscr%d" % u) for u in range(UPT)]

    def _writeback(i):
        src, kc = units[i]
        wb = wbf[i % 6]
        u = i % UPT
        S.op("sp", lambda e: e.dma_start(out=wscr[u, :, 0:kc * 256], in_=wb.ap[:, 0:kc * 256]),
             reads=(wb,), writes=(scr_res[u],), dma="wsb%d" % (i % 6))

    def _load(i):
        src, kc = units[i]
        wb = wbf[i % 6]
        u = i % UPT
        if i < UPT:
            st = wst[i % 2]
            dst = st.ap[:, 0:kc * 256].rearrange("p (k c) -> p k c", k=kc)
            S.op("sp", lambda e: e.dma_start(out=dst, in_=src.rearrange("(k p) c -> p k c", p=128)),
                 writes=(st,), dma="wst%d" % (i % 2))
            if ntiles > 1 and i >= 1:
                _writeback(i - 1)
            if i % 3 == 0:
                S.op("pool", lambda e: e.tensor_copy(wb.ap[:, 0:kc * 256], st.ap[:, 0:kc * 256]), reads=(st,), writes=(wb,))
            elif i % 3 == 1:
                S.op("dve", lambda e: e.tensor_copy(wb.ap[:, 0:kc * 256], st.ap[:, 0:kc * 256]), reads=(st,), writes=(wb,))
            else:
                S.op("act", lambda e: e.activation(wb.ap[:, 0:kc * 256], st.ap[:, 0:kc * 256], AF.Copy), reads=(st,), writes=(wb,))
        else:
            if i == UPT:
                _writeback(UPT - 1)
            S.op("sp", lambda e: e.dma_start(out=wb.ap[:, 0:kc * 256], in_=wscr[u, :, 0:kc * 256]),
                 reads=(scr_res[u],), writes=(wb,), dma="wld%d" % (i % 6))

    def next_unit():
        i = uused[0]
        uused[0] += 1
        while uloaded[0] < min(len(units), i + 1 + PREF):
            _load(uloaded[0])
            uloaded[0] += 1
        kc = units[i][1]
        wb = wbf[i % 6]
        return wb, wb.ap[:, 0:kc * 256].rearrange("p (k c) -> p k c", k=kc)

    dma_in(cst, cst.ap, cst_d, "c_cst")
    dma_in(gvec, gvec.ap, gvec_d, "c_gvec")
    dma_in(gm, gm.ap, gm_d, "c_gm")
    dma_in(bbc, bbc.flat, bbc_d, "c_bbc")
    dma_in(wsp_f, wsp_f.flat, wsp_d, "c_wsp")
    dma_in(cw, cw.flat, cw_d, "c_cw")
    dma_in(cb, cb.ap, cb_d, "c_cb")
    dma_in(pos_i, pos_i.ap, pos_d, "c_pos")
    copy("dve", ident_b.ap, ident_f, (cst,), (ident_b,))
    copy("dve", mcur_b.ap, mcur_f, (cst,), (mcur_b,))
    copy("dve", mprev_b.ap, mprev_f, (cst,), (mprev_b,))
    S.op("pool", lambda e: e.memset(ones_f.ap, 1.0), writes=(ones_f,))
    S.op("pool", lambda e: e.memset(ones_b.ap, 1.0), writes=(ones_b,))
    S.op("pool", lambda e: e.memset(halo.flat, 0.0), writes=(halo,))
    for c in range(2):
        S.op("pool", lambda e, c=c: e.memset(K2[c].ap, 0.0), writes=(K2[c],))
    for p in range(2):
        S.op("pool", lambda e, p=p: e.memset(V2[p].flat, 0.0), writes=(V2[p],))
    dve_tt(wsp_f.ap, wsp_f.ap, mcur_f.unsqueeze(1).broadcast_to([128, 4, 128]), ALU.mult, (wsp_f, cst), (wsp_f,))
    copy("dve", pos_f.ap, pos_i.ap, (pos_i,), (pos_f,))
    ang = Buf("ang", [32, 8], F32, at=P0)
    kq = Buf("kq", [32, 8], F32, at=P0 + 2 * KB)
    ki = Buf("ki", [32, 8], I32, at=P0 + 4 * KB)
    red = Buf("red", [32, 8], F32, at=P0 + 6 * KB)
    dve_tt(ang.ap, pos_f.ap.unsqueeze(2).broadcast_to([128, 32, 8]), invf.unsqueeze(1).broadcast_to([128, 32, 8]),
           ALU.mult, (pos_f, cst), (ang,))
    TWO_PI = 2.0 * np.pi
    C1 = float(np.float32(6.28125))
    C2 = float(np.float32(TWO_PI - 6.28125))
    C3 = float(TWO_PI - 6.28125 - float(np.float32(TWO_PI - 6.28125)))
    msk = Buf("msk", [32, 8], F32, at=P0 + 8 * KB)
    PI = float(np.pi)
    for (tab, shift) in ((sin_t, 0.0), (cos_t, float(np.pi / 2))):
        dve_ts(kq.ap, ang.ap, float(1.0 / TWO_PI), None, ALU.mult, None, (ang,), (kq,))
        copy("dve", ki.ap, kq.ap, (kq,), (ki,))
        copy("dve", kq.ap, ki.ap, (ki,), (kq,))
        dve_stt(red.ap, kq.ap, -C1, ang.ap, ALU.mult, ALU.add, (kq, ang), (red,))
        dve_stt(red.ap, kq.ap, -C2, red.ap, ALU.mult, ALU.add, (kq, red), (red,))
        dve_stt(red.ap, kq.ap, -C3, red.ap, ALU.mult, ALU.add, (kq, red), (red,))
        if shift != 0.0:
            dve_ts(red.ap, red.ap, shift, None, ALU.add, None, (red,), (red,))
        for _ in range(2):
            dve_ts(msk.ap, red.ap, PI, -TWO_PI, ALU.is_gt, ALU.mult, (red,), (msk,))
            dve_tt(red.ap, red.ap, msk.ap, ALU.add, (red, msk), (red,))
            dve_ts(msk.ap, red.ap, -PI, TWO_PI, ALU.is_lt, ALU.mult, (red,), (msk,))
            dve_tt(red.ap, red.ap, msk.ap, ALU.add, (red, msk), (red,))
        act(tab.ap, red.ap, AF.Sin, (red,), (tab,))

    def dump(name, buf, ap):
        if name in dbg_out:
            S.op("sp", lambda e: e.dma_start(out=dbg_out[name], in_=ap), reads=(buf,), dma="dbg_" + name)

    dump("cos", cos_t, cos_t.flat)
    dump("sin", sin_t, sin_t.flat)

    def rms_norm(src_chunks, load_from, tok0, gcol0, write_fn):
        ssS = Sess(4)
        for kc in range(8):
            if load_from is not None:
                slot = xr[kc % 3]
                dma_in(slot, slot.ap, load_from[kc * 128:(kc + 1) * 128, tok0:tok0 + TT], "xr%d" % (kc % 3))
                src, sres = slot.ap, slot
            else:
                src, sres = src_chunks[kc].ap, src_chunks[kc]
            sqb = sq[kc % 2]
            act(sqb.ap, src, AF.Square, (sres,), (sqb,))
            for b in range(4):
                ssS.mm(ssS.t[:, b:b + 1], sqb.ap[:, b * 128:(b + 1) * 128], ones_b.ap[:, 0:1], (sqb, ones_b))
        dve_ts(ss_sb.ap, ssS.t[:, 0:4], 1.0 / D, EPS, ALU.mult, ALU.add, (ssS.res,), (ss_sb,))
        act(sd_sb.ap, ss_sb.ap, AF.Sqrt, (ss_sb,), (sd_sb,))
        S.op("dve", lambda e: e.reciprocal(rstd.ap, sd_sb.ap), reads=(sd_sb,), writes=(rstd,))
        dve_tt(diag.ap, ident_f.unsqueeze(1).broadcast_to([128, 4, 128]),
               rstd.ap.unsqueeze(2).broadcast_to([128, 4, 128]), ALU.mult, (cst, rstd), (diag,))
        rb = Sess(5)
        rb.mm(rb.t[:, 0:TT], ones_f.ap, diag.flat, (ones_f, diag))
        for kc in range(8):
            if load_from is not None:
                slot = xr[(kc + 2) % 3]
                dma_in(slot, slot.ap, load_from[kc * 128:(kc + 1) * 128, tok0:tok0 + TT], "xr%d" % ((kc + 2) % 3))
                src, sres = slot.ap, slot
            else:
                src, sres = src_chunks[kc].ap, src_chunks[kc]
            write_fn(kc, src, sres, gvec.ap[:, gcol0 + kc:gcol0 + kc + 1], rb.t[:, 0:TT], rb.res)

    def proj_fm(wb, wv, col0, rhs_chunks, kc_n):
        s_ = acc()
        for kc in range(kc_n):
            s_.mm(s_.t[:, 0:TT], wv[:, kc, col0:col0 + 128], rhs_chunks[kc].ap, (wb, rhs_chunks[kc]))
        return s_

    def w_h(kc, src, sres, gcol, rbc, rbres):
        dve_stt(hT[kc].ap, src, gcol, rbc, ALU.mult, ALU.mult, (sres, gvec, rbres), (hT[kc],))

    for t in range(ntiles):
        tok0 = t * TT
        par = t % 2
        j2, s2 = t // 4, t % 4

        if LIM < 1:
            continue
        if t == 0:
            rms_norm(None, xT, tok0, 0, w_h)
        if t == dbg.get("_tile", 0):
            for kc in range(8):
                if "hT" in dbg_out:
                    S.op("sp", lambda e, kc=kc: e.dma_start(out=dbg_out["hT"][kc * 128:(kc + 1) * 128, :], in_=hT[kc].ap),
                         reads=(hT[kc],), dma="dbg_hT")
        if LIM < 2:
            continue
        for j in range(3):
            wb, wv = next_unit()
            for oc in range(2):
                c = 2 * j + oc
                s_ = proj_fm(wb, wv, oc * 128, hT, 8)
                act(ya[c].ap, s_.t[:, 0:TT], AF.Gelu, (s_.res,), (ya[c],))

        if LIM < 3:
            continue
        for j in range(3):
            wb, wv = next_unit()
            for b in range(4):
                s_ = acc()
                for kc in range(8):
                    s_.mm(s_.t[:, 0:256], hT[kc].ap[:, b * 128:(b + 1) * 128], wv[:, kc, :], (hT[kc], wb))
                act(vg[b].ap[:, j * 256:(j + 1) * 256], s_.t[:, 0:256], AF.Gelu, (s_.res,), (vg[b],))
        for b in range(4):
            S.op("act", lambda e, b=b: e.activation(junk.ap, vg[b].ap, AF.Square, accum_out=ssv.ap[:, b:b + 1]),
                 reads=(vg[b],), writes=(junk, ssv))
        dve_ts(ss_sb.ap, ssv.ap, 1.0 / 768, EPS, ALU.mult, ALU.add, (ssv,), (ss_sb,))
        act(sd_sb.ap, ss_sb.ap, AF.Sqrt, (ss_sb,), (sd_sb,))
        S.op("dve", lambda e: e.reciprocal(rstd_v.ap, sd_sb.ap), reads=(sd_sb,), writes=(rstd_v,))
        for b in range(4):
            dve_ts(wp[b].ap, wsp_f.ap, rstd_v.ap[:, b:b + 1], None, ALU.mult, None, (wsp_f, rstd_v), (wp[b],))

        if LIM < 4.2:
            continue
        for j in range(6):
            wb, wv = next_unit()
            for b in range(4):
                s_ = acc()
                for kc in range(8):
                    s_.mm(s_.t[:, 0:256], hT[kc].ap[:, b * 128:(b + 1) * 128], wv[:, kc, :], (hT[kc], wb))
                stg = stgs[(4 * j + b) % 2]
                copy("act", stg.ap, s_.t[:, 0:256], (s_.res,), (stg,))
                copy("pool", qk_tm[b].ap[:, j * 256:(j + 1) * 256], stg.ap, (stg,), (qk_tm[b],))
                copy("dve", rot[b].ap[:, 4 * j:4 * j + 4, :],
                     stg.ap.rearrange("p (h d) -> p h d", h=4)[:, :, 0:16], (stg,), (rot[b],))
        if LIM < 4.5:
            continue
        for b in range(4):
            B = 4 * t + b
            Cb = cos_t.ap[:, B, :].unsqueeze(1).broadcast_to([128, 24, 8])
            Sb = sin_t.ap[:, B, :].unsqueeze(1).broadcast_to([128, 24, 8])
            t1v = rot[b].ap[:, :, 0:8]
            t2v = rot[b].ap[:, :, 8:16]
            qv = qk_tm[b].ap.rearrange("p (h d) -> p h d", h=24)
            dve_tt(ropeA.ap, t1v, Cb, ALU.mult, (rot[b], cos_t), (ropeA,), eng="pool")
            dve_tt(ropeB.ap, t2v, Sb, ALU.mult, (rot[b], sin_t), (ropeB,), eng="pool")
            dve_tt(qv[:, :, 0:8], ropeA.ap, ropeB.ap, ALU.subtract, (ropeA, ropeB), (qk_tm[b],), eng="pool")
            dve_tt(ropeA.ap, t1v, Sb, ALU.mult, (rot[b], sin_t), (ropeA,), eng="pool")
            dve_tt(ropeB.ap, t2v, Cb, ALU.mult, (rot[b], cos_t), (ropeB,), eng="pool")
            dve_tt(qv[:, :, 8:16], ropeA.ap, ropeB.ap, ALU.add, (ropeA, ropeB), (qk_tm[b],), eng="pool")
        if LIM < 4.8:
            continue
        for c in range(12):
            s_ = acc()
            tb = s_.t[:, 0:256].bitcast(BF16)
            for b in range(4):
                s_.tr(tb[:, b * 128:(b + 1) * 128], qk_tm[b].ap[:, c * 128:(c + 1) * 128], ident_b.ap, (qk_tm[b], ident_b))
            if c < 6:
                copy("act", qT[c].ap, tb, (s_.res,), (qT[c],))
            else:
                g, cc = (c - 6) // 2, (c - 6) % 2
                if g == 0:
                    copy("act", K0[par][cc].ap, tb, (s_.res,), (K0[par][cc],))
                elif g == 1:
                    copy("act", K1[par][cc].ap, tb, (s_.res,), (K1[par][cc],))
                else:
                    copy("act", K2[cc].ap[:, tok0:tok0 + TT], tb, (s_.res,), (K2[cc],))
        if t == dbg.get("_tile", 0) and "qT" in dbg_out:
            for c in range(6):
                S.op("sp", lambda e, c=c: e.dma_start(out=dbg_out["qT"][c * 128:(c + 1) * 128, :], in_=qT[c].ap),
                     reads=(qT[c],), dma="dbg_qT")

        if LIM < 4:
            continue
        for c in range(6):
            s_ = acc()
            f0 = 128 * c
            pieces = []
            f = f0
            while f < f0 + 128:
                g = f // 192
                fe = min(f0 + 128, (g + 1) * 192)
                pieces.append((f, fe, g))
                f = fe
            for b in range(4):
                for (fa, fe, g) in pieces:
                    r0 = fa - f0
                    s_.mm(s_.t[r0:r0 + (fe - fa), b * 128:(b + 1) * 128], vg[b].ap[:, fa:fe], wp[b].ap[:, g, :],
                          (vg[b], wp[b]), row0=r0, rows=fe - fa)
            dve_stt(rden.ap.rearrange("p (b t) -> p b t", b=4), s_.t[:, 0:TT].rearrange("p (b t) -> p b t", b=4),
                    gm.ap[:, c:c + 1], bbc.ap[:, c, :].unsqueeze(1).broadcast_to([128, 4, 128]),
                    ALU.mult, ALU.add, (s_.res, gm, bbc), (rden,))
            dve_tt(ya[c].ap, rden.ap, ya[c].ap, ALU.mult, (rden, ya[c]), (ya[c],), eng="pool")
        if t == dbg.get("_tile", 0) and "ya" in dbg_out:
            for c in range(6):
                S.op("sp", lambda e, c=c: e.dma_start(out=dbg_out["ya"][c * 128:(c + 1) * 128, :], in_=ya[c].ap),
                     reads=(ya[c],), dma="dbg_ya")

        if LIM < 6:
            continue
        for j in range(3):
            wb, wv = next_unit()
            for oc in range(2):
                c = 2 * j + oc
                s_ = proj_fm(wb, wv, oc * 128, hT, 8)
                copy("act", vT[c].ap, s_.t[:, 0:TT], (s_.res,), (vT[c],))
        s_ = acc()
        tb = s_.t[:, 0:512].bitcast(BF16)
        for b in range(4):
            for cc in range(2):
                s_.tr(tb[:, (b * 2 + cc) * 128:(b * 2 + cc + 1) * 128], vT[cc].ap[:, b * 128:(b + 1) * 128], ident_b.ap,
                      (vT[cc], ident_b))
        copy("act", V0[par].flat, tb, (s_.res,), (V0[par],))
        s_ = acc()
        tb = s_.t[:, 0:512].bitcast(BF16)
        for r in range(4):
            for cc in range(2):
                s_.tr(tb[:, (r * 2 + cc) * 128:(r * 2 + cc + 1) * 128], vT[2 + cc].ap[:, r::4], ident_b.ap,
                      (vT[2 + cc], ident_b))
        copy("act", V1[par].flat, tb, (s_.res,), (V1[par],))
        p0 = 32 * s2
        for rq in range(4):
            s_ = acc()
            tb = s_.t[:, 0:512].bitcast(BF16)
            for rr in range(4):
                r = 4 * rq + rr
                for cc in range(2):
                    s_.tr(tb[p0:p0 + 32, (rr * 2 + cc) * 128:(rr * 2 + cc + 1) * 128], vT[4 + cc].ap[:, r::16], ident_b.ap,
                          (vT[4 + cc], ident_b), tp=(0, p0))
            copy("act", V2[j2 % 2].flat[p0:p0 + 32, rq * 1024:(rq + 1) * 1024], tb[p0:p0 + 32, :], (s_.res,), (V2[j2 % 2],))

        if LIM < 9:
            continue
        acc_ring[0] = [0, 1, 2, 3]
        nE = [0]
        pend = [None]
        gst = {"c": 0, "wb": None, "wv": None}

        def gate_chunk():
            c = gst["c"]
            if c >= 16:
                return
            if c % 2 == 0:
                gst["wb"], gst["wv"] = next_unit()
            s_ = proj_fm(gst["wb"], gst["wv"], (c % 2) * 128, hT, 8)
            act(tg[c].ap, s_.t[:, 0:TT], AF.Tanh, (s_.res,), (tg[c],), scale=0.5)
            gst["c"] = c + 1

        for pair in range(2):
            NS = Sess(4 + pair)
            DS = Sess(6 + pair)
            for half in range(2):
                hg = 2 * pair + half
                hp = 64 * half
                def attend(tiles, ncol, mask_ap, dview, col_lo=0, NS=NS, DS=DS, hp=hp):
                    sS = acc()
                    for i, tl in enumerate(tiles):
                        if tl is None:
                            continue
                        (k_ap, kres, v_ap, vres, q_ap, qres, osel) = tl
                        sS.mm(sS.t[:, i * ncol:(i + 1) * ncol], k_ap, q_ap, (kres, qres))
                    E = Eb[nE[0] % 2]
                    nE[0] += 1
                    act(E.ap[:, col_lo:TT], sS.t[:, col_lo:TT], AF.Exp, (sS.res,), (E,), scale=0.125)
                    nt_ = TT // ncol
                    ev = E.ap.rearrange("p (a c) -> p a c", a=nt_)
                    a0 = col_lo // ncol
                    S.op("pool", lambda e: e.tensor_tensor(ev[:, a0:, :], ev[:, a0:, :],
                                                           mask_ap.unsqueeze(1).broadcast_to([128, nt_ - a0, ncol]), ALU.mult),
                         reads=(E, mcur_b, mprev_b), writes=(E,))

                    def phase2():
                        for i, tl in enumerate(tiles):
                            if tl is None:
                                continue
                            (k_ap, kres, v_ap, vres, q_ap, qres, osel) = tl
                            NS.mm(osel(NS.t[hp:hp + 64, :]), v_ap, E.ap[:, i * ncol:(i + 1) * ncol], (vres, E), row0=hp, rows=64)
                        dv_ = dview(DS.t[hp:hp + 64, :])
                        er_ = E.ap[:, col_lo:TT]
                        if len(dv_.shape) == 3:
                            er_ = er_.rearrange("p (r i) -> p r i", r=dv_.shape[1])
                        DS.mm(dv_, ones_b.ap[:, 0:64], er_, (ones_b, E), row0=hp, rows=64)

                    if pend[0] is not None:
                        pend[0]()
                    pend[0] = phase2
                    gate_chunk()

                c0 = half_c = pair
                qc = qT[0 + pair]
                cur, prv = [], []
                for b in range(4):
                    q_ap = qc.ap[hp:hp + 64, b * 128:(b + 1) * 128]
                    osel = (lambda a, b=b: a[:, b * 128:(b + 1) * 128])
                    cur.append((K0[par][pair].ap[hp:hp + 64, b * 128:(b + 1) * 128], K0[par][pair],
                                V0[par].ap[:, b, hg * 64:(hg + 1) * 64], V0[par], q_ap, qc, osel))
                    if b >= 1:
                        prv.append((K0[par][pair].ap[hp:hp + 64, (b - 1) * 128:b * 128], K0[par][pair],
                                    V0[par].ap[:, b - 1, hg * 64:(hg + 1) * 64], V0[par], q_ap, qc, osel))
                    elif t >= 1:
                        prv.append((K0[1 - par][pair].ap[hp:hp + 64, 384:512], K0[1 - par][pair],
                                    V0[1 - par].ap[:, 3, hg * 64:(hg + 1) * 64], V0[1 - par], q_ap, qc, osel))
                    else:
                        prv.append(None)
                attend(cur, 128, mcur_b.ap, lambda a: a[:, 0:TT])
                lo = 0 if t >= 1 else 128
                attend(prv, 128, mprev_b.ap, lambda a, lo=lo: a[:, lo:TT], col_lo=lo)
                qc = qT[2 + pair]
                cur, prv = [], []
                for r in range(4):
                    q_ap = qc.ap[hp:hp + 64, r::4]
                    osel = (lambda a, r=r: a[:, r::4])
                    cur.append((K1[par][pair].ap[hp:hp + 64, r::4], K1[par][pair],
                                V1[par].ap[:, r, hg * 64:(hg + 1) * 64], V1[par], q_ap, qc, osel))
                    prv.append((K1[1 - par][pair].ap[hp:hp + 64, r::4], K1[1 - par][pair],
                                V1[1 - par].ap[:, r, hg * 64:(hg + 1) * 64], V1[1 - par], q_ap, qc, osel))
                dv1 = lambda a: a.rearrange("p (i r) -> p r i", r=4)
                attend(cur, 128, mcur_b.ap, dv1)
                if t >= 1:
                    attend(prv, 128, mprev_b.ap, dv1)
                qc = qT[4 + pair]
                cur, prv = [], []
                for r in range(16):
                    q_ap = qc.ap[hp:hp + 64, r::16]
                    osel = (lambda a, r=r: a[:, r::16])
                    kc_ap = K2[pair].ap[hp:hp + 64, 2048 * j2 + r:2048 * (j2 + 1):16]
                    cur.append((kc_ap, K2[pair], V2[j2 % 2].ap[:, r, hg * 64:(hg + 1) * 64], V2[j2 % 2], q_ap, qc, osel))
                    if j2 >= 1:
                        kp_ap = K2[pair].ap[hp:hp + 64, 2048 * (j2 - 1) + r:2048 * j2:16]
                        prv.append((kp_ap, K2[pair], V2[(j2 - 1) % 2].ap[:, r, hg * 64:(hg + 1) * 64], V2[(j2 - 1) % 2],
                                    q_ap, qc, osel))
                dv2 = lambda a: a.rearrange("p (i r) -> p r i", r=16)
                attend(cur, 32, mcur_b.ap[:, p0:p0 + 32], dv2)
                if j2 >= 1:
                    attend(prv, 32, mprev_b.ap[:, p0:p0 + 32], dv2)
            if pend[0] is not None:
                pend[0]()
                pend[0] = None
            S.op("dve", lambda e, DS=DS: e.reciprocal(rden.ap, DS.t[:, 0:TT]), reads=(DS.res,), writes=(rden,))
            dve_tt(yb[pair].ap, NS.t[:, 0:TT], rden.ap, ALU.mult, (NS.res, rden), (yb[pair],))
        if t == dbg.get("_tile", 0) and "yb" in dbg_out:
            for c in range(2):
                S.op("sp", lambda e, c=c: e.dma_start(out=dbg_out["yb"][c * 128:(c + 1) * 128, :], in_=yb[c].ap),
                     reads=(yb[c],), dma="dbg_yb")

        if LIM < 10:
            continue
        acc_ring[0] = [0, 1, 2, 3, 6, 7]
        while gst["c"] < 16:
            gate_chunk()

        if LIM < 11:
            continue
        for j in range(4):
            wba, wva = next_unit()
            wbb, wvb = next_unit()
            for oc in range(2):
                m = 2 * j + oc
                sa = proj_fm(wba, wva, oc * 128, ya, 6)
                sb = proj_fm(wbb, wvb, oc * 128, yb, 2)
                dve_stt(t1b.ap, tg[m].ap, 1.0, sa.t[:, 0:TT], ALU.add, ALU.mult, (tg[m], sa.res), (t1b,))
                dve_stt(t2b.ap, tg[8 + m].ap, 1.0, sb.t[:, 0:TT], ALU.add, ALU.mult, (tg[8 + m], sb.res), (t2b,))
                dve_tt(mrg[m].ap, t1b.ap, t2b.ap, ALU.add, (t1b, t2b), (mrg[m],), eng="pool")

        if LIM < 12:
            continue
        for j in range(4):
            wb, wv = next_unit()
            for oc in range(2):
                m = 2 * j + oc
                s_ = proj_fm(wb, wv, oc * 128, mrg, 8)
                slot = xr[m % 3]
                dma_in(slot, slot.ap, xT[m * 128:(m + 1) * 128, tok0:tok0 + TT], "xr%d" % (m % 3))
                dve_stt(x1[m].ap, s_.t[:, 0:TT], 0.5, slot.ap, ALU.mult, ALU.add, (s_.res, slot), (x1[m],))
        if t == dbg.get("_tile", 0) and "x1" in dbg_out:
            for c in range(8):
                S.op("sp", lambda e, c=c: e.dma_start(out=dbg_out["x1"][c * 128:(c + 1) * 128, :], in_=x1[c].ap),
                     reads=(x1[c],), dma="dbg_x1")

        if LIM < 13:
            continue
        rms_norm(x1, None, tok0, 8, w_h)

        if LIM < 14:
            continue
        for j in range(11):
            wba, wva = next_unit()
            wbv, wvv = next_unit()
            for oc in range(2):
                c = 2 * j + oc
                sa = proj_fm(wba, wva, oc * 128, hT, 8)
                sv = proj_fm(wbv, wvv, oc * 128, hT, 8)
                o = ob[c % 2]
                g_ = gel[c % 2]
                ab = abuf[c % 2]
                w0 = cw.ap[:, c, 0:1]
                w1 = cw.ap[:, c, 1:2]
                w2 = cw.ap[:, c, 2:3]
                copy("pool", ab.ap[:, 0:2], halo.ap[:, c, :], (halo,), (ab,))
                copy("act", ab.ap[:, 2:TT + 2], sa.t[:, 0:TT], (sa.res,), (ab,))
                S.op("act", lambda e, o=o, ab=ab, w2=w2, c=c: e.activation(o.ap, ab.ap[:, 2:TT + 2], AF.Identity,
                                                                    bias=cb.ap[:, c:c + 1], scale=w2),
                     reads=(ab, cw, cb), writes=(o,))
                copy("pool", halo.ap[:, c, :], ab.ap[:, TT:TT + 2], (ab,), (halo,))
                dve_stt(o.ap, ab.ap[:, 1:TT + 1], w1, o.ap, ALU.mult, ALU.add, (ab, cw, o), (o,))
                dve_stt(o.ap, ab.ap[:, 0:TT], w0, o.ap, ALU.mult, ALU.add, (ab, cw, o), (o,))
                act(g_.ap, o.ap, AF.Gelu, (o,), (g_,))
                dve_tt(gg[c].ap, g_.ap, sv.t[:, 0:TT], ALU.mult, (g_, sv.res), (gg[c],))

        if LIM < 15:
            continue
        for j in range(4):
            wb0, wv0 = next_unit()
            wb1, wv1 = next_unit()
            for oc in range(2):
                m = 2 * j + oc
                s_ = acc()
                for kc in range(NFF):
                    wbx, wvx = (wb0, wv0) if kc < 11 else (wb1, wv1)
                    s_.mm(s_.t[:, 0:TT], wvx[:, kc % 11, oc * 128:(oc + 1) * 128], gg[kc].ap, (wbx, gg[kc]))
                dve_tt(x1[m].ap, s_.t[:, 0:TT], x1[m].ap, ALU.add, (s_.res, x1[m]), (x1[m],))

        if LIM < 16:
            continue
        def w_o(kc, src, sres, gcol, rbc, rbres):
            dve_stt(x1[kc].ap, src, gcol, rbc, ALU.mult, ALU.mult, (sres, gvec, rbres), (x1[kc],))
            S.op("sp", lambda e, kc=kc, tok0=tok0: e.dma_start(out=outT[kc * 128:(kc + 1) * 128, tok0:tok0 + TT], in_=x1[kc].ap),
                 reads=(x1[kc],), dma="out%d" % kc)
        if t + 1 < ntiles:
            rms_norm(None, xT, tok0 + TT, 0, w_h)
        rms_norm(x1, None, tok0, 16, w_o)

    sem_ctx = []
    esem = {}
    for e in ("pe", "act", "dve", "pool"):
        c_ = nc.semaphore("sem_" + e)
        esem[e] = c_.__enter__()
        sem_ctx.append(c_)
    dsem = {}
    for name in S.dma_cum:
        c_ = nc.semaphore("dsem_" + name)
        dsem[name] = c_.__enter__()
        sem_ctx.append(c_)
    with nc.Block() as block:
        S.emit(nc, block, esem, dsem)
    for c_ in reversed(sem_ctx):
        c_.__exit__(None, None, None)
    for c_ in reversed(ctx):
        c_.__exit__(None, None, None)
    return nc


def make_shared(mix_norm_g, w_in, gmlp_norm_g, w_spatial, b_spatial, w_branch_a, w_branch_b, w_out,
                ffn_norm_g, w_up, conv_w, conv_b, w_down, final_norm_g):
    f32 = np.float32
    A = lambda a: np.ascontiguousarray(np.asarray(a, dtype=f32))

    def pk(v):
        v = np.asarray(v, dtype=f32)
        return v.reshape(-1, 128).T

    gvec = np.concatenate([pk(mix_norm_g[0]), pk(ffn_norm_g[0]), pk(final_norm_g)], axis=1)
    gm = pk(gmlp_norm_g[0])
    bsp = np.asarray(b_spatial[0], dtype=f32)
    grp = (np.arange(768) // 192).reshape(6, 128)
    bbc = bsp[grp]
    bbc = np.transpose(bbc, (1, 0, 2)).reshape(128, 6 * 128)
    wspT = np.transpose(np.asarray(w_spatial[0], dtype=f32), (2, 0, 1)).reshape(128, 4 * 128)
    cwv = np.asarray(conv_w[0], dtype=f32)
    cwl = np.transpose(cwv.reshape(3, NFF, 128), (2, 1, 0)).reshape(128, NFF * 3)
    cbl = pk(conv_b[0])
    ident = np.eye(128, dtype=f32)
    kk = np.arange(128)[:, None]
    qq = np.arange(128)[None, :]
    mcur = (kk <= qq).astype(f32)
    mprev = (kk >= qq).astype(f32)
    invf = (np.float32(500000.0) ** (-np.arange(0, 16, 2, dtype=f32) / np.float32(16))).astype(f32)
    consts = np.concatenate([ident, mcur, mprev, np.broadcast_to(invf[None, :], (128, 8))], axis=1)
    return {
        "w_in": A(w_in[0]), "w_pa": A(w_branch_a[0]), "w_pb": A(w_branch_b[0]), "w_out": A(w_out[0]),
        "w_up": A(w_up[0]), "w_down": A(w_down[0]),
        "gvec": A(gvec), "gm": A(gm), "bbc": A(bbc), "wspT": A(wspT), "convw": A(cwl), "convb": A(cbl),
        "consts": A(consts),
    }


_NC_CACHE = {}


def kernel(x, positions, mix_norm_g, w_in, gmlp_norm_g, w_spatial, b_spatial, w_branch_a, w_branch_b, w_out,
           ffn_norm_g, w_up, conv_w, conv_b, w_down, final_norm_g):
    x = np.asarray(x, dtype=np.float32)
    positions = np.asarray(positions)
    shared = make_shared(mix_norm_g, w_in, gmlp_norm_g, w_spatial, b_spatial, w_branch_a, w_branch_b, w_out,
                         ffn_norm_g, w_up, conv_w, conv_b, w_down, final_norm_g)
    n = x.shape[0]
    in_maps = []
    for b in range(n):
        m = dict(shared)
        m["xT"] = np.ascontiguousarray(x[b].T)
        m["pos"] = np.ascontiguousarray(positions[b].astype(np.int32).reshape(32, 128).T)
        in_maps.append(m)
    if "nc" not in _NC_CACHE:
        _NC_CACHE["nc"] = build_nc()
    res = run_bass_kernel_spmd(_NC_CACHE["nc"], in_maps, core_ids=list(range(n)))
    out = np.stack([np.ascontiguousarray(r["outT"].T) for r in res.results], axis=0)
    return out.astype(np.float32)
```

```python
import numpy as np
import concourse.bass as bass
import concourse.mybir as mybir
from concourse.bass_utils import run_bass_kernel_spmd

F32 = mybir.dt.float32
BF16 = mybir.dt.bfloat16
I32 = mybir.dt.int32
AF = mybir.ActivationFunctionType
ALU = mybir.AluOpType

D = 1024
SEQ = 4096
TT = 512
NT = SEQ // TT
INW = 5888
DFF = 2816
NFF = DFF // 128
EPS = 1e-6
KB = 1024


class Res:
    _n = 0

    def __init__(self, name, lo=None, hi=None):
        Res._n += 1
        self.rid = Res._n
        self.name = name
        self.lo, self.hi = lo, hi
        self.ov = [self]


class Sched:
    ENG = ("pe", "act", "dve", "pool", "sp")

    def __init__(self):
        self.ops = {e: [] for e in self.ENG}
        self.lastw = {}
        self.readers = {}
        self.dma_cum = {}

    def op(self, eng, fn, reads=(), writes=(), dma=None):
        deps = set()
        for b in reads:
            ev = self.lastw.get(b.rid)
            if ev is not None:
                deps.add(ev)
        for b in writes:
            for x in b.ov:
                ev = self.lastw.get(x.rid)
                if ev is not None:
                    deps.add(ev)
                for r in self.readers.get(x.rid, ()):
                    deps.add(r)
        if dma is not None:
            v = self.dma_cum.get(dma, 0) + 16
            self.dma_cum[dma] = v
            ev = ("dma:" + dma, v)
        else:
            ev = (eng, len(self.ops[eng]))
        self.ops[eng].append((fn, deps, dma))
        for b in reads:
            self.readers.setdefault(b.rid, []).append(ev)
        for b in writes:
            for x in b.ov:
                self.lastw[x.rid] = ev
                self.readers[x.rid] = []
        return ev

    def emit(self, nc, block, esem, dsem):
        comp = ("pe", "act", "dve", "pool")
        needed = {e: set() for e in comp}
        for e in self.ENG:
            for (_, deps, _) in self.ops[e]:
                for (k, v) in deps:
                    if k in needed and not (k == "pe" and e == "pe"):
                        needed[k].add(v)
        rank = {e: {idx: i + 1 for i, idx in enumerate(sorted(needed[e]))} for e in comp}
        ops = self.ops

        def run(ename, eng):
            known = {}
            for idx, (fn, deps, dma) in enumerate(ops[ename]):
                want = {}
                for (k, v) in deps:
                    if k == "pe" and ename == "pe":
                        continue
                    if k in rank:
                        val = rank[k][v]
                    else:
                        val = v
                    if val > want.get(k, 0):
                        want[k] = val
                for k, val in want.items():
                    if known.get(k, 0) >= val:
                        continue
                    known[k] = val
                    sem = esem[k] if k in esem else dsem[k[4:]]
                    eng.wait_ge(sem, val)
                ins = fn(eng)
                if dma is not None:
                    ins.then_inc(dsem[dma], 16)
                elif ename in rank and idx in rank[ename]:
                    ins.then_inc(esem[ename], 1)

        @block.tensor
        def _(eng):
            run("pe", eng)

        @block.scalar
        def _(eng):
            run("act", eng)

        @block.vector
        def _(eng):
            run("dve", eng)

        @block.gpsimd
        def _(eng):
            run("pool", eng)

        @block.sync
        def _(eng):
            run("sp", eng)
            for name, v in self.dma_cum.items():
                eng.wait_ge(dsem[name], v)


def build_nc(ntiles=NT, dbg=None):
    nc = bass.Bass("TRN2", target_bir_lowering=False)
    S = Sched()
    dbg = dbg or {}
    LIM = dbg.get("_lim", 99)

    def din(name, shape, dt=F32):
        return nc.dram_tensor(name, list(shape), dt, kind="ExternalInput").ap()

    xT = din("xT", [D, SEQ])
    pos_d = din("pos", [128, 32], I32)
    w_in = din("w_in", [D, INW])
    w_pa = din("w_pa", [768, D])
    w_pb = din("w_pb", [256, D])
    w_out = din("w_out", [D, D])
    w_up = din("w_up", [D, 2 * DFF])
    w_down = din("w_down", [DFF, D])
    gvec_d = din("gvec", [128, 24])
    gm_d = din("gm", [128, 6])
    bbc_d = din("bbc", [128, 6 * 128])
    wsp_d = din("wspT", [128, 4 * 128])
    cw_d = din("convw", [128, NFF * 3])
    cb_d = din("convb", [128, NFF])
    cst_d = din("consts", [128, 3 * 128 + 8])
    outT = nc.dram_tensor("outT", [D, SEQ], F32, kind="ExternalOutput").ap()
    dbg_out = {}
    for name, (shape, dt) in dbg.items():
        dbg_out[name] = nc.dram_tensor("dbg_" + name, list(shape), dt, kind="ExternalOutput").ap()

    ARENA_BYTES = 207 * KB
    ctx = []
    arena_t = nc.sbuf_tensor("arena", [128, ARENA_BYTES // 4], F32)
    arena = arena_t.__enter__()
    ctx.append(arena_t)
    banks = []
    for i in range(8):
        t = nc.psum_tensor("ps%d" % i, [128, 512], F32)
        banks.append(t.__enter__())
        ctx.append(t)
    bankres = [Res("ps%d" % i) for i in range(8)]

    allocs = []
    top = [0]

    class Buf(Res):
        def __init__(self, name, shape, dt, at=None):
            esz = 4 if dt in (F32, I32) else 2
            n = 1
            for s_ in shape:
                n *= s_
            nbytes = n * esz
            if at is None:
                off = (top[0] + 63) // 64 * 64
                top[0] = off + nbytes
            else:
                off = at
            assert off % 4 == 0 and off + nbytes <= ARENA_BYTES, (name, off, nbytes)
            Res.__init__(self, name, off, off + nbytes)
            for o in allocs:
                if o.lo < self.hi and self.lo < o.hi:
                    o.ov.append(self)
                    self.ov.append(o)
            allocs.append(self)
            v = arena[:, off // 4:(off + nbytes) // 4]
            if dt != F32:
                v = v.bitcast(dt)
            self.flat = v
            if len(shape) == 1:
                self.ap = v
            elif len(shape) == 2:
                self.ap = v.rearrange("p (a b) -> p a b", a=shape[0])
            elif len(shape) == 3:
                self.ap = v.rearrange("p (a b c) -> p a b c", a=shape[0], b=shape[1])
            else:
                raise ValueError(shape)
            self.shape = shape
            self.dt = dt

    cst = Buf("cst", [3 * 128 + 8], F32)
    ident_f = cst.ap[:, 0:128]
    mcur_f = cst.ap[:, 128:256]
    mprev_f = cst.ap[:, 256:384]
    invf = cst.ap[:, 384:392]
    ident_b = Buf("ident_b", [128], BF16)
    mcur_b = Buf("mcur_b", [128], BF16)
    mprev_b = Buf("mprev_b", [128], BF16)
    ones_f = Buf("ones_f", [128], F32)
    ones_b = Buf("ones_b", [128], BF16)
    gvec = Buf("gvec", [24], F32)
    gm = Buf("gm", [6], F32)
    bbc = Buf("bbc", [6, 128], F32)
    wsp_f = Buf("wsp_f", [4, 128], F32)
    wsp_b = Buf("wsp_b", [4, 128], BF16)
    cw = Buf("cw", [NFF, 3], F32)
    cb = Buf("cb", [NFF], F32)
    pos_i = Buf("pos_i", [32], I32)
    pos_f = Buf("pos_f", [32], F32)
    cos_t = Buf("cos_t", [32, 8], F32)
    sin_t = Buf("sin_t", [32, 8], F32)
    halo = Buf("halo", [NFF, 2], F32)
    ss_sb = Buf("ss_sb", [4], F32)
    sd_sb = Buf("sd_sb", [4], F32)
    rstd = Buf("rstd", [4], F32)
    rstd_v = Buf("rstd_v", [4], F32)
    ssv = Buf("ssv", [4], F32)
    diag = Buf("diag", [4, 128], F32)
    junk = Buf("junk", [768], BF16)
    ropeA = Buf("ropeA", [24, 8], F32)
    ropeB = Buf("ropeB", [24, 8], F32)
    xr = [Buf("xr%d" % i, [TT], F32) for i in range(3)]
    sq = [Buf("sq%d" % i, [TT], BF16) for i in range(2)]
    hT = [Buf("hT%d" % i, [TT], BF16) for i in range(8)]
    x1 = [Buf("x1_%d" % i, [TT], F32) for i in range(8)]
    K0 = [[Buf("K0_%d_%d" % (p, c), [TT], BF16) for c in range(2)] for p in range(2)]
    K1 = [[Buf("K1_%d_%d" % (p, c), [TT], BF16) for c in range(2)] for p in range(2)]
    K2 = [Buf("K2_%d" % c, [SEQ], BF16) for c in range(2)]
    V0 = [Buf("V0_%d" % p, [4, 256], BF16) for p in range(2)]
    V1 = [Buf("V1_%d" % p, [4, 256], BF16) for p in range(2)]
    V2 = [Buf("V2_%d" % p, [16, 256], BF16) for p in range(2)]
    USZ = 11 * 256
    wst = [Buf("wst%d" % i, [USZ], F32) for i in range(2)]
    wbf = [Buf("wbf%d" % i, [USZ], BF16) for i in range(6)]
    P0 = (top[0] + 63) // 64 * 64

    def PB(name, shape, dt, kb_off):
        return Buf(name, shape, dt, at=P0 + int(kb_off * KB))

    ya = [PB("ya%d" % c, [TT], BF16, 0 + c) for c in range(6)]
    vg = [PB("vg%d" % b, [768], BF16, 6 + 1.5 * b) for b in range(4)]
    wp = [PB("wp%d" % b, [4, 128], BF16, 12 + b) for b in range(4)]
    qk_tm = [PB("qk%d" % b, [1536], BF16, 16 + 3 * b) for b in range(4)]
    rot = [PB("rot%d" % b, [24, 16], F32, 28 + 1.5 * b) for b in range(4)]
    qT = [PB("qT%d" % c, [TT], BF16, 34 + c) for c in range(6)]
    vT = [PB("vT%d" % c, [TT], BF16, 40 + c) for c in range(6)]
    Eb = [PB("E%d" % i, [TT], BF16, 46 + i) for i in range(2)]
    rden = PB("rden", [TT], F32, 48)
    stgs = [PB("stg%d" % i, [256], F32, 48 + i) for i in range(2)]
    yb = [PB("yb%d" % i, [TT], BF16, 50 + i) for i in range(2)]
    tg = [PB("tg%d" % c, [TT], BF16, 16 + c) for c in range(16)]
    t1b = PB("t1b", [TT], F32, 6)
    t2b = PB("t2b", [TT], F32, 8)
    mrg = [PB("mrg%d" % c, [TT], BF16, 40 + c) for c in range(8)]
    gg = [PB("gg%d" % c, [TT], BF16, 0 + c) for c in range(NFF)]
    ob = [PB("ob%d" % i, [TT], F32, 22 + 2 * i) for i in range(2)]
    gel = [PB("gel%d" % i, [TT], F32, 26 + 2 * i) for i in range(2)]
    abuf = [PB("abuf%d" % i, [TT + 2], F32, 30 + 2.25 * i) for i in range(2)]
    assert P0 + 52 * KB <= ARENA_BYTES, P0

    def dma_in(dst_buf, dst_ap, src_ap, sem, extra_w=()):
        S.op("sp", lambda e: e.dma_start(out=dst_ap, in_=src_ap), reads=(), writes=(dst_buf,) + tuple(extra_w), dma=sem)

    def act(out, in_, func, reads, writes, **kw):
        return S.op("act", lambda e: e.activation(out, in_, func, **kw), reads=reads, writes=writes)

    def dve_tt(out, in0, in1, op, reads, writes, eng="dve"):
        return S.op(eng, lambda e: e.tensor_tensor(out, in0, in1, op), reads=reads, writes=writes)

    def dve_ts(out, in0, s1, s2, op0, op1, reads, writes, eng="dve"):
        if op1 is None:
            return S.op(eng, lambda e: e.tensor_scalar(out, in0, s1, None, op0), reads=reads, writes=writes)
        return S.op(eng, lambda e: e.tensor_scalar(out, in0, s1, s2, op0, op1), reads=reads, writes=writes)

    def dve_stt(out, in0, scalar, in1, op0, op1, reads, writes):
        return S.op("dve", lambda e: e.scalar_tensor_tensor(out, in0, scalar, in1, op0, op1), reads=reads, writes=writes)

    def copy(eng, out, in_, reads, writes):
        if eng == "act":
            return S.op("act", lambda e: e.activation(out, in_, AF.Copy), reads=reads, writes=writes)
        return S.op(eng, lambda e: e.tensor_copy(out, in_), reads=reads, writes=writes)

    class Sess:
        def __init__(self, bi):
            self.bi = bi
            self.res = bankres[bi]
            self.t = banks[bi]
            self.started = set()

        def mm(self, out, lhsT, rhs, reads, row0=0, rows=128, tp=None):
            qs = set(range(row0 // 32, (row0 + rows + 31) // 32))
            first = not (qs & self.started)
            assert first or qs <= self.started
            self.started |= qs
            kw = {}
            if tp is not None:
                kw["tile_position"] = tp
            S.op("pe", lambda e: e.matmul(out, lhsT, rhs, start=first, stop=True, skip_group_check=True, **kw),
                 reads=reads, writes=(self.res,))

        def tr(self, out, in_, ident, reads, tp=None):
            kw = {}
            if tp is not None:
                kw["tile_position"] = tp
            S.op("pe", lambda e: e.transpose(out, in_, ident, **kw), reads=reads, writes=(self.res,))

    accn = [0]

    acc_ring = [[0, 1, 2, 3, 6, 7]]

    def acc():
        r_ = acc_ring[0]
        b = r_[accn[0] % len(r_)]
        accn[0] += 1
        return Sess(b)

    units = []

    def add_unit(w, r0, kc, c0):
        units.append((w[r0:r0 + kc * 128, c0:c0 + 256], kc))

    for t in range(ntiles):
        for j in range(23):
            add_unit(w_in, 0, 8, 256 * j)
        for j in range(4):
            add_unit(w_pa, 0, 6, 256 * j)
            add_unit(w_pb, 0, 2, 256 * j)
        for j in range(4):
            add_unit(w_out, 0, 8, 256 * j)
        for j in range(11):
            add_unit(w_up, 0, 8, 256 * j)
            add_unit(w_up, 0, 8, DFF + 256 * j)
        for j in range(4):
            add_unit(w_down, 0, 11, 256 * j)
            add_unit(w_down, 11 * 128, 11, 256 * j)
    uloaded = [0]
    uused = [0]
    PREF = 4
    UPT = len(units) // ntiles
    wscr = nc.dram_tensor("wscr", [UPT, 128, USZ], BF16, kind="Internal").ap()
    scr_res = [Res("wscr%d" % u) for u in range(UPT)]

    def _writeback(i):
        src, kc = units[i]
        wb = wbf[i % 6]
        u = i % UPT
        S.op("sp", lambda e: e.dma_start(out=wscr[u, :, 0:kc * 256], in_=wb.ap[:, 0:kc * 256]),
             reads=(wb,), writes=(scr_res[u],), dma="wsb%d" % (i % 6))

    def _load(i):
        src, kc = units[i]
        wb = wbf[i % 6]
        u = i % UPT
        if i < UPT:
            st = wst[i % 2]
            dst = st.ap[:, 0:kc * 256].rearrange("p (k c) -> p k c", k=kc)
            S.op("sp", lambda e: e.dma_start(out=dst, in_=src.rearrange("(k p) c -> p k c", p=128)),
                 writes=(st,), dma="wst%d" % (i % 2))
            if ntiles > 1 and i >= 1:
                _writeback(i - 1)
            if i % 5 == 0:
                S.op("pool", lambda e: e.tensor_copy(wb.ap[:, 0:kc * 256], st.ap[:, 0:kc * 256]), reads=(st,), writes=(wb,))
            elif i % 5 in (1, 3):
                S.op("dve", lambda e: e.tensor_copy(wb.ap[:, 0:kc * 256], st.ap[:, 0:kc * 256]), reads=(st,), writes=(wb,))
            else:
                S.op("act", lambda e: e.activation(wb.ap[:, 0:kc * 256], st.ap[:, 0:kc * 256], AF.Copy), reads=(st,), writes=(wb,))
        else:
            if i == UPT:
                _writeback(UPT - 1)
            S.op("sp", lambda e: e.dma_start(out=wb.ap[:, 0:kc * 256], in_=wscr[u, :, 0:kc * 256]),
                 reads=(scr_res[u],), writes=(wb,), dma="wld%d" % (i % 6))

    def next_unit():
        i = uused[0]
        uused[0] += 1
        while uloaded[0] < min(len(units), i + 1 + PREF):
            _load(uloaded[0])
            uloaded[0] += 1
        kc = units[i][1]
        wb = wbf[i % 6]
        return wb, wb.ap[:, 0:kc * 256].rearrange("p (k c) -> p k c", k=kc)

    dma_in(cst, cst.ap, cst_d, "c_cst")
    dma_in(gvec, gvec.ap, gvec_d, "c_gvec")
    dma_in(gm, gm.ap, gm_d, "c_gm")
    dma_in(bbc, bbc.flat, bbc_d, "c_bbc")
    dma_in(wsp_f, wsp_f.flat, wsp_d, "c_wsp")
    dma_in(cw, cw.flat, cw_d, "c_cw")
    dma_in(cb, cb.ap, cb_d, "c_cb")
    dma_in(pos_i, pos_i.ap, pos_d, "c_pos")
    copy("dve", ident_b.ap, ident_f, (cst,), (ident_b,))
    copy("dve", mcur_b.ap, mcur_f, (cst,), (mcur_b,))
    copy("dve", mprev_b.ap, mprev_f, (cst,), (mprev_b,))
    S.op("pool", lambda e: e.memset(ones_f.ap, 1.0), writes=(ones_f,))
    S.op("pool", lambda e: e.memset(ones_b.ap, 1.0), writes=(ones_b,))
    S.op("pool", lambda e: e.memset(halo.flat, 0.0), writes=(halo,))
    for c in range(2):
        S.op("pool", lambda e, c=c: e.memset(K2[c].ap, 0.0), writes=(K2[c],))
    for p in range(2):
        S.op("pool", lambda e, p=p: e.memset(V2[p].flat, 0.0), writes=(V2[p],))
    dve_tt(wsp_f.ap, wsp_f.ap, mcur_f.unsqueeze(1).broadcast_to([128, 4, 128]), ALU.mult, (wsp_f, cst), (wsp_f,))
    copy("dve", pos_f.ap, pos_i.ap, (pos_i,), (pos_f,))
    ang = Buf("ang", [32, 8], F32, at=P0)
    kq = Buf("kq", [32, 8], F32, at=P0 + 2 * KB)
    ki = Buf("ki", [32, 8], I32, at=P0 + 4 * KB)
    red = Buf("red", [32, 8], F32, at=P0 + 6 * KB)
    dve_tt(ang.ap, pos_f.ap.unsqueeze(2).broadcast_to([128, 32, 8]), invf.unsqueeze(1).broadcast_to([128, 32, 8]),
           ALU.mult, (pos_f, cst), (ang,))
    TWO_PI = 2.0 * np.pi
    C1 = float(np.float32(6.28125))
    C2 = float(np.float32(TWO_PI - 6.28125))
    C3 = float(TWO_PI - 6.28125 - float(np.float32(TWO_PI - 6.28125)))
    msk = Buf("msk", [32, 8], F32, at=P0 + 8 * KB)
    PI = float(np.pi)
    for (tab, shift) in ((sin_t, 0.0), (cos_t, float(np.pi / 2))):
        dve_ts(kq.ap, ang.ap, float(1.0 / TWO_PI), None, ALU.mult, None, (ang,), (kq,))
        copy("dve", ki.ap, kq.ap, (kq,), (ki,))
        copy("dve", kq.ap, ki.ap, (ki,), (kq,))
        dve_stt(red.ap, kq.ap, -C1, ang.ap, ALU.mult, ALU.add, (kq, ang), (red,))
        dve_stt(red.ap, kq.ap, -C2, red.ap, ALU.mult, ALU.add, (kq, red), (red,))
        dve_stt(red.ap, kq.ap, -C3, red.ap, ALU.mult, ALU.add, (kq, red), (red,))
        if shift != 0.0:
            dve_ts(red.ap, red.ap, shift, None, ALU.add, None, (red,), (red,))
        for _ in range(2):
            dve_ts(msk.ap, red.ap, PI, -TWO_PI, ALU.is_gt, ALU.mult, (red,), (msk,))
            dve_tt(red.ap, red.ap, msk.ap, ALU.add, (red, msk), (red,))
            dve_ts(msk.ap, red.ap, -PI, TWO_PI, ALU.is_lt, ALU.mult, (red,), (msk,))
            dve_tt(red.ap, red.ap, msk.ap, ALU.add, (red, msk), (red,))
        act(tab.ap, red.ap, AF.Sin, (red,), (tab,))

    def dump(name, buf, ap):
        if name in dbg_out:
            S.op("sp", lambda e: e.dma_start(out=dbg_out[name], in_=ap), reads=(buf,), dma="dbg_" + name)

    dump("cos", cos_t, cos_t.flat)
    dump("sin", sin_t, sin_t.flat)

    def rms_norm(src_chunks, load_from, tok0, gcol0, write_fn, mid_fn=None):
        ssS = Sess(4)
        for kc in range(8):
            if load_from is not None:
                slot = xr[kc % 3]
                dma_in(slot, slot.ap, load_from[kc * 128:(kc + 1) * 128, tok0:tok0 + TT], "xr%d" % (kc % 3))
                src, sres = slot.ap, slot
            else:
                src, sres = src_chunks[kc].ap, src_chunks[kc]
            sqb = sq[kc % 2]
            act(sqb.ap, src, AF.Square, (sres,), (sqb,))
            for b in range(4):
                ssS.mm(ssS.t[:, b:b + 1], sqb.ap[:, b * 128:(b + 1) * 128], ones_b.ap[:, 0:1], (sqb, ones_b))
        dve_ts(ss_sb.ap, ssS.t[:, 0:4], 1.0 / D, EPS, ALU.mult, ALU.add, (ssS.res,), (ss_sb,))
        act(sd_sb.ap, ss_sb.ap, AF.Sqrt, (ss_sb,), (sd_sb,))
        S.op("dve", lambda e: e.reciprocal(rstd.ap, sd_sb.ap), reads=(sd_sb,), writes=(rstd,))
        dve_tt(diag.ap, ident_f.unsqueeze(1).broadcast_to([128, 4, 128]),
               rstd.ap.unsqueeze(2).broadcast_to([128, 4, 128]), ALU.mult, (cst, rstd), (diag,))
        if mid_fn is not None:
            mid_fn()
        rb = Sess(5)
        rb.mm(rb.t[:, 0:TT], ones_f.ap, diag.flat, (ones_f, diag))
        for kc in range(8):
            if load_from is not None:
                slot = xr[(kc + 2) % 3]
                dma_in(slot, slot.ap, load_from[kc * 128:(kc + 1) * 128, tok0:tok0 + TT], "xr%d" % ((kc + 2) % 3))
                src, sres = slot.ap, slot
            else:
                src, sres = src_chunks[kc].ap, src_chunks[kc]
            write_fn(kc, src, sres, gvec.ap[:, gcol0 + kc:gcol0 + kc + 1], rb.t[:, 0:TT], rb.res)

    def proj_fm(wb, wv, col0, rhs_chunks, kc_n):
        s_ = acc()
        for kc in range(kc_n):
            s_.mm(s_.t[:, 0:TT], wv[:, kc, col0:col0 + 128], rhs_chunks[kc].ap, (wb, rhs_chunks[kc]))
        return s_

    def w_h(kc, src, sres, gcol, rbc, rbres):
        dve_stt(hT[kc].ap, src, gcol, rbc, ALU.mult, ALU.mult, (sres, gvec, rbres), (hT[kc],))

    def emit_S2():
        for j in range(3):
            wb, wv = next_unit()
            for oc in range(2):
                c = 2 * j + oc
                s_ = proj_fm(wb, wv, oc * 128, hT, 8)
                act(ya[c].ap, s_.t[:, 0:TT], AF.Gelu, (s_.res,), (ya[c],))

    for t in range(ntiles):
        tok0 = t * TT
        par = t % 2
        j2, s2 = t // 4, t % 4

        if LIM < 1:
            continue
        if t == 0:
            rms_norm(None, xT, tok0, 0, w_h)
        if t == dbg.get("_tile", 0):
            for kc in range(8):
                if "hT" in dbg_out:
                    S.op("sp", lambda e, kc=kc: e.dma_start(out=dbg_out["hT"][kc * 128:(kc + 1) * 128, :], in_=hT[kc].ap),
                         reads=(hT[kc],), dma="dbg_hT")
        if LIM < 2:
            continue
        if t == 0:
            emit_S2()
        if LIM < 3:
            continue
        for j in range(3):
            wb, wv = next_unit()
            for b in range(4):
                s_ = acc()
                for kc in range(8):
                    s_.mm(s_.t[:, 0:256], hT[kc].ap[:, b * 128:(b + 1) * 128], wv[:, kc, :], (hT[kc], wb))
                act(vg[b].ap[:, j * 256:(j + 1) * 256], s_.t[:, 0:256], AF.Gelu, (s_.res,), (vg[b],))
        for b in range(4):
            S.op("act", lambda e, b=b: e.activation(junk.ap, vg[b].ap, AF.Square, accum_out=ssv.ap[:, b:b + 1]),
                 reads=(vg[b],), writes=(junk, ssv))
        dve_ts(ss_sb.ap, ssv.ap, 1.0 / 768, EPS, ALU.mult, ALU.add, (ssv,), (ss_sb,))
        act(sd_sb.ap, ss_sb.ap, AF.Sqrt, (ss_sb,), (sd_sb,))
        S.op("dve", lambda e: e.reciprocal(rstd_v.ap, sd_sb.ap), reads=(sd_sb,), writes=(rstd_v,))
        for b in range(4):
            dve_ts(wp[b].ap, wsp_f.ap, rstd_v.ap[:, b:b + 1], None, ALU.mult, None, (wsp_f, rstd_v), (wp[b],))

        if LIM < 4.2:
            continue
        for j in range(6):
            wb, wv = next_unit()
            for b in range(4):
                s_ = acc()
                for kc in range(8):
                    s_.mm(s_.t[:, 0:256], hT[kc].ap[:, b * 128:(b + 1) * 128], wv[:, kc, :], (hT[kc], wb))
                stg = stgs[(4 * j + b) % 2]
                copy("act", stg.ap, s_.t[:, 0:256], (s_.res,), (stg,))
                copy("pool", qk_tm[b].ap[:, j * 256:(j + 1) * 256], stg.ap, (stg,), (qk_tm[b],))
                copy("dve", rot[b].ap[:, 4 * j:4 * j + 4, :],
                     stg.ap.rearrange("p (h d) -> p h d", h=4)[:, :, 0:16], (stg,), (rot[b],))
        if LIM < 4.5:
            continue
        for b in range(4):
            B = 4 * t + b
            Cb = cos_t.ap[:, B, :].unsqueeze(1).broadcast_to([128, 24, 8])
            Sb = sin_t.ap[:, B, :].unsqueeze(1).broadcast_to([128, 24, 8])
            t1v = rot[b].ap[:, :, 0:8]
            t2v = rot[b].ap[:, :, 8:16]
            qv = qk_tm[b].ap.rearrange("p (h d) -> p h d", h=24)
            dve_tt(ropeA.ap, t1v, Cb, ALU.mult, (rot[b], cos_t), (ropeA,), eng="pool")
            dve_tt(ropeB.ap, t2v, Sb, ALU.mult, (rot[b], sin_t), (ropeB,), eng="pool")
            dve_tt(qv[:, :, 0:8], ropeA.ap, ropeB.ap, ALU.subtract, (ropeA, ropeB), (qk_tm[b],), eng="pool")
            dve_tt(ropeA.ap, t1v, Sb, ALU.mult, (rot[b], sin_t), (ropeA,), eng="pool")
            dve_tt(ropeB.ap, t2v, Cb, ALU.mult, (rot[b], cos_t), (ropeB,), eng="pool")
            dve_tt(qv[:, :, 8:16], ropeA.ap, ropeB.ap, ALU.add, (ropeA, ropeB), (qk_tm[b],), eng="pool")
        if LIM < 4:
            continue
        for c in range(6):
            s_ = acc()
            f0 = 128 * c
            pieces = []
            f = f0
            while f < f0 + 128:
                g = f // 192
                fe = min(f0 + 128, (g + 1) * 192)
                pieces.append((f, fe, g))
                f = fe
            for b in range(4):
                for (fa, fe, g) in pieces:
                    r0 = fa - f0
                    s_.mm(s_.t[r0:r0 + (fe - fa), b * 128:(b + 1) * 128], vg[b].ap[:, fa:fe], wp[b].ap[:, g, :],
                          (vg[b], wp[b]), row0=r0, rows=fe - fa)
            dve_stt(rden.ap.rearrange("p (b t) -> p b t", b=4), s_.t[:, 0:TT].rearrange("p (b t) -> p b t", b=4),
                    gm.ap[:, c:c + 1], bbc.ap[:, c, :].unsqueeze(1).broadcast_to([128, 4, 128]),
                    ALU.mult, ALU.add, (s_.res, gm, bbc), (rden,))
            dve_tt(ya[c].ap, rden.ap, ya[c].ap, ALU.mult, (rden, ya[c]), (ya[c],), eng="pool")
        if t == dbg.get("_tile", 0) and "ya" in dbg_out:
            for c in range(6):
                S.op("sp", lambda e, c=c: e.dma_start(out=dbg_out["ya"][c * 128:(c + 1) * 128, :], in_=ya[c].ap),
                     reads=(ya[c],), dma="dbg_ya")

        if LIM < 6:
            continue
        for j in range(3):
            wb, wv = next_unit()
            for oc in range(2):
                c = 2 * j + oc
                s_ = proj_fm(wb, wv, oc * 128, hT, 8)
                copy("act", vT[c].ap, s_.t[:, 0:TT], (s_.res,), (vT[c],))
        if LIM < 4.8:
            continue
        for c in range(12):
            s_ = acc()
            tb = s_.t[:, 0:256].bitcast(BF16)
            for b in range(4):
                s_.tr(tb[:, b * 128:(b + 1) * 128], qk_tm[b].ap[:, c * 128:(c + 1) * 128], ident_b.ap, (qk_tm[b], ident_b))
            if c < 6:
                copy("act", qT[c].ap, tb, (s_.res,), (qT[c],))
            else:
                g, cc = (c - 6) // 2, (c - 6) % 2
                if g == 0:
                    copy("act", K0[par][cc].ap, tb, (s_.res,), (K0[par][cc],))
                elif g == 1:
                    copy("act", K1[par][cc].ap, tb, (s_.res,), (K1[par][cc],))
                else:
                    copy("act", K2[cc].ap[:, tok0:tok0 + TT], tb, (s_.res,), (K2[cc],))
        if t == dbg.get("_tile", 0) and "qT" in dbg_out:
            for c in range(6):
                S.op("sp", lambda e, c=c: e.dma_start(out=dbg_out["qT"][c * 128:(c + 1) * 128, :], in_=qT[c].ap),
                     reads=(qT[c],), dma="dbg_qT")

        s_ = acc()
        tb = s_.t[:, 0:512].bitcast(BF16)
        for b in range(4):
            for cc in range(2):
                s_.tr(tb[:, (b * 2 + cc) * 128:(b * 2 + cc + 1) * 128], vT[cc].ap[:, b * 128:(b + 1) * 128], ident_b.ap,
                      (vT[cc], ident_b))
        copy("act", V0[par].flat, tb, (s_.res,), (V0[par],))
        s_ = acc()
        tb = s_.t[:, 0:512].bitcast(BF16)
        for r in range(4):
            for cc in range(2):
                s_.tr(tb[:, (r * 2 + cc) * 128:(r * 2 + cc + 1) * 128], vT[2 + cc].ap[:, r::4], ident_b.ap,
                      (vT[2 + cc], ident_b))
        copy("act", V1[par].flat, tb, (s_.res,), (V1[par],))
        p0 = 32 * s2
        for rq in range(4):
            s_ = acc()
            tb = s_.t[:, 0:512].bitcast(BF16)
            for rr in range(4):
                r = 4 * rq + rr
                for cc in range(2):
                    s_.tr(tb[p0:p0 + 32, (rr * 2 + cc) * 128:(rr * 2 + cc + 1) * 128], vT[4 + cc].ap[:, r::16], ident_b.ap,
                          (vT[4 + cc], ident_b), tp=(0, p0))
            copy("act", V2[j2 % 2].flat[p0:p0 + 32, rq * 1024:(rq + 1) * 1024], tb[p0:p0 + 32, :], (s_.res,), (V2[j2 % 2],))

        if LIM < 9:
            continue
        acc_ring[0] = [0, 1, 2, 3]
        nE = [0]
        pend = [None]
        gst = {"c": 0, "wb": None, "wv": None}

        def gate_chunk():
            c = gst["c"]
            if c >= 16:
                return
            if c % 2 == 0:
                gst["wb"], gst["wv"] = next_unit()
            s_ = proj_fm(gst["wb"], gst["wv"], (c % 2) * 128, hT, 8)
            act(tg[c].ap, s_.t[:, 0:TT], AF.Tanh, (s_.res,), (tg[c],), scale=0.5)
            gst["c"] = c + 1

        for pair in range(2):
            NS = Sess(4 + pair)
            DS = Sess(6 + pair)
            for half in range(2):
                hg = 2 * pair + half
                hp = 64 * half
                def attend(tiles, ncol, mask_ap, dview, col_lo=0, NS=NS, DS=DS, hp=hp):
                    sS = acc()
                    for i, tl in enumerate(tiles):
                        if tl is None:
                            continue
                        (k_ap, kres, v_ap, vres, q_ap, qres, osel) = tl
                        sS.mm(sS.t[:, i * ncol:(i + 1) * ncol], k_ap, q_ap, (kres, qres))
                    E = Eb[nE[0] % 2]
                    nE[0] += 1
                    act(E.ap[:, col_lo:TT], sS.t[:, col_lo:TT], AF.Exp, (sS.res,), (E,), scale=0.125)
                    nt_ = TT // ncol
                    ev = E.ap.rearrange("p (a c) -> p a c", a=nt_)
                    a0 = col_lo // ncol
                    S.op("pool", lambda e: e.tensor_tensor(ev[:, a0:, :], ev[:, a0:, :],
                                                           mask_ap.unsqueeze(1).broadcast_to([128, nt_ - a0, ncol]), ALU.mult),
                         reads=(E, mcur_b, mprev_b), writes=(E,))

                    def phase2():
                        for i, tl in enumerate(tiles):
                            if tl is None:
                                continue
                            (k_ap, kres, v_ap, vres, q_ap, qres, osel) = tl
                            NS.mm(osel(NS.t[hp:hp + 64, :]), v_ap, E.ap[:, i * ncol:(i + 1) * ncol], (vres, E), row0=hp, rows=64)
                        dv_ = dview(DS.t[hp:hp + 64, :])
                        er_ = E.ap[:, col_lo:TT]
                        if len(dv_.shape) == 3:
                            er_ = er_.rearrange("p (r i) -> p r i", r=dv_.shape[1])
                        DS.mm(dv_, ones_b.ap[:, 0:64], er_, (ones_b, E), row0=hp, rows=64)

                    if pend[0] is not None:
                        pend[0]()
                    pend[0] = phase2
                    gate_chunk()

                c0 = half_c = pair
                qc = qT[0 + pair]
                cur, prv = [], []
                for b in range(4):
                    q_ap = qc.ap[hp:hp + 64, b * 128:(b + 1) * 128]
                    osel = (lambda a, b=b: a[:, b * 128:(b + 1) * 128])
                    cur.append((K0[par][pair].ap[hp:hp + 64, b * 128:(b + 1) * 128], K0[par][pair],
                                V0[par].ap[:, b, hg * 64:(hg + 1) * 64], V0[par], q_ap, qc, osel))
                    if b >= 1:
                        prv.append((K0[par][pair].ap[hp:hp + 64, (b - 1) * 128:b * 128], K0[par][pair],
                                    V0[par].ap[:, b - 1, hg * 64:(hg + 1) * 64], V0[par], q_ap, qc, osel))
                    elif t >= 1:
                        prv.append((K0[1 - par][pair].ap[hp:hp + 64, 384:512], K0[1 - par][pair],
                                    V0[1 - par].ap[:, 3, hg * 64:(hg + 1) * 64], V0[1 - par], q_ap, qc, osel))
                    else:
                        prv.append(None)
                attend(cur, 128, mcur_b.ap, lambda a: a[:, 0:TT])
                lo = 0 if t >= 1 else 128
                attend(prv, 128, mprev_b.ap, lambda a, lo=lo: a[:, lo:TT], col_lo=lo)
                qc = qT[2 + pair]
                cur, prv = [], []
                for r in range(4):
                    q_ap = qc.ap[hp:hp + 64, r::4]
                    osel = (lambda a, r=r: a[:, r::4])
                    cur.append((K1[par][pair].ap[hp:hp + 64, r::4], K1[par][pair],
                                V1[par].ap[:, r, hg * 64:(hg + 1) * 64], V1[par], q_ap, qc, osel))
                    prv.append((K1[1 - par][pair].ap[hp:hp + 64, r::4], K1[1 - par][pair],
                                V1[1 - par].ap[:, r, hg * 64:(hg + 1) * 64], V1[1 - par], q_ap, qc, osel))
                dv1 = lambda a: a.rearrange("p (i r) -> p r i", r=4)
                attend(cur, 128, mcur_b.ap, dv1)
                if t >= 1:
                    attend(prv, 128, mprev_b.ap, dv1)
                qc = qT[4 + pair]
                cur, prv = [], []
                for r in range(16):
                    q_ap = qc.ap[hp:hp + 64, r::16]
                    osel = (lambda a, r=r: a[:, r::16])
                    kc_ap = K2[pair].ap[hp:hp + 64, 2048 * j2 + r:2048 * (j2 + 1):16]
                    cur.append((kc_ap, K2[pair], V2[j2 % 2].ap[:, r, hg * 64:(hg + 1) * 64], V2[j2 % 2], q_ap, qc, osel))
                    if j2 >= 1:
                        kp_ap = K2[pair].ap[hp:hp + 64, 2048 * (j2 - 1) + r:2048 * j2:16]
                        prv.append((kp_ap, K2[pair], V2[(j2 - 1) % 2].ap[:, r, hg * 64:(hg + 1) * 64], V2[(j2 - 1) % 2],
                                    q_ap, qc, osel))
                dv2 = lambda a: a.rearrange("p (i r) -> p r i", r=16)
                attend(cur, 32, mcur_b.ap[:, p0:p0 + 32], dv2)
                if j2 >= 1:
                    attend(prv, 32, mprev_b.ap[:, p0:p0 + 32], dv2)
            if pend[0] is not None:
                pend[0]()
                pend[0] = None
            S.op("dve", lambda e, DS=DS: e.reciprocal(rden.ap, DS.t[:, 0:TT]), reads=(DS.res,), writes=(rden,))
            dve_tt(yb[pair].ap, NS.t[:, 0:TT], rden.ap, ALU.mult, (NS.res, rden), (yb[pair],))
        if t == dbg.get("_tile", 0) and "yb" in dbg_out:
            for c in range(2):
                S.op("sp", lambda e, c=c: e.dma_start(out=dbg_out["yb"][c * 128:(c + 1) * 128, :], in_=yb[c].ap),
                     reads=(yb[c],), dma="dbg_yb")

        if LIM < 10:
            continue
        acc_ring[0] = [0, 1, 2, 3, 6, 7]
        while gst["c"] < 16:
            gate_chunk()

        if LIM < 11:
            continue
        for j in range(4):
            wba, wva = next_unit()
            wbb, wvb = next_unit()
            for oc in range(2):
                m = 2 * j + oc
                sa = proj_fm(wba, wva, oc * 128, ya, 6)
                sb = proj_fm(wbb, wvb, oc * 128, yb, 2)
                dve_stt(t1b.ap, tg[m].ap, 1.0, sa.t[:, 0:TT], ALU.add, ALU.mult, (tg[m], sa.res), (t1b,))
                dve_stt(t2b.ap, tg[8 + m].ap, 1.0, sb.t[:, 0:TT], ALU.add, ALU.mult, (tg[8 + m], sb.res), (t2b,))
                dve_tt(mrg[m].ap, t1b.ap, t2b.ap, ALU.add, (t1b, t2b), (mrg[m],), eng="pool")

        if LIM < 12:
            continue
        for j in range(4):
            wb, wv = next_unit()
            for oc in range(2):
                m = 2 * j + oc
                s_ = proj_fm(wb, wv, oc * 128, mrg, 8)
                slot = xr[m % 3]
                dma_in(slot, slot.ap, xT[m * 128:(m + 1) * 128, tok0:tok0 + TT], "xr%d" % (m % 3))
                dve_stt(x1[m].ap, s_.t[:, 0:TT], 0.5, slot.ap, ALU.mult, ALU.add, (s_.res, slot), (x1[m],))
        if t == dbg.get("_tile", 0) and "x1" in dbg_out:
            for c in range(8):
                S.op("sp", lambda e, c=c: e.dma_start(out=dbg_out["x1"][c * 128:(c + 1) * 128, :], in_=x1[c].ap),
                     reads=(x1[c],), dma="dbg_x1")

        if LIM < 13:
            continue
        rms_norm(x1, None, tok0, 8, w_h)

        if LIM < 14:
            continue
        for j in range(11):
            wba, wva = next_unit()
            wbv, wvv = next_unit()
            for oc in range(2):
                c = 2 * j + oc
                sa = proj_fm(wba, wva, oc * 128, hT, 8)
                sv = proj_fm(wbv, wvv, oc * 128, hT, 8)
                o = ob[c % 2]
                g_ = gel[c % 2]
                ab = abuf[c % 2]
                w0 = cw.ap[:, c, 0:1]
                w1 = cw.ap[:, c, 1:2]
                w2 = cw.ap[:, c, 2:3]
                copy("pool", ab.ap[:, 0:2], halo.ap[:, c, :], (halo,), (ab,))
                copy("act", ab.ap[:, 2:TT + 2], sa.t[:, 0:TT], (sa.res,), (ab,))
                S.op("act", lambda e, o=o, ab=ab, w2=w2, c=c: e.activation(o.ap, ab.ap[:, 2:TT + 2], AF.Identity,
                                                                    bias=cb.ap[:, c:c + 1], scale=w2),
                     reads=(ab, cw, cb), writes=(o,))
                copy("pool", halo.ap[:, c, :], ab.ap[:, TT:TT + 2], (ab,), (halo,))
                dve_stt(o.ap, ab.ap[:, 1:TT + 1], w1, o.ap, ALU.mult, ALU.add, (ab, cw, o), (o,))
                dve_stt(o.ap, ab.ap[:, 0:TT], w0, o.ap, ALU.mult, ALU.add, (ab, cw, o), (o,))
                act(g_.ap, o.ap, AF.Gelu, (o,), (g_,))
                dve_tt(gg[c].ap, g_.ap, sv.t[:, 0:TT], ALU.mult, (g_, sv.res), (gg[c],))

        if LIM < 15:
            continue
        for j in range(4):
            wb0, wv0 = next_unit()
            wb1, wv1 = next_unit()
            for oc in range(2):
                m = 2 * j + oc
                s_ = acc()
                for kc in range(NFF):
                    wbx, wvx = (wb0, wv0) if kc < 11 else (wb1, wv1)
                    s_.mm(s_.t[:, 0:TT], wvx[:, kc % 11, oc * 128:(oc + 1) * 128], gg[kc].ap, (wbx, gg[kc]))
                dve_tt(x1[m].ap, s_.t[:, 0:TT], x1[m].ap, ALU.add, (s_.res, x1[m]), (x1[m],))

        if LIM < 16:
            continue
        def w_o(kc, src, sres, gcol, rbc, rbres):
            dve_stt(x1[kc].ap, src, gcol, rbc, ALU.mult, ALU.mult, (sres, gvec, rbres), (x1[kc],))
            S.op("sp", lambda e, kc=kc, tok0=tok0: e.dma_start(out=outT[kc * 128:(kc + 1) * 128, tok0:tok0 + TT], in_=x1[kc].ap),
                 reads=(x1[kc],), dma="out%d" % kc)
        if t + 1 < ntiles:
            rms_norm(None, xT, tok0 + TT, 0, w_h)
            rms_norm(x1, None, tok0, 16, w_o, mid_fn=emit_S2)
        else:
            rms_norm(x1, None, tok0, 16, w_o)

    sem_ctx = []
    esem = {}
    for e in ("pe", "act", "dve", "pool"):
        c_ = nc.semaphore("sem_" + e)
        esem[e] = c_.__enter__()
        sem_ctx.append(c_)
    dsem = {}
    for name in S.dma_cum:
        c_ = nc.semaphore("dsem_" + name)
        dsem[name] = c_.__enter__()
        sem_ctx.append(c_)
    with nc.Block() as block:
        S.emit(nc, block, esem, dsem)
    for c_ in reversed(sem_ctx):
        c_.__exit__(None, None, None)
    for c_ in reversed(ctx):
        c_.__exit__(None, None, None)
    return nc


def make_shared(mix_norm_g, w_in, gmlp_norm_g, w_spatial, b_spatial, w_branch_a, w_branch_b, w_out,
                ffn_norm_g, w_up, conv_w, conv_b, w_down, final_norm_g):
    f32 = np.float32
    A = lambda a: np.ascontiguousarray(np.asarray(a, dtype=f32))

    def pk(v):
        v = np.asarray(v, dtype=f32)
        return v.reshape(-1, 128).T

    gvec = np.concatenate([pk(mix_norm_g[0]), pk(ffn_norm_g[0]), pk(final_norm_g)], axis=1)
    gm = pk(gmlp_norm_g[0])
    bsp = np.asarray(b_spatial[0], dtype=f32)
    grp = (np.arange(768) // 192).reshape(6, 128)
    bbc = bsp[grp]
    bbc = np.transpose(bbc, (1, 0, 2)).reshape(128, 6 * 128)
    wspT = np.transpose(np.asarray(w_spatial[0], dtype=f32), (2, 0, 1)).reshape(128, 4 * 128)
    cwv = np.asarray(conv_w[0], dtype=f32)
    cwl = np.transpose(cwv.reshape(3, NFF, 128), (2, 1, 0)).reshape(128, NFF * 3)
    cbl = pk(conv_b[0])
    ident = np.eye(128, dtype=f32)
    kk = np.arange(128)[:, None]
    qq = np.arange(128)[None, :]
    mcur = (kk <= qq).astype(f32)
    mprev = (kk >= qq).astype(f32)
    invf = (np.float32(500000.0) ** (-np.arange(0, 16, 2, dtype=f32) / np.float32(16))).astype(f32)
    consts = np.concatenate([ident, mcur, mprev, np.broadcast_to(invf[None, :], (128, 8))], axis=1)
    return {
        "w_in": A(w_in[0]), "w_pa": A(w_branch_a[0]), "w_pb": A(w_branch_b[0]), "w_out": A(w_out[0]),
        "w_up": A(w_up[0]), "w_down": A(w_down[0]),
        "gvec": A(gvec), "gm": A(gm), "bbc": A(bbc), "wspT": A(wspT), "convw": A(cwl), "convb": A(cbl),
        "consts": A(consts),
    }


_NC_CACHE = {}


def kernel(x, positions, mix_norm_g, w_in, gmlp_norm_g, w_spatial, b_spatial, w_branch_a, w_branch_b, w_out,
           ffn_norm_g, w_up, conv_w, conv_b, w_down, final_norm_g):
    x = np.asarray(x, dtype=np.float32)
    positions = np.asarray(positions)
    shared = make_shared(mix_norm_g, w_in, gmlp_norm_g, w_spatial, b_spatial, w_branch_a, w_branch_b, w_out,
                         ffn_norm_g, w_up, conv_w, conv_b, w_down, final_norm_g)
    n = x.shape[0]
    in_maps = []
    for b in range(n):
        m = dict(shared)
        m["xT"] = np.ascontiguousarray(x[b].T)
        m["pos"] = np.ascontiguousarray(positions[b].astype(np.int32).reshape(32, 128).T)
        in_maps.append(m)
    if "nc" not in _NC_CACHE:
        _NC_CACHE["nc"] = build_nc()
    res = run_bass_kernel_spmd(_NC_CACHE["nc"], in_maps, core_ids=list(range(n)))
    out = np.stack([np.ascontiguousarray(r["outT"].T) for r in res.results], axis=0)
    return out.astype(np.float32)
```
